# Optimizing a Trainium2 kernel written in Bass

```python
import jax, jax.numpy as jnp
from jax import lax
import numpy as np

D_MODEL = 2048
BATCH = 16
SEQ = 2048
DEPTH = 1
DEC_BATCH = 8
DEC_SEQ = 16
PAST_LEN = 4096

CHUNK = 64
N_META = 16
D_MIX = D_MODEL
RWKV_DIM = D_MIX // 2
HEAD_DIM = 64
RWKV_HEADS = RWKV_DIM // HEAD_DIM
DECAY_LORA = 64
AAA_LORA = 64
GATE_LORA = 160
RWKV_COLS = 3 * RWKV_DIM + DECAY_LORA + AAA_LORA + GATE_LORA
CONV_CH = D_MIX - RWKV_DIM
CONV_W = 31
IN_COLS = RWKV_COLS + 2 * CONV_CH
D_FF = 5632
FFN_CONV_W = 3
RMS_EPS = 1e-6
LN_EPS = 1e-5
GN_EPS = 1e-5 * HEAD_DIM

kernel_name = "hymba_rwkv7_conformer_convffn_stream"


def rms_norm(x, g):
    xf = x.astype(jnp.float32)
    y = xf * lax.rsqrt(jnp.mean(xf * xf, -1, keepdims=True) + RMS_EPS)
    return (y * g.astype(jnp.float32)).astype(x.dtype)


def layer_norm(x, g, b):
    xf = x.astype(jnp.float32)
    mu = jnp.mean(xf, -1, keepdims=True)
    var = jnp.mean(jnp.square(xf - mu), -1, keepdims=True)
    y = (xf - mu) * lax.rsqrt(var + LN_EPS)
    return (y * g.astype(jnp.float32) + b.astype(jnp.float32)).astype(x.dtype)


def causal_dwconv(buf, x, w, b):
    full = jnp.concatenate([buf.astype(x.dtype), x], axis=1)
    y = lax.conv_general_dilated(full, w.astype(x.dtype)[:, None, :], window_strides=(1,),
                                 padding='VALID', dimension_numbers=('NWC', 'WIO', 'NWC'),
                                 feature_group_count=x.shape[-1])
    return y + b.astype(x.dtype), full[:, -(w.shape[0] - 1):, :]


def token_shift(prev, p, mu):
    p_prev = jnp.concatenate([prev.astype(p.dtype), p[:, :-1]], axis=1)
    return p + (p_prev - p) * mu.astype(p.dtype), p[:, -1:]


def wkv7_scan(S0, r, w, k, v, a_vec, b_vec):
    def step(S, inp):
        r_t, w_t, k_t, v_t, a_t, b_t = inp
        sa = jnp.einsum('bhvk,bhk->bhv', S, a_t)
        S = S * w_t[:, :, None, :] + sa[..., None] * b_t[:, :, None, :] + v_t[..., None] * k_t[:, :, None, :]
        return S, jnp.einsum('bhvk,bhk->bhv', S, r_t)
    xs = tuple(jnp.swapaxes(t, 0, 1) for t in (r, w, k, v, a_vec, b_vec))
    S, ys = lax.scan(step, S0, xs)
    return jnp.swapaxes(ys, 0, 1), S


def trunk_layer(x, shift_buf, wkv_state, conv_buf, ffn_buf,
                norm_mix, w_in, tshift_mu, w0, w_decay_up, a0, w_aaa_up, w_gate_up,
                k_k, k_a, r_k, lnx_g, lnx_b, conv_w, conv_b, conv_ln_g, conv_ln_b,
                w_out, norm_ffn, w_ffn_up, ffn_conv_w, ffn_conv_b, w_ffn_down):
    dt = x.dtype
    f32 = jnp.float32
    B, T, _ = x.shape
    h = rms_norm(x, norm_mix)
    proj = h @ w_in.astype(dt)
    p_rwkv, new_shift = token_shift(shift_buf, proj[..., :RWKV_COLS], tshift_mu)
    p_conv = proj[..., RWKV_COLS:]
    d3 = 3 * RWKV_DIM
    r, k, v, wl, al, gl = jnp.split(p_rwkv.astype(f32),
                                    [RWKV_DIM, 2 * RWKV_DIM, d3, d3 + DECAY_LORA, d3 + DECAY_LORA + AAA_LORA], axis=-1)
    w_log = -jax.nn.softplus(-(w0.astype(f32) + jnp.tanh(wl) @ w_decay_up.astype(f32))) - 0.5
    decay = jnp.exp(-jnp.exp(w_log))
    a = jax.nn.sigmoid(a0.astype(f32) + al @ w_aaa_up.astype(f32))
    g = jax.nn.sigmoid(gl) @ w_gate_up.astype(f32)
    heads = lambda t: t.reshape(B, T, RWKV_HEADS, HEAD_DIM)
    kk = heads(k * k_k.astype(f32))
    kk = kk / jnp.maximum(jnp.sqrt(jnp.sum(kk * kk, -1, keepdims=True)), 1e-12)
    k = k * (1.0 + (a - 1.0) * k_a.astype(f32))
    rh, kh, vh, ah = heads(r), heads(k), heads(v), heads(a)
    y, S = wkv7_scan(wkv_state.astype(f32), rh, heads(decay), kh, vh, -kk, kk * ah)
    mu = jnp.mean(y, -1, keepdims=True)
    var = jnp.mean(jnp.square(y - mu), -1, keepdims=True)
    y = ((y - mu) * lax.rsqrt(var + GN_EPS)).reshape(B, T, RWKV_DIM) * lnx_g.astype(f32) + lnx_b.astype(f32)
    bonus = jnp.sum(rh * kh * r_k.astype(f32), -1, keepdims=True) * vh
    y_rwkv = ((y + bonus.reshape(B, T, RWKV_DIM)) * g).astype(dt)
    c_val, c_gate = jnp.split(p_conv, 2, axis=-1)
    c = c_val * jax.nn.sigmoid(c_gate)
    c, new_conv = causal_dwconv(conv_buf, c, conv_w, conv_b)
    c = jax.nn.silu(layer_norm(c, conv_ln_g, conv_ln_b))
    x = x + jnp.concatenate([y_rwkv, c], -1) @ w_out.astype(dt)
    h = rms_norm(x, norm_ffn)
    u, z = jnp.split(h @ w_ffn_up.astype(dt), 2, axis=-1)
    u, new_ffn = causal_dwconv(ffn_buf, u, ffn_conv_w, ffn_conv_b)
    x = x + (jax.nn.gelu(u) * z) @ w_ffn_down.astype(dt)
    return x, (new_shift, S.astype(dt), new_conv, new_ffn)


def setup_inputs(seed: int = 0) -> dict:
    key = jax.random.key(seed)
    ks = jax.random.split(key, 32)
    nrm = lambda i, shape, s: s * jax.random.normal(ks[i], shape, jnp.float32)
    L = DEPTH
    return {
        "x_prompt": nrm(0, (BATCH, SEQ, D_MODEL), 1.0),
        "x_sample": nrm(1, (DEC_BATCH, DEC_SEQ, D_MODEL), 1.0),
        "state_shift": nrm(2, (L, DEC_BATCH, 1, RWKV_COLS), 1.0),
        "state_wkv": nrm(3, (L, DEC_BATCH, RWKV_HEADS, HEAD_DIM, HEAD_DIM), 0.3),
        "cache_conv": nrm(4, (L, DEC_BATCH, CONV_W - 1, CONV_CH), 0.5),
        "cache_ffn_conv": nrm(5, (L, DEC_BATCH, FFN_CONV_W - 1, D_FF), 1.0),
        "meta": nrm(6, (N_META, D_MODEL), 1.0),
        "norm_mix": 1.0 + nrm(7, (L, D_MODEL), 0.02),
        "w_in": nrm(8, (L, D_MODEL, IN_COLS), D_MODEL ** -0.5),
        "tshift_mu": jax.random.uniform(ks[9], (L, RWKV_COLS), jnp.float32),
        "w0": jax.random.uniform(ks[10], (L, RWKV_DIM), jnp.float32, -6.0, 0.0),
        "w_decay_up": nrm(11, (L, DECAY_LORA, RWKV_DIM), 0.1),
        "a0": nrm(12, (L, RWKV_DIM), 0.5),
        "w_aaa_up": nrm(13, (L, AAA_LORA, RWKV_DIM), 0.5 * AAA_LORA ** -0.5),
        "w_gate_up": nrm(14, (L, GATE_LORA, RWKV_DIM), GATE_LORA ** -0.5),
        "k_k": 0.85 + nrm(15, (L, RWKV_DIM), 0.05),
        "k_a": 1.0 + nrm(16, (L, RWKV_DIM), 0.05),
        "r_k": nrm(17, (L, RWKV_HEADS, HEAD_DIM), 0.1),
        "lnx_g": 1.0 + nrm(18, (L, RWKV_DIM), 0.02),
        "lnx_b": nrm(19, (L, RWKV_DIM), 0.02),
        "conv_w": nrm(20, (L, CONV_W, CONV_CH), CONV_W ** -0.5),
        "conv_b": nrm(21, (L, CONV_CH), 0.02),
        "conv_ln_g": 1.0 + nrm(22, (L, CONV_CH), 0.02),
        "conv_ln_b": nrm(23, (L, CONV_CH), 0.02),
        "w_out": nrm(24, (L, D_MIX, D_MODEL), D_MIX ** -0.5),
        "norm_ffn": 1.0 + nrm(25, (L, D_MODEL), 0.02),
        "w_ffn_up": nrm(26, (L, D_MODEL, 2 * D_FF), D_MODEL ** -0.5),
        "ffn_conv_w": nrm(27, (L, FFN_CONV_W, D_FF), FFN_CONV_W ** -0.5),
        "ffn_conv_b": nrm(28, (L, D_FF), 0.02),
        "w_ffn_down": nrm(29, (L, D_FF, D_MODEL), D_FF ** -0.5),
        "final_norm": 1.0 + nrm(30, (D_MODEL,), 0.02),
    }


def reference(x_prompt, x_sample, state_shift, state_wkv, cache_conv, cache_ffn_conv, meta,
              norm_mix, w_in, tshift_mu, w0, w_decay_up, a0, w_aaa_up, w_gate_up,
              k_k, k_a, r_k, lnx_g, lnx_b, conv_w, conv_b, conv_ln_g, conv_ln_b,
              w_out, norm_ffn, w_ffn_up, ffn_conv_w, ffn_conv_b, w_ffn_down, final_norm):
    dt = x_prompt.dtype
    bp = x_prompt.shape[0]
    xp = jnp.concatenate([jnp.broadcast_to(meta.astype(dt)[None], (bp, N_META, D_MODEL)), x_prompt], axis=1)
    xs = x_sample
    sp_shift, sp_wkv, sp_conv, sp_ffn = [], [], [], []
    ss_shift, ss_wkv, ss_conv, ss_ffn = [], [], [], []
    for l in range(DEPTH):
        weights = (norm_mix[l], w_in[l], tshift_mu[l], w0[l], w_decay_up[l], a0[l], w_aaa_up[l], w_gate_up[l],
                   k_k[l], k_a[l], r_k[l], lnx_g[l], lnx_b[l], conv_w[l], conv_b[l], conv_ln_g[l], conv_ln_b[l],
                   w_out[l], norm_ffn[l], w_ffn_up[l], ffn_conv_w[l], ffn_conv_b[l], w_ffn_down[l])
        xp, (a1, a2, a3, a4) = trunk_layer(
            xp, jnp.zeros((bp, 1, RWKV_COLS), dt), jnp.zeros((bp, RWKV_HEADS, HEAD_DIM, HEAD_DIM), dt),
            jnp.zeros((bp, CONV_W - 1, CONV_CH), dt), jnp.zeros((bp, FFN_CONV_W - 1, D_FF), dt), *weights)
        xs, (b1, b2, b3, b4) = trunk_layer(
            xs, state_shift[l], state_wkv[l], cache_conv[l], cache_ffn_conv[l], *weights)
        sp_shift.append(a1); sp_wkv.append(a2); sp_conv.append(a3); sp_ffn.append(a4)
        ss_shift.append(b1); ss_wkv.append(b2); ss_conv.append(b3); ss_ffn.append(b4)
    y_prompt = rms_norm(xp, final_norm)[:, N_META:]
    y_sample = rms_norm(xs, final_norm)
    return (y_prompt, y_sample,
            jnp.stack(sp_shift), jnp.stack(sp_wkv), jnp.stack(sp_conv), jnp.stack(sp_ffn),
            jnp.stack(ss_shift), jnp.stack(ss_wkv), jnp.stack(ss_conv), jnp.stack(ss_ffn))
```

```python
import numpy as np
import concourse.bass as bass
import concourse.mybir as mybir

F32 = mybir.dt.float32
BF16 = mybir.dt.bfloat16
AF = mybir.ActivationFunctionType
ALU = mybir.AluOpType

_ESZ = {F32: 4, BF16: 2}
try:
    _ESZ[mybir.dt.float32r] = 4
except Exception:
    pass


def _esize(dt):
    if dt in _ESZ:
        return _ESZ[dt]
    s = str(dt)
    if "64" in s:
        return 8
    if "32" in s:
        return 4
    if "16" in s:
        return 2
    return 1


SB_BASE = {}


def ap_box(ap):
    es = _esize(ap.dtype)
    a = ap.ap
    off = ap.offset
    name = ap.tensor.name
    space = str(ap.space)
    if "SB" in space or "PSUM" in space:
        base = SB_BASE.get(name) if "PSUM" not in space else None
        if base is not None:
            name = "SBUF"
            base_b = base
        else:
            base_b = 0
        pstep, pcount = a[0]
        if pstep == 0:
            p0 = 0
            f0 = off
            p1 = 1
        else:
            p0 = off // pstep
            f0 = off % pstep
            p1 = p0 + pcount
        lo = f0
        hi = f0
        for st, cnt in a[1:]:
            ext = st * (cnt - 1)
            if ext < 0:
                lo += ext
            else:
                hi += ext
        if "PSUM" in space:
            return (name, (p0 // 32) * 32, ((p1 + 31) // 32) * 32, 0, 1 << 20)
        return (name, p0, p1, base_b + lo * es, base_b + (hi + 1) * es)
    lo = off
    hi = off
    for st, cnt in a:
        ext = st * (cnt - 1)
        if ext < 0:
            lo += ext
        else:
            hi += ext
    return (name, 0, 1, lo * es, (hi + 1) * es)


def _overlap(a, b):
    return a[1] < b[2] and b[1] < a[2] and a[3] < b[4] and b[3] < a[4]


def _contains(outer, inner):
    return outer[1] <= inner[1] and inner[2] <= outer[2] and outer[3] <= inner[3] and inner[4] <= outer[4]


class Op:
    __slots__ = ("eng", "fn", "idx", "deps", "is_dma", "signal", "sig", "waits", "ring", "ring_val", "gidx")

    def __init__(self, eng, fn, is_dma=False):
        self.eng = eng
        self.fn = fn
        self.is_dma = is_dma
        self.deps = []
        self.signal = False
        self.sig = None
        self.waits = []
        self.ring = None
        self.ring_val = None


ENGS = ("pe", "act", "dve", "pool", "sp")
SIG_WRAP = 16000


class Prog:
    def __init__(self, nc, n_dma_rings=12):
        self.nc = nc
        self.ops = {e: [] for e in ENGS}
        self.all = []
        self.hist = {}
        self.n_rings = n_dma_rings
        self.ring_last = [None] * n_dma_rings
        self.ring_cnt = [0] * n_dma_rings
        self.ring_next = 0

    def _track(self, op, reads, writes):
        rd2 = []
        writes = list(writes)
        for ap in reads:
            if "PSUM" in str(ap.space):
                writes.append(ap)
            else:
                rd2.append(ap)
        reads = rd2
        deps = set()
        for ap in reads:
            bx = ap_box(ap)
            h = self.hist.setdefault(bx[0], [])
            for (b, o, w) in h:
                if w and _overlap(b, bx):
                    deps.add(o)
        for ap in writes:
            bx = ap_box(ap)
            h = self.hist.setdefault(bx[0], [])
            for (b, o, w) in h:
                if _overlap(b, bx):
                    deps.add(o)
        deps.discard(op)
        for ap in reads:
            bx = ap_box(ap)
            h = self.hist[bx[0]]
            if not op.is_dma:
                h[:] = [e for e in h if not ((not e[2]) and e[0] == bx and e[1].eng == op.eng and not e[1].is_dma)]
            h.append((bx, op, False))
        for ap in writes:
            bx = ap_box(ap)
            h = self.hist[bx[0]]
            h[:] = [e for e in h if not _contains(bx, e[0])]
            h.append((bx, op, True))
        op.deps = list(deps)

    def add(self, eng, fn, reads=(), writes=()):
        op = Op(eng, fn)
        op.idx = len(self.ops[eng])
        op.gidx = len(self.all)
        self.ops[eng].append(op)
        self.all.append(op)
        self._track(op, reads, writes)
        return op

    def dma(self, queue, out, in_, **kw):
        def fn(e, out=out, in_=in_, kw=kw):
            return e.dma_start(out=out, in_=in_, **kw)
        op = Op(queue, fn, is_dma=True)
        op.idx = len(self.ops[queue])
        op.gidx = len(self.all)
        self.ops[queue].append(op)
        self.all.append(op)
        self._track(op, [in_], [out])
        r = self.ring_next
        self.ring_next = (self.ring_next + 1) % self.n_rings
        prev = self.ring_last[r]
        if prev is not None and prev not in op.deps:
            op.deps.append(prev)
        self.ring_cnt[r] += 1
        op.ring = r
        op.ring_val = 16 * self.ring_cnt[r]
        self.ring_last[r] = op
        return op

    def finalize(self):
        last_vc = {e: {} for e in ENGS}
        dma_known = {e: {} for e in ENGS}
        op_vc = {}
        for op in self.all:
            E = op.eng
            vc = dict(last_vc[E])
            waits = []
            best = {}
            for d in sorted(op.deps, key=lambda o: o.gidx):
                if d.is_dma:
                    if dma_known[E].get(d.ring, 0) >= d.ring_val:
                        continue
                    dma_known[E][d.ring] = d.ring_val
                    waits.append(d)
                    continue
                if d.eng not in best or d.idx > best[d.eng].idx:
                    best[d.eng] = d
            for De, d in best.items():
                if De == E:
                    if E in ("pe", "sp"):
                        continue
                    if vc.get(E, -1) >= d.idx:
                        continue
                    vc[E] = d.idx
                    waits.append(d)
                    d.signal = True
                    continue
                if vc.get(De, -1) >= d.idx:
                    continue
                waits.append(d)
                d.signal = True
                dvc = op_vc.get(d, {})
                for k2, v2 in dvc.items():
                    if vc.get(k2, -1) < v2:
                        vc[k2] = v2
                if vc.get(De, -1) < d.idx:
                    vc[De] = d.idx
            op.waits = waits
            if not op.is_dma:
                vcc = dict(vc)
                vcc[E] = op.idx
                op_vc[op] = vcc
            last_vc[E] = vc
        self.nsig = {}
        for e in ENGS:
            c = 0
            for op in self.ops[e]:
                if op.is_dma:
                    continue
                if op.signal:
                    op.sig = c
                    c += 1
            self.nsig[e] = c

    def emit(self, final_waits=()):
        nc = self.nc
        self.finalize()
        import contextlib
        with contextlib.ExitStack() as st:
            sems = {}
            for e in ENGS:
                n = (self.nsig[e] + SIG_WRAP - 1) // SIG_WRAP
                sems[e] = [st.enter_context(nc.semaphore(f"s_{e}_{i}")) for i in range(max(n, 1))]
            rsems = [st.enter_context(nc.semaphore(f"s_ring_{i}")) for i in range(self.n_rings)]
            block = st.enter_context(nc.Block())

            def run(engname, eng):
                for op in self.ops[engname]:
                    for d in op.waits:
                        if d.is_dma:
                            eng.wait_ge(rsems[d.ring], d.ring_val)
                        else:
                            eng.wait_ge(sems[d.eng][d.sig // SIG_WRAP], d.sig % SIG_WRAP + 1)
                    ins = op.fn(eng)
                    if op.is_dma:
                        ins.then_inc(rsems[op.ring], 16)
                    elif op.signal:
                        ins.then_inc(sems[engname][op.sig // SIG_WRAP], 1)
                if engname in ("sp", "pool", "act"):
                    lastv = {}
                    for op in self.ops[engname]:
                        if op.is_dma:
                            lastv[op.ring] = max(lastv.get(op.ring, 0), op.ring_val)
                    for r, v in lastv.items():
                        eng.wait_ge(rsems[r], v)

            @block.tensor
            def _(t):
                run("pe", t)

            @block.scalar
            def _(a):
                run("act", a)

            @block.vector
            def _(v):
                run("dve", v)

            @block.gpsimd
            def _(g):
                run("pool", g)

            @block.sync
            def _(s):
                run("sp", s)

    def mm(self, out, lhsT, rhs, start=True, stop=True, extra_reads=()):
        def fn(e):
            return e.matmul(out, lhsT=lhsT, rhs=rhs, start=start, stop=stop)
        rd = [lhsT, rhs] + list(extra_reads)
        if not start:
            rd.append(out)
        return self.add("pe", fn, rd, [out])

    def transpose(self, out, in_, ident):
        def fn(e):
            return e.transpose(out=out, in_=in_, identity=ident)
        return self.add("pe", fn, [in_, ident], [out])

    def act(self, out, in_, func, bias=None, scale=None, accum_out=None, eng="act"):
        kw = {}
        rd = [in_]
        wr = [out]
        if bias is not None:
            kw["bias"] = bias
            if not isinstance(bias, (int, float)):
                rd.append(bias)
        if scale is not None:
            kw["scale"] = scale
            if not isinstance(scale, (int, float)):
                rd.append(scale)
        if accum_out is not None:
            kw["accum_out"] = accum_out
            wr.append(accum_out)

        def fn(e):
            return e.activation(out=out, in_=in_, func=func, **kw)
        return self.add("act", fn, rd, wr)

    def tt(self, eng, out, in0, in1, op):
        def fn(e):
            return e.tensor_tensor(out=out, in0=in0, in1=in1, op=op)
        return self.add(eng, fn, [in0, in1], [out])

    def ts(self, eng, out, in0, s1, op0, s2=None, op1=None, accum_out=None):
        rd = [in0]
        if not isinstance(s1, (int, float)):
            rd.append(s1)
        if s2 is not None and not isinstance(s2, (int, float)):
            rd.append(s2)
        wr = [out]
        kw = {}
        if accum_out is not None:
            kw["accum_out"] = accum_out
            wr.append(accum_out)

        def fn(e):
            if op1 is None:
                return e.tensor_scalar(out=out, in0=in0, scalar1=s1, scalar2=None, op0=op0, **kw)
            return e.tensor_scalar(out=out, in0=in0, scalar1=s1, scalar2=s2, op0=op0, op1=op1, **kw)
        return self.add(eng, fn, rd, wr)

    def stt(self, out, in0, scalar, in1, op0, op1):
        rd = [in0, in1]
        if not isinstance(scalar, (int, float)):
            rd.append(scalar)

        def fn(e):
            return e.scalar_tensor_tensor(out=out, in0=in0, scalar=scalar, in1=in1, op0=op0, op1=op1)
        return self.add("dve", fn, rd, [out])

    def copy(self, eng, out, in_):
        if eng == "act":
            def fn(e):
                return e.copy(out=out, in_=in_)
        else:
            def fn(e):
                return e.tensor_copy(out=out, in_=in_)
        return self.add(eng, fn, [in_], [out])

    def memset(self, eng, ap, val):
        def fn(e):
            return e.memset(ap, val)
        return self.add(eng, fn, [], [ap])

    def recip(self, out, in_):
        def fn(e):
            return e.reciprocal(out=out, in_=in_)
        return self.add("dve", fn, [in_], [out])

    def scan(self, out, d0, d1, init, op0, op1):
        rd = [d0, d1]
        if not isinstance(init, (int, float)):
            rd.append(init)

        def fn(e):
            return e.tensor_tensor_scan(out=out, data0=d0, data1=d1, initial=init, op0=op0, op1=op1)
        return self.add("dve", fn, rd, [out])

    def affine_select(self, out, in_, compare_op, fill, base, pattern, channel_multiplier):
        def fn(e):
            return e.affine_select(out=out, in_=in_, compare_op=compare_op, fill=fill, base=base,
                                   pattern=pattern, channel_multiplier=channel_multiplier)
        return self.add("pool", fn, [in_], [out])

from concourse.bass_utils import run_bass_kernel_spmd
from concourse.ap import AP
import contextlib

D = 2048
RW = 1024
DFF = 5632
NJ = 44
INC = 5408
RWC = 3360
C_DEC = 0.6065306597126334
RMS_EPS = 1e-6
LN_EPS = 1e-5
GN_EPS = 64e-5
F32R = mybir.dt.float32r


def R(ap):
    return ap.bitcast(F32R)

WIN_CONV0 = 0
WIN_LORA0 = 2048
WIN_RKV0 = 2336


def _prod(s):
    r = 1
    for v in s:
        r *= v
    return r


def view(ap2d, shape):
    if len(shape) == 1:
        return ap2d
    if len(shape) == 2:
        return ap2d.rearrange("p (a b) -> p a b", a=shape[0])
    if len(shape) == 3:
        return ap2d.rearrange("p (a b c) -> p a b c", a=shape[0], b=shape[1])
    raise ValueError(shape)


class Mem:
    def __init__(self, t):
        self.t = t

    def f32(self, off, shape):
        n = _prod(shape)
        return view(self.t[:, off:off + n], shape)

    def bf(self, off, shape):
        n = _prod(shape)
        assert n % 2 == 0
        return view(self.t[:, off:off + n // 2].bitcast(BF16), shape)


class _Stop(Exception):
    pass


def build_program(NSEQ=2, NT=4, debug=None):
    def chk(stage):
        if debug is not None and debug == stage:
            raise _Stop()
    nc = bass.Bass("TRN2", target_bir_lowering=False)
    P = Prog(nc)
    T = 512
    SEQ = NT * T

    def din(name, shape):
        return nc.dram_tensor(name, list(shape), F32, kind="ExternalInput").ap()

    def dout(name, shape):
        return nc.dram_tensor(name, list(shape), F32, kind="ExternalOutput").ap()

    xp = din("xp", (NSEQ, SEQ, D))
    xs = din("xs", (16, D))
    st_shift = din("st_shift", (RWC,))
    st_wkv = din("st_wkv", (16, 64, 64))
    st_conv = din("st_conv", (30, 1024))
    st_ffn = din("st_ffn", (2, DFF))
    meta = din("meta", (16, D))
    norm_mix = din("norm_mix", (D,))
    w_in = din("w_in", (D, INC))
    tshift_mu = din("tshift_mu", (RWC,))
    w0 = din("w0", (RW,))
    w_decay_up = din("w_decay_up", (64, RW))
    a0 = din("a0", (RW,))
    w_aaa_up = din("w_aaa_up", (64, RW))
    w_gate_up = din("w_gate_up", (160, RW))
    k_k = din("k_k", (RW,))
    k_a = din("k_a", (RW,))
    r_k = din("r_k", (RW,))
    lnx_g = din("lnx_g", (RW,))
    lnx_b = din("lnx_b", (RW,))
    conv_w = din("conv_w", (31, 1024))
    conv_b = din("conv_b", (1024,))
    conv_ln_g = din("conv_ln_g", (1024,))
    conv_ln_b = din("conv_ln_b", (1024,))
    w_out = din("w_out", (D, D))
    norm_ffn = din("norm_ffn", (D,))
    w_ffn_up = din("w_ffn_up", (D, 2 * DFF))
    ffn_conv_w = din("ffn_conv_w", (3, DFF))
    ffn_conv_b = din("ffn_conv_b", (DFF,))
    w_ffn_down = din("w_ffn_down", (DFF, D))
    final_norm = din("final_norm", (D,))

    yp = dout("yp", (NSEQ, SEQ, D))
    ys = dout("ys", (16, D))
    o_shift_p = dout("o_shift_p", (NSEQ, RWC))
    o_wkv_p = dout("o_wkv_p", (NSEQ, 16, 64, 64))
    o_conv_p = dout("o_conv_p", (NSEQ, 30, 1024))
    o_ffn_p = dout("o_ffn_p", (NSEQ, 2, DFF))
    o_shift_s = dout("o_shift_s", (RWC,))
    o_wkv_s = dout("o_wkv_s", (16, 64, 64))
    o_conv_s = dout("o_conv_s", (30, 1024))
    o_ffn_s = dout("o_ffn_s", (2, DFF))

    win_s = nc.dram_tensor("win_s", [D, INC], BF16).ap()
    wout_s = nc.dram_tensor("wout_s", [D, D], BF16).ap()
    wup_s = nc.dram_tensor("wup_s", [D, 2 * DFF], BF16).ap()
    wdn_s = nc.dram_tensor("wdn_s", [DFF, D], BF16).ap()

    st = contextlib.ExitStack()
    SB0 = 16512
    AB = 25056
    XB = AB + 10496
    XR0 = XB + 4864
    NH = 6912
    NR = 12544
    assert SB0 + 4 * (XR0 + NR) <= 229376
    arena_t = nc.alloc_sbuf_tensor_at("arena", [128, XR0], F32, offset=SB0)
    arena_h = nc.alloc_sbuf_tensor_at("arena_h", [128, NH], F32, offset=SB0 + 4 * XR0)
    arena_r = nc.alloc_sbuf_tensor_at("arena_r", [128, NR], F32, offset=SB0 + 4 * XR0)
    SB_BASE.clear()
    SB_BASE[arena_t.name] = SB0
    SB_BASE[arena_h.name] = SB0 + 4 * XR0
    SB_BASE[arena_r.name] = SB0 + 4 * XR0
    M = Mem(arena_t)
    MH = Mem(arena_h)
    MR = Mem(arena_r)
    HB = XR0 - AB
    banks = [st.enter_context(nc.psum_tensor(f"pb{i}", [128, 512], F32)) for i in range(8)]
    A0, A1, TR, MS, IA, IB, SQ, YT = banks

    XT = M.f32(0, (4, 2048))
    HT = M.bf(8192, (16, 512))
    WR = [M.bf(12288 + i * 4096, (8192,)) for i in range(2)]
    o = 20480
    NPT = 640
    PT = M.f32(o, (NPT,)); o += NPT
    IDF = M.f32(o, (128,)); o += 128
    IDB = M.bf(o, (128,)); o += 64
    MK = M.f32(o, (5, 64)); o += 320
    BONES = M.f32(o, (128,)); o += 128
    CMAT = M.f32(o, (128,)); o += 128
    ONES = M.f32(o, (128,)); o += 128
    RMASK = M.f32(o, (128,)); o += 128
    RMASK16 = M.f32(o, (32,)); o += 32
    WLAD = M.bf(o, (1024,)); o += 512
    WLAA = M.bf(o, (1024,)); o += 512
    WG1 = M.bf(o, (1024,)); o += 512
    WG2 = M.bf(o, (1024,)); o += 512

    class StateSet:
        pass
    sets = []
    for i in range(2):
        s_ = StateSet()
        s_.SH = M.f32(o, (32,)); o += 32
        s_.HBD = MR.f32(10496 + i * 1024, (8, 128))
        s_.CH = M.f32(o, (8, 32)); o += 256
        s_.FH = M.f32(o, (NJ, 2)); o += 88
        sets.append(s_)
    STA, STB = sets
    SMALL = M.f32(o, (64,)); o += 64
    assert o <= AB, (o, AB)

    YCAT = M.bf(AB + 0, (16, 512))
    PBUF = [M.f32(AB + 4096 + i * 512, (512,)) for i in range(2)]
    DTMP = [M.f32(AB + 5120 + i * 512, (512,)) for i in range(2)]
    L1S = M.bf(AB + 6144, (512,))
    SG1 = M.bf(AB + 6400, (512,))
    SG2 = M.bf(AB + 6656, (512,))
    L1F = M.f32(AB + 6912, (512,))
    RKV = [[M.f32(AB + 7424 + (b * 3 + i) * 512, (512,)) for i in range(3)] for b in range(2)]
    HN_A = [M.bf(XB + i * 1024, (2048,)) for i in range(2)]
    GLU_OFF = XB
    CT = [M.f32(XB + 4336, (512,))] + [MH.f32(4096 + i * 512, (512,)) for i in range(5)]
    CO = MH.f32(0, (8, 512))
    STREAMS = []
    for s_i in range(2):
        S = {}
        S["WT"] = [M.f32(XB + s_i * 2048 + i * 128, (128,)) for i in range(16)]
        S["BKF"] = M.f32(XB + 4096 + s_i * 256, (2, 128))
        S["DT"] = M.f32(XB + 4608 + s_i * 128, (128,))
        S["BK"] = MR.f32(s_i * 256, (2, 128))
        S["SC5"] = MR.f32(512 + s_i * 1280, (4, 5, 64))
        S["NTB"] = [MR.f32(3072 + (s_i * 2 + i) * 512, (4, 128)) for i in range(2)]
        S["MB"] = [MR.f32(5120 + (s_i * 2 + i) * 256, (4, 64)) for i in range(2)]
        S["TTB"] = MR.f32(6144 + s_i * 256, (4, 64))
        S["RU"] = MR.f32(6656 + s_i * 128, (128,))
        S["ARZ"] = [MR.f32(6912 + (s_i * 2 + hd) * 256, (2, 128)) for hd in range(2)]
        S["TM"] = MR.f32(7936 + s_i * 1024, (2, 4, 128))
        S["UZ"] = [MR.f32(9984 + (s_i * 2 + hd) * 128, (128,)) for hd in range(2)]
        S["banks"] = (IA, IB) if s_i == 0 else (SQ, YT)
        STREAMS.append(S)
    GB = M.bf(AB + 0, (NJ, 512))
    FN = M.f32(AB + 11264, (2048,))
    UB = [M.f32(AB + 13312 + i * 520, (520,)) for i in range(2)]
    CV = [M.f32(AB + 14352, (512,)), MH.f32(4096, (512,))]
    YO = [MH.f32(i * 2048, (2048,)) for i in range(2)]
    GL = [MH.f32(4608 + i * 512, (512,)) for i in range(2)]
    HN_F = [M.bf(AB + 4096 + i * 1024, (2048,)) for i in range(2)]
    assert AB + 14864 <= XR0
    ST32 = [M.f32(AB + i * 5632, (5632,)) for i in range(2)]
    ST16 = [M.bf(AB + 11264, (5632,)), MH.bf(0, (5632,))]
    PSTG = [MH.f32(2816 + i * 128, (128,)) for i in range(2)]
    cols = {}
    cpos = [0]

    def pcol(name, n):
        cols[name] = cpos[0]
        cpos[0] += n
    for nm, n in (("mu", 27), ("w0", 8), ("a0", 8), ("k_k", 8), ("k_a", 8), ("r_k", 8), ("lnx_g", 8),
                  ("lnx_b", 8), ("conv_b", 8), ("cln_g", 8), ("cln_b", 8), ("nmix", 16), ("nffn", 16),
                  ("cw", 248), ("fw", 132), ("fb", 44), ("omk_a", 8), ("nw0", 8), ("na0", 8)):
        pcol(nm, n)
    assert cpos[0] <= NPT

    def pt(name, i=0, rows=slice(0, 128)):
        c = cols[name] + i
        return PT[rows, c:c + 1]

    def rows2d(ap1d, n):
        return ap1d.rearrange("(c p) -> c p", p=n)

    P.memset("pool", IDF, 0.0)
    P.affine_select(IDF, IDF, ALU.not_equal, 1.0, 0, [[-1, 128]], 1)
    P.copy("dve", IDB, IDF)
    P.memset("pool", ONES, 1.0)
    P.memset("pool", BONES, 0.0)
    P.memset("pool", BONES[0:64, 0:64], 1.0)
    P.memset("pool", BONES[64:128, 64:128], 1.0)
    P.stt(CMAT, BONES, -1.0 / 64.0, IDF, ALU.mult, ALU.add)
    P.memset("pool", MK[0:64], 1.0)
    for q, cmp_, cm, pat in ((0, ALU.is_gt, -1, 1), (1, ALU.is_ge, -1, 1), (2, ALU.is_gt, -1, 1),
                             (3, ALU.is_ge, -1, 1), (4, ALU.is_gt, 1, -1)):
        P.affine_select(MK[0:64, q, :], MK[0:64, q, :], cmp_, 0.0, 0, [[pat, 64]], cm)
    P.memset("pool", RMASK, 1.0)
    P.memset("pool", RMASK[:, 0:128:64], 0.0)
    P.memset("pool", RMASK16, 1.0)
    P.memset("pool", RMASK16[:, 0:32:16], 0.0)
    ZERO = M.f32(AB + 0, (2048,))
    P.memset("pool", ZERO, 0.0)
    P.copy("dve", R(MR.f32(6912, (1024,))), ZERO[:, 0:1024])
    P.copy("dve", R(MR.f32(7936, (2048,))[0:64]), ZERO[0:64, 0:2048])
    P.copy("dve", R(MR.f32(9984, (512,))[0:64]), ZERO[0:64, 0:512])
    P.memset("pool", SG2, 0.0)
    for s_ in sets:
        P.copy("dve", R(s_.HBD.rearrange("p a b -> p (a b)")), ZERO[:, 0:1024])
    P.memset("pool", STA.SH, 0.0)
    P.memset("pool", STA.CH, 0.0)
    P.memset("pool", STA.FH, 0.0)
    P.memset("pool", STB.SH, 0.0)
    P.memset("pool", STB.CH, 0.0)

    if debug is not None and debug <= 0:
        P.emit(); st.close(); return nc
    row_jobs = []

    def addrows(ap1d, name, n, base=0):
        row_jobs.append((rows2d(ap1d, 128), n, cols[name] + base))
    def rwkv_rows(src1d):
        jobs = []
        jobs.append((rows2d(src1d[3072:3328], 128), 2, 0))
        jobs.append((src1d[3328:3360].rearrange("(a b) -> a b", a=1), 1, 2))
        for pp in range(8):
            for i in range(3):
                jobs.append((rows2d(src1d[i * 1024 + pp * 128: i * 1024 + (pp + 1) * 128], 128), 1, 3 + pp * 3 + i))
        return jobs
    for (ap_, n, c) in rwkv_rows(tshift_mu):
        row_jobs.append((ap_, n, cols["mu"] + c))
    addrows(w0, "w0", 8); addrows(a0, "a0", 8); addrows(k_k, "k_k", 8); addrows(k_a, "k_a", 8)
    addrows(r_k, "r_k", 8); addrows(lnx_g, "lnx_g", 8); addrows(lnx_b, "lnx_b", 8)
    addrows(conv_b, "conv_b", 8); addrows(conv_ln_g, "cln_g", 8); addrows(conv_ln_b, "cln_b", 8)
    addrows(norm_mix, "nmix", 16); addrows(norm_ffn, "nffn", 16)
    cwf = conv_w.rearrange("w c -> (w c)")
    for blk in range(0, 248, 124):
        row_jobs.append((rows2d(cwf[blk * 128:(blk + 124) * 128], 128), 124, cols["cw"] + blk))
    fwf = ffn_conv_w.rearrange("w c -> (w c)")
    for blk in range(0, 132, 66):
        row_jobs.append((rows2d(fwf[blk * 128:(blk + 66) * 128], 128), 66, cols["fw"] + blk))
    addrows(ffn_conv_b, "fb", 44)

    def run_row_jobs(jobs, dest_fn, k0=0):
        k = k0
        i = 0
        while i < len(jobs):
            stg = PSTG[k % 2]
            k += 1
            batch = []
            used = 0
            P.memset("pool", stg, 0.0)
            while i < len(jobs) and used + jobs[i][1] <= 128:
                ap_, n, c = jobs[i]
                w = ap_.shape[1]
                P.dma("sp", stg[used:used + n, 0:w], ap_)
                batch.append((used, n, c))
                used += n
                i += 1
            P.transpose(TR[:, 0:128], stg, IDF)
            for (r0, n, c) in batch:
                P.copy("dve", dest_fn(c, n), TR[:, r0:r0 + n])
        return k
    kk_ = run_row_jobs(row_jobs, lambda c, n: PT[:, c:c + n])
    P.ts("dve", PT[:, cols["omk_a"]:cols["omk_a"] + 8], PT[:, cols["k_a"]:cols["k_a"] + 8], -1.0, ALU.mult, 1.0, ALU.add)
    P.ts("dve", PT[:, cols["nw0"]:cols["nw0"] + 8], PT[:, cols["w0"]:cols["w0"] + 8], -1.0, ALU.mult)
    P.ts("dve", PT[:, cols["na0"]:cols["na0"] + 8], PT[:, cols["a0"]:cols["a0"] + 8], -1.0, ALU.mult)

    if debug is not None and debug <= 1:
        P.emit(); st.close(); return nc
    P.memset("pool", WLAD, 0.0)
    P.memset("pool", WLAA, 0.0)
    P.memset("pool", WG2, 0.0)
    s32 = ST32[0]
    P.dma("sp", s32[0:64, 0:1024], w_decay_up)
    P.dma("sp", s32[64:128, 0:1024], w_aaa_up)
    P.copy("dve", WLAD[0:64, :], s32[0:64, 0:1024])
    P.copy("dve", WLAA[64:128, :], s32[64:128, 0:1024])
    P.dma("sp", s32[:, 1024:2048], w_gate_up[0:128, :])
    P.dma("sp", s32[0:32, 2048:3072], w_gate_up[128:160, :])
    P.copy("dve", WG1, s32[:, 1024:2048])
    P.copy("dve", WG2[0:32, :], s32[0:32, 2048:3072])

    if debug is not None and debug <= 2:
        P.emit(); st.close(); return nc
    run_row_jobs([(a_, n, c) for (a_, n, c) in rwkv_rows(st_shift)], lambda c, n: STB.SH[:, c:c + n], kk_)
    s32b = ST32[1]
    P.dma("sp", view(s32b[0:64, 0:1024], (16, 64)), st_wkv.rearrange("h v k -> v h k"))
    for pp in range(8):
        P.transpose(TR[:, 0:64], s32b[0:64, pp * 128:(pp + 1) * 128], IDF[0:64, 0:64])
        P.copy("dve", R(STB.HBD[0:64, pp, 0:64]), TR[0:64, 0:64])
        P.copy("dve", R(STB.HBD[64:128, pp, 64:128]), TR[64:128, 0:64])
    P.dma("sp", s32b[0:30, 1024:2048], st_conv)
    for j in range(8):
        P.transpose(TR[:, 0:128], s32b[:, 1024 + j * 128:1024 + (j + 1) * 128], IDF)
        P.copy("dve", STB.CH[:, j, 0:30], TR[:, 0:30])
    P.dma("sp", s32b[0:88, 2048:2176], st_ffn.rearrange("t (j p) -> (t j) p", p=128))
    P.transpose(TR[:, 128:256], s32b[:, 2048:2176], IDF)
    P.copy("dve", STB.FH.rearrange("p j t -> p t j"), view(TR[:, 128:216], (2, NJ)))

    if debug is not None and debug <= 3:
        P.emit(); st.close(); return nc
    cast_engs = ["dve", "act", "pool"]
    cast_i = [0]

    def cast(out, in_):
        e = cast_engs[cast_i[0] % 3]
        cast_i[0] += 1
        P.copy(e, out, in_)
    sidx = [0]

    def conv_rows_generic(src_rows_ap, ncols, dst_rows_ap, permute=None):
        i = sidx[0] % 2
        sidx[0] += 1
        a32 = ST32[i][:, 0:ncols]
        a16 = ST16[i][:, 0:ncols]
        P.dma("sp", a32, src_rows_ap)
        if permute is None:
            half = ncols // 2
            cast(a16[:, 0:half], a32[:, 0:half])
            cast(a16[:, half:ncols], a32[:, half:ncols])
        else:
            permute(a16, a32)
        P.dma("sp", dst_rows_ap, a16)

    def perm_win(a16, a32):
        cast(a16[:, 0:2048].rearrange("q (j t c) -> q j t c", j=8, t=2),
             a32[:, 3360:5408].rearrange("q (t j c) -> q j t c", t=2, j=8))
        cast(a16[:, 2048:2336], a32[:, 3072:3360])
        cast(a16[:, 2336:5408].rearrange("q (p i c) -> q p i c", p=8, i=3),
             a32[:, 0:3072].rearrange("q (i p c) -> q p i c", i=3, p=8))
    for kc in range(16):
        conv_rows_generic(w_in[kc * 128:(kc + 1) * 128, :], INC, win_s[kc * 128:(kc + 1) * 128, :], perm_win)
    for kc in range(16):
        conv_rows_generic(w_out[kc * 128:(kc + 1) * 128, :], D, wout_s[kc * 128:(kc + 1) * 128, :])

    def perm_up(a16, a32):
        cast(a16.rearrange("q (j t c) -> q j t c", j=22, t=2), a32.rearrange("q (t j c) -> q j t c", t=2, j=22))
    for kc in range(16):
        for hf in range(2):
            src = w_ffn_up[kc * 128:(kc + 1) * 128, :].rearrange("q (t n) -> q t n", t=2)[:, :, hf * 2816:(hf + 1) * 2816]
            i = sidx[0] % 2
            sidx[0] += 1
            a32 = ST32[i]
            a16 = ST16[i]
            P.dma("sp", view(a32, (2, 2816)), src)
            perm_up(a16, a32)
            P.dma("sp", wup_s[kc * 128:(kc + 1) * 128, hf * 5632:(hf + 1) * 5632], a16)
    for kc in range(NJ):
        conv_rows_generic(w_ffn_down[kc * 128:(kc + 1) * 128, :], D, wdn_s[kc * 128:(kc + 1) * 128, :])

    if debug is not None and debug <= 4:
        P.emit(); st.close(); return nc
    wslot = [0]

    def wload(src_ap, shape):
        s_ = WR[wslot[0] % 2]
        wslot[0] += 1
        n = _prod(shape)
        v = view(s_[:, 0:n], shape)
        P.dma("sp", v, src_ap)
        return v

    win_v = win_s.rearrange("(kc p) n -> p kc n", p=128)
    wout_v = wout_s.rearrange("(kc p) n -> p kc n", p=128)
    wup_v = wup_s.rearrange("(kc p) n -> p kc n", p=128)
    wdn_v = wdn_s.rearrange("(kc p) n -> p kc n", p=128)

    abank = [0]

    def next_abank():
        b = (A0, A1)[abank[0] % 2]
        abank[0] += 1
        return b

    def tile(Tt, segs, C, x_loads, y_stores):
        nb = max(1, Tt // 128)
        tbs = min(Tt, 128)
        nch_tile = Tt // C
        rmask = RMASK if C == 64 else RMASK16
        nlev = {64: 6, 16: 4}[C]

        for (dst, src) in x_loads:
            P.dma("sp", dst, src)

        def rmsnorm_to_hT(gname, HN):
            for tb in range(nb):
                hn = HN[tb % 2]
                xa = XT[0:tbs, tb, :]
                ss = SMALL[0:tbs, tb:tb + 1]
                P.act(hn[0:tbs, :], xa, AF.Square, accum_out=ss)
                sd = SMALL[0:tbs, 8 + tb:9 + tb]
                P.act(sd, ss, AF.Sqrt, bias=RMS_EPS, scale=1.0 / D)
                rs = SMALL[0:tbs, 16 + tb:17 + tb]
                P.recip(rs, sd)
                P.ts("dve", hn[0:tbs, :], xa, rs, ALU.mult)
                for k4 in range(4):
                    trb = TR[:, (k4 % 2) * 256:(k4 % 2) * 256 + 256].bitcast(BF16)
                    for q in range(4):
                        kc = k4 * 4 + q
                        P.transpose(trb[:, q * 128:q * 128 + tbs], hn[0:tbs, kc * 128:(kc + 1) * 128], IDB[0:tbs, 0:tbs])
                    gain = PT[:, cols[gname] + k4 * 4: cols[gname] + k4 * 4 + 4]
                    gb = AP(gain.tensor, gain.offset, [list(gain.ap[0]), [1, 4], [0, tbs]])
                    src = view(trb, (4, 128))[:, :, 0:tbs]
                    P.tt("dve", HT[:, k4 * 4:(k4 + 1) * 4, tb * 128:tb * 128 + tbs], src, gb, ALU.mult)

        rmsnorm_to_hT("nmix", HN_A)

        chk(5)
        def proj_chunk(wv, c0, Mrows, bank):
            for kc in range(16):
                P.mm(bank[0:Mrows, 0:Tt], wv[:, kc, c0:c0 + Mrows], HT[:, kc, 0:Tt], start=(kc == 0), stop=(kc == 15))

        pb_i = [0]

        def shift_epi(bank, Mrows, mcol, out_ap):
            pb = PBUF[pb_i[0] % 2]
            dt = DTMP[pb_i[0] % 2]
            pb_i[0] += 1
            P.copy("act", pb[0:Mrows, 0:Tt], bank[0:Mrows, 0:Tt])
            for sg in segs:
                s0, L, S_ = sg["start"], sg["L"], sg["st"]
                P.tt("dve", dt[0:Mrows, s0 + 1:s0 + L], pb[0:Mrows, s0:s0 + L - 1], pb[0:Mrows, s0 + 1:s0 + L], ALU.subtract)
                P.tt("dve", dt[0:Mrows, s0:s0 + 1], S_.SH[0:Mrows, mcol:mcol + 1], pb[0:Mrows, s0:s0 + 1], ALU.subtract)
                P.copy("pool", S_.SH[0:Mrows, mcol:mcol + 1], pb[0:Mrows, s0 + L - 1:s0 + L])
            P.stt(out_ap, dt[0:Mrows, 0:Tt], PT[0:Mrows, cols["mu"] + mcol:cols["mu"] + mcol + 1], pb[0:Mrows, 0:Tt], ALU.mult, ALU.add)

        nseg = len(segs)
        Lmax = max(sg["L"] for sg in segs)
        GW = 30 + Lmax
        G = M.f32(GLU_OFF, (8, nseg, GW))
        for jb in range(4):
            wv = wload(win_v[:, :, WIN_CONV0 + jb * 512: WIN_CONV0 + (jb + 1) * 512], (16, 512))
            for jj in range(2):
                j = jb * 2 + jj
                bv = next_abank()
                proj_chunk(wv, jj * 256, 128, bv)
                bg = next_abank()
                proj_chunk(wv, jj * 256 + 128, 128, bg)
                sgt = CT[0]
                P.act(sgt[:, 0:Tt], bg[:, 0:Tt], AF.Sigmoid)
                for si, sg in enumerate(segs):
                    s0, L, S_ = sg["start"], sg["L"], sg["st"]
                    P.copy("pool", G[:, j, si, 0:30], S_.CH[:, j, 0:30])
                    P.tt("dve", G[:, j, si, 30:30 + L], bv[:, s0:s0 + L], sgt[:, s0:s0 + L], ALU.mult)
                    P.copy("pool", S_.CH[:, j, 0:30], G[:, j, si, L:L + 30])
        chk(6)
        for j in range(8):
            for si, sg in enumerate(segs):
                s0, L = sg["start"], sg["L"]
                acc = CO[:, j, s0:s0 + L]
                P.ts("dve", acc, G[:, j, si, 0:L], pt("cw", 0 * 8 + j), ALU.mult, pt("conv_b", j), ALU.add)
                for w in range(1, 31):
                    P.stt(acc, G[:, j, si, w:w + L], pt("cw", w * 8 + j), acc, ALU.mult, ALU.add)
            P.mm(MS[:, 0:Tt], ONES, CO[:, j, 0:Tt], start=(j == 0), stop=(j == 7))
            sq = CT[1 + (j % 2)]
            P.act(sq[:, 0:Tt], CO[:, j, 0:Tt], AF.Square)
            P.mm(TR[:, 0:Tt], ONES, sq[:, 0:Tt], start=(j == 0), stop=(j == 7))
        mean, m2, rstd = CT[3], CT[4], CT[5]
        P.ts("dve", mean[:, 0:Tt], MS[:, 0:Tt], 1.0 / 1024, ALU.mult)
        P.tt("dve", m2[:, 0:Tt], mean[:, 0:Tt], mean[:, 0:Tt], ALU.mult)
        P.stt(m2[:, 0:Tt], TR[:, 0:Tt], 1.0 / 1024, m2[:, 0:Tt], ALU.mult, ALU.subtract)
        P.act(m2[:, 0:Tt], m2[:, 0:Tt], AF.Sqrt, bias=LN_EPS, scale=1.0)
        P.recip(rstd[:, 0:Tt], m2[:, 0:Tt])
        for j in range(8):
            t1 = CT[1 + (j % 2)]
            P.tt("dve", t1[:, 0:Tt], CO[:, j, 0:Tt], mean[:, 0:Tt], ALU.subtract)
            P.tt("dve", t1[:, 0:Tt], t1[:, 0:Tt], rstd[:, 0:Tt], ALU.mult)
            P.act(YCAT[:, 8 + j, 0:Tt], t1[:, 0:Tt], AF.Silu, bias=pt("cln_b", j), scale=pt("cln_g", j))

        chk(7)
        wv = wload(win_v[:, :, WIN_LORA0:WIN_LORA0 + 288], (16, 288))
        b_ = next_abank()
        proj_chunk(wv, 0, 128, b_)
        shift_epi(b_, 128, 0, L1F[:, 0:Tt])
        P.act(L1S[0:64, 0:Tt], L1F[0:64, 0:Tt], AF.Tanh)
        P.copy("pool", L1S[64:128, 0:Tt], L1F[64:128, 0:Tt])
        b_ = next_abank()
        proj_chunk(wv, 128, 128, b_)
        shift_epi(b_, 128, 1, L1F[:, 0:Tt])
        P.act(SG1[:, 0:Tt], L1F[:, 0:Tt], AF.Sigmoid)
        P.memset("pool", SG2[32:64, :], 0.0)
        P.memset("pool", SG2[64:128, :], 0.0)
        b_ = next_abank()
        proj_chunk(wv, 256, 32, b_)
        shift_epi(b_, 32, 2, L1F[0:32, 0:Tt])
        P.act(SG2[0:32, 0:Tt], L1F[0:32, 0:Tt], AF.Sigmoid)

        chk(8)
        QT = min(128, Tt)
        quarters = [(h0, QT) for h0 in range(0, Tt, QT)]

        def wkv_gen(pp, h0, TH, S, rb):
            rS, kS, vS = rb
            pc = slice(pp * 128, (pp + 1) * 128)
            hc = slice(h0, h0 + TH)
            nch = TH // C
            WTs = S["WT"]
            (lws, cum, cumx, ein, einv, eex, alr, kk, t0, kkn, tk, kmod, bvec, rk, bonus, gate) = [w_[:, 0:TH] for w_ in WTs]
            BK_, BKF_, ARZ_, SC5_, TM_, ntb, mb, TTB_, RU_, UZ_, DT2 = (S["BK"], S["BKF"], S["ARZ"], S["SC5"], S["TM"],
                                                                       S["NTB"], S["MB"], S["TTB"], S["RU"], S["UZ"], S["DT"])
            pa, pb_ = S["banks"]
            P.mm(MS[:, 0:TH], WLAD[:, pc], L1S[:, hc])
            P.act(lws, MS[:, 0:TH], AF.Exp, bias=pt("nw0", pp), scale=-1.0)
            P.ts("dve", lws, lws, 1.0, ALU.add)
            P.recip(lws, lws)
            P.scan(cum, rmask[:, 0:TH], lws, 0.0, ALU.mult, ALU.add)
            P.tt("pool", cumx, cum, lws, ALU.subtract)
            yield
            P.act(ein, cum, AF.Exp, scale=-C_DEC)
            P.act(einv, cum, AF.Exp, scale=C_DEC)
            P.act(eex, cumx, AF.Exp, scale=-C_DEC)
            P.mm(MS[:, 0:TH], WLAA[:, pc], L1S[:, hc])
            P.act(alr, MS[:, 0:TH], AF.Exp, bias=pt("na0", pp), scale=-1.0)
            P.ts("dve", alr, alr, 1.0, ALU.add)
            P.recip(alr, alr)
            yield
            P.ts("pool", kk, kS[:, hc], pt("k_k", pp), ALU.mult, 0.0, ALU.add)
            P.act(t0, kk, AF.Square)
            P.mm(MS[:, 0:TH], BONES, t0)
            P.ts("dve", t0, MS[:, 0:TH], 1e-24, ALU.max)
            yield
            P.act(t0, t0, AF.Ln)
            P.act(t0, t0, AF.Exp, scale=-0.5)
            P.tt("pool", kkn, kk, t0, ALU.mult)
            P.ts("pool", tk, alr, pt("k_a", pp), ALU.mult, pt("omk_a", pp), ALU.add)
            P.tt("pool", kmod, kS[:, hc], tk, ALU.mult)
            yield
            for hd in range(2):
                rw = slice(hd * 64, hd * 64 + 64)
                P.stt(R(ARZ_[hd][rw, 0, 0:TH]), kkn[rw], -1.0, eex[rw], ALU.mult, ALU.mult)
                P.tt("dve", R(ARZ_[hd][rw, 1, 0:TH]), rS[rw, hc], ein[rw], ALU.mult)
            yield
            P.tt("pool", bvec, kkn, alr, ALU.mult)
            P.tt("dve", R(BK_[:, 0, 0:TH]), bvec, einv, ALU.mult)
            P.tt("dve", R(BK_[:, 1, 0:TH]), kmod, einv, ALU.mult)
            e0 = WTs[3][:, 0:1]
            wcb = AP(e0.tensor, e0.offset + C - 1, [list(e0.ap[0]), [0, 2], [C, nch], [0, C]])
            P.tt("dve", BKF_[:, :, 0:TH].rearrange("p a (c t) -> p a c t", c=nch),
                 BK_[:, :, 0:TH].rearrange("p a (c t) -> p a c t", c=nch), wcb, ALU.mult)
            yield
            P.stt(rk, rS[:, hc], pt("r_k", pp), kmod, ALU.mult, ALU.mult)
            P.mm(MS[:, 0:TH], BONES, rk)
            P.tt("dve", bonus, MS[:, 0:TH], vS[:, hc], ALU.mult)
            yield
            P.mm(MS[:, 0:TH], WG1[:, pc], SG1[:, hc], start=True, stop=False)
            P.mm(MS[:, 0:TH], WG2[:, pc], SG2[:, hc], start=False, stop=True)
            P.copy("act", gate, MS[:, 0:TH])
            yield
            for c in range(nch):
                cc = slice(c * C, (c + 1) * C)
                tb_ = (pa, pb_)[c % 2]
                P.transpose(tb_[0:C, 0:128], BKF_[:, 0, cc], IDF)
                P.transpose(tb_[0:C, 128:256], BKF_[:, 1, cc], IDF)
                P.transpose(tb_[0:C, 256:384], vS[:, h0 + c * C:h0 + (c + 1) * C], IDF)
            yield
            for c in range(nch):
                tb_ = (pa, pb_)[c % 2]
                P.copy("act", R(TM_[0:C, c, 0:2, :]), view(tb_[0:C, 0:256], (2, 128)))
                vz0 = TM_[0:C, c, 2, 0:64]
                vzo = AP(vz0.tensor, vz0.offset, [list(vz0.ap[0]), [192, 2], [1, 64]])
                P.copy("act", R(vzo), view(tb_[0:C, 256:384], (2, 64)))
            yield
            for c in range(nch):
                cc = slice(c * C, (c + 1) * C)
                for hd in range(2):
                    g = c * 2 + hd
                    psb = (pa, pb_)[g % 2]
                    P.mm(psb[0:C, 0:2 * C], R(BK_[:, 0, cc]), R(ARZ_[hd][:, :, cc]))
                    P.mm(psb[0:C, 2 * C:4 * C], R(BK_[:, 1, cc]), R(ARZ_[hd][:, :, cc]))
                    P.mm(psb[0:C, 4 * C:5 * C], R(ARZ_[hd][:, 0, cc]), R(BK_[:, 0, cc]))
                    P.tt("dve", R(SC5_[0:C, g, :, 0:C]), view(psb[0:C, 0:5 * C], (5, C)), MK[0:C, :, 0:C], ALU.mult)
                    yield
            gn = 2 * nch
            pav = view(pa[0:C, :], (4, 128))
            pbv = view(pb_[0:C, :], (4, 128))
            for q in range(gn):
                P.mm(pav[:, q, 0:C], R(SC5_[0:C, q, 4, 0:C]), R(SC5_[0:C, q, 0, 0:C]))
                P.mm(pav[:, q, C:2 * C], R(SC5_[0:C, q, 0, 0:C]), R(SC5_[0:C, q, 4, 0:C]))
            yield
            P.copy("act", R(ntb[0][0:C, 0:gn, 0:2 * C]), pav[:, 0:gn, 0:2 * C])
            idb_ = IDF[0:C, 0:C]
            idbc = AP(idb_.tensor, idb_.offset, [list(idb_.ap[0]), [0, gn], [1, C]])
            P.tt("dve", R(mb[0][0:C, 0:gn, 0:C]), SC5_[0:C, 0:gn, 0, 0:C], idbc, ALU.add)
            yield
            for m in range(1, nlev):
                last = (m == nlev - 1)
                cur = (m - 1) % 2
                nxt = m % 2
                for q in range(gn):
                    Nc = R(ntb[cur][0:C, q, 0:C])
                    Mc = R(ntb[cur][0:C, q, C:2 * C])
                    Tc = R(mb[cur][0:C, q, 0:C])
                    P.mm(pbv[:, q, 0:C], Mc, Tc)
                    if not last:
                        if m < nlev - 2:
                            P.mm(pav[:, q, 0:C], Mc, Nc)
                        P.mm(pav[:, q, C:2 * C], Nc, Mc)
                yield
                if not last:
                    P.copy("act", R(ntb[nxt][0:C, 0:gn, 0:2 * C]), pav[:, 0:gn, 0:2 * C])
                    P.tt("dve", R(mb[nxt][0:C, 0:gn, 0:C]), pbv[:, 0:gn, 0:C], mb[cur][0:C, 0:gn, 0:C], ALU.add)
                else:
                    P.tt("dve", R(TTB_[0:C, 0:gn, 0:C]), pbv[:, 0:gn, 0:C], mb[cur][0:C, 0:gn, 0:C], ALU.add)
                yield
            for c in range(nch):
                cc = slice(c * C, (c + 1) * C)
                tok0 = h0 + c * C
                sg = [s_ for s_ in segs if s_["start"] <= tok0 < s_["start"] + s_["L"]][0]
                H = sg["st"].HBD[:, pp, :]
                P.mm(pa[0:C, 0:128], R(ARZ_[0][:, 0, cc]), R(H), start=True, stop=False)
                P.mm(pa[0:C, 0:128], R(ARZ_[1][:, 0, cc]), R(H), start=False, stop=False)
                for hd in range(2):
                    hv = slice(hd * 64, hd * 64 + 64)
                    P.mm(pa[0:C, hv], R(SC5_[0:C, c * 2 + hd, 2, 0:C]), R(TM_[0:C, c, 2 + hd, hv]), start=False, stop=(hd == 1))
                yield
                P.copy("act", R(RU_[0:C, :]), pa[0:C, 0:128])
                yield
                for hd in range(2):
                    hv = slice(hd * 64, hd * 64 + 64)
                    P.mm(pa[0:C, 128 + hd * 64:192 + hd * 64], R(TTB_[0:C, c * 2 + hd, 0:C]), R(RU_[0:C, hv]))
                yield
                uz0 = UZ_[0][0:C, 0:64]
                uzo = AP(uz0.tensor, uz0.offset, [list(uz0.ap[0]), [192, 2], [1, 64]])
                P.copy("act", R(uzo), view(pa[0:C, 128:256], (2, 64)))
                yield
                yo_ = pb_[:, cc]
                P.mm(yo_, R(H), R(ARZ_[0][:, 1, cc]), start=True, stop=False)
                P.mm(yo_, R(H), R(ARZ_[1][:, 1, cc]), start=False, stop=False)
                for hd in range(2):
                    P.mm(yo_, R(UZ_[hd][0:C, :]), R(SC5_[0:C, c * 2 + hd, 1, 0:C]), start=False, stop=False)
                for hd in range(2):
                    P.mm(yo_, R(TM_[0:C, c, 2 + hd, :]), R(SC5_[0:C, c * 2 + hd, 3, 0:C]), start=False, stop=(hd == 1))
                dps = pa[:, 256:384]
                P.mm(dps, R(TM_[0:C, c, 0, :]), R(UZ_[0][0:C, :]), start=True, stop=False)
                P.mm(dps, R(TM_[0:C, c, 0, :]), R(UZ_[1][0:C, :]), start=False, stop=False)
                P.mm(dps, R(TM_[0:C, c, 1, :]), R(TM_[0:C, c, 2, :]), start=False, stop=False)
                P.mm(dps, R(TM_[0:C, c, 1, :]), R(TM_[0:C, c, 3, :]), start=False, stop=True)
                yield
                P.tt("dve", DT2, dps, BONES, ALU.mult)
                P.stt(R(H), H, WTs[3][:, (c + 1) * C - 1:(c + 1) * C], DT2, ALU.mult, ALU.add)
                yield
            ysb, dd, dsq = lws, cum, cumx
            P.copy("act", ysb, pb_[:, 0:TH])
            yield
            P.mm(MS[:, 0:TH], CMAT, ysb)
            P.copy("act", dd, MS[:, 0:TH])
            P.act(dsq, MS[:, 0:TH], AF.Square)
            yield
            P.mm(MS[:, 0:TH], BONES, dsq)
            P.act(t0, MS[:, 0:TH], AF.Ln, bias=GN_EPS, scale=1.0 / 64)
            yield
            P.act(t0, t0, AF.Exp, scale=-0.5)
            P.tt("dve", dd, dd, t0, ALU.mult)
            yield
            P.ts("dve", dd, dd, pt("lnx_g", pp), ALU.mult, pt("lnx_b", pp), ALU.add)
            P.tt("dve", dd, dd, bonus, ALU.add)
            P.tt("dve", YCAT[:, pp, hc], dd, gate, ALU.mult)

        def slot_gen(s_i):
            for pp in range(s_i, 8, 2):
                wv = wload(win_v[:, :, WIN_RKV0 + pp * 384: WIN_RKV0 + (pp + 1) * 384], (16, 384))
                rb = RKV[s_i]
                for i in range(3):
                    b_ = next_abank()
                    proj_chunk(wv, i * 128, 128, b_)
                    shift_epi(b_, 128, 3 + pp * 3 + i, rb[i][:, 0:Tt])
                    yield
                for (h0, TH) in quarters:
                    yield from wkv_gen(pp, h0, TH, STREAMS[s_i], rb)

        g0, g1 = slot_gen(0), slot_gen(1)
        alive = [True, True]
        lead = 3 + 16 if Tt >= 256 else 3
        for _ in range(lead):
            next(g0)
        while alive[0] or alive[1]:
            for k_, g_ in enumerate((g0, g1)):
                if alive[k_]:
                    try:
                        next(g_)
                    except StopIteration:
                        alive[k_] = False

        chk(15)
        for ob in range(4):
            wv = wload(wout_v[:, :, ob * 512:(ob + 1) * 512], (16, 512))
            for tb in range(nb):
                b_ = next_abank()
                for kc in range(16):
                    P.mm(b_[0:tbs, :], YCAT[:, kc, tb * 128:tb * 128 + tbs], wv[:, kc, :], start=(kc == 0), stop=(kc == 15))
                xa = XT[0:tbs, tb, ob * 512:(ob + 1) * 512]
                P.tt("dve", xa, xa, b_[0:tbs, :], ALU.add)

        chk(16)
        rmsnorm_to_hT("nffn", HN_F)
        P.dma("sp", FN, final_norm.partition_broadcast(128))

        chk(17)
        zb_i = [0]
        for jb in range(22):
            wv = wload(wup_v[:, :, jb * 512:(jb + 1) * 512], (16, 512))
            for jj in range(2):
                j = jb * 2 + jj
                bu = next_abank()
                proj_chunk(wv, jj * 256, 128, bu)
                bz = (TR, MS)[zb_i[0] % 2]
                zb_i[0] += 1
                proj_chunk(wv, jj * 256 + 128, 128, bz)
                ub = UB[j % 2]
                cv = CV[j % 2]
                gl = GL[j % 2]
                for si, sg in enumerate(segs):
                    s0, L, S_ = sg["start"], sg["L"], sg["st"]
                    o_ = si * (Lmax + 2)
                    P.copy("pool", ub[:, o_:o_ + 2], S_.FH[:, j, :])
                    P.copy("act", ub[:, o_ + 2:o_ + 2 + L], bu[:, s0:s0 + L])
                    P.copy("pool", S_.FH[:, j, :], ub[:, o_ + L:o_ + L + 2])
                    P.act(cv[:, s0:s0 + L], ub[:, o_ + 2:o_ + 2 + L], AF.Identity, bias=pt("fb", j), scale=pt("fw", 2 * NJ + j))
                    P.stt(cv[:, s0:s0 + L], ub[:, o_ + 1:o_ + 1 + L], pt("fw", 1 * NJ + j), cv[:, s0:s0 + L], ALU.mult, ALU.add)
                    P.stt(cv[:, s0:s0 + L], ub[:, o_:o_ + L], pt("fw", 0 * NJ + j), cv[:, s0:s0 + L], ALU.mult, ALU.add)
                P.act(gl[:, 0:Tt], cv[:, 0:Tt], AF.Gelu_apprx_tanh)
                P.tt("dve", GB[:, j, 0:Tt], gl[:, 0:Tt], bz[:, 0:Tt], ALU.mult)

        chk(18)
        dbanks = (IA, IB, SQ, YT)
        kgroups = ((0, 16), (16, 32), (32, 44))
        for ob in range(4):
            for (k0, k1) in kgroups:
                wv = wload(wdn_v[:, k0:k1, ob * 512:(ob + 1) * 512], (k1 - k0, 512))
                for tb in range(nb):
                    for kc in range(k0, k1):
                        P.mm(dbanks[tb][0:tbs, :], GB[:, kc, tb * 128:tb * 128 + tbs], wv[:, kc - k0, :], start=(kc == 0), stop=(kc == NJ - 1))
            for tb in range(nb):
                xa = XT[0:tbs, tb, ob * 512:(ob + 1) * 512]
                P.tt("dve", xa, xa, dbanks[tb][0:tbs, :], ALU.add)

        chk(19)
        for tb in range(nb):
            yo = YO[tb % 2]
            xa = XT[0:tbs, tb, :]
            ss = SMALL[0:tbs, 24 + tb:25 + tb]
            P.act(yo[0:tbs, :], xa, AF.Square, accum_out=ss)
            sd = SMALL[0:tbs, 32 + tb:33 + tb]
            P.act(sd, ss, AF.Sqrt, bias=RMS_EPS, scale=1.0 / D)
            rs = SMALL[0:tbs, 40 + tb:41 + tb]
            P.recip(rs, sd)
            P.stt(yo[0:tbs, :], xa, rs, FN[0:tbs, :], ALU.mult, ALU.mult)
            for (r0, r1, dst) in y_stores(tb):
                P.dma("act", dst, yo[r0:r1, :])

    def store_states(S_, o_shift, o_wkv, o_conv, o_ffn):
        stg = ST32[0]
        P.transpose(TR[0:32, 0:128], S_.SH[:, 0:32], IDF)
        P.copy("dve", stg[0:32, 0:128], TR[0:32, 0:128])
        P.dma("act", rows2d(o_shift[3072:3328], 128), stg[0:2, 0:128])
        P.dma("act", o_shift[3328:3360].rearrange("(a b) -> a b", a=1), stg[2:3, 0:32])
        for pp in range(8):
            for i in range(3):
                r = 3 + pp * 3 + i
                P.dma("act", rows2d(o_shift[i * 1024 + pp * 128:i * 1024 + (pp + 1) * 128], 128), stg[r:r + 1, 0:128])
        for pp in range(8):
            P.transpose(TR[:, 128:256], S_.HBD[:, pp, :], IDF)
            P.copy("act", stg[:, 128 + pp * 128:256 + pp * 128], TR[:, 128:256])
            for hd in range(2):
                P.dma("act", o_wkv[2 * pp + hd], stg[hd * 64:hd * 64 + 64, 128 + pp * 128 + hd * 64:128 + pp * 128 + hd * 64 + 64])
        for j in range(8):
            P.transpose(TR[0:32, 256:384], S_.CH[:, j, :], IDF)
            P.copy("dve", stg[0:30, 1280 + j * 128:1280 + (j + 1) * 128], TR[0:30, 256:384])
        P.dma("act", o_conv, stg[0:30, 1280:2304])
        P.copy("dve", view(stg[:, 2304:2392], (2, NJ)), S_.FH.rearrange("p j t -> p t j"))
        P.transpose(TR[0:96, 384:512], stg[:, 2304:2400], IDF)
        P.copy("dve", stg[0:88, 2432:2560], TR[0:88, 384:512])
        P.dma("act", o_ffn.rearrange("t (j p) -> (t j) p", p=128), stg[0:88, 2432:2560])

    try:
        def small_y(tb):
            return [(16, 32, ys)]
        tile(32, [dict(start=0, L=16, st=STA), dict(start=16, L=16, st=STB)], 16,
             [(XT[0:16, 0, :], meta), (XT[16:32, 0, :], xs)], small_y)
        store_states(STB, o_shift_s, o_wkv_s, o_conv_s, o_ffn_s)
        for q in range(NSEQ):
            P.copy("pool", STB.SH, STA.SH)
            P.copy("pool", R(STB.HBD), STA.HBD)
            P.copy("pool", STB.CH, STA.CH)
            P.copy("pool", STB.FH, STA.FH)
            for tt_ in range(NT):
                def main_y(tb, q=q, tt_=tt_):
                    return [(0, 128, yp[q, tt_ * T + tb * 128: tt_ * T + (tb + 1) * 128, :])]
                tile(T, [dict(start=0, L=T, st=STB)], 64,
                     [(XT, xp[q, tt_ * T:(tt_ + 1) * T, :].rearrange("(nb p) d -> p nb d", p=128))], main_y)
            store_states(STB, o_shift_p[q], o_wkv_p[q], o_conv_p[q], o_ffn_p[q])


    except _Stop:
        pass
    P.emit()
    st.close()
    return nc


_NC_CACHE = {}


def _get_nc(NSEQ, NT, debug=None):
    key = (NSEQ, NT, debug)
    if key not in _NC_CACHE:
        _NC_CACHE[key] = build_program(NSEQ, NT, debug)
    return _NC_CACHE[key]


WEIGHT_KEYS = ("meta", "norm_mix", "w_in", "tshift_mu", "w0", "w_decay_up", "a0", "w_aaa_up", "w_gate_up",
               "k_k", "k_a", "r_k", "lnx_g", "lnx_b", "conv_w", "conv_b", "conv_ln_g", "conv_ln_b",
               "w_out", "norm_ffn", "w_ffn_up", "ffn_conv_w", "ffn_conv_b", "w_ffn_down", "final_norm")


def run_cores(inputs, n_cores, NSEQ, NT, debug=None):
    f = lambda a: np.ascontiguousarray(np.asarray(a, dtype=np.float32))
    shared = {}
    for k in WEIGHT_KEYS:
        a = f(inputs[k])
        if k in ("meta", "final_norm"):
            shared[k] = a
        elif k == "r_k":
            shared[k] = a.reshape(-1)
        else:
            shared[k] = a[0] if a.shape[0] == 1 else a
    in_maps = []
    xpf = f(inputs["x_prompt"])
    for c in range(n_cores):
        m = dict(shared)
        m["xp"] = np.ascontiguousarray(xpf[c * NSEQ:(c + 1) * NSEQ, :NT * 512])
        m["xs"] = f(inputs["x_sample"][c])
        m["st_shift"] = f(inputs["state_shift"][0, c, 0])
        m["st_wkv"] = f(inputs["state_wkv"][0, c])
        m["st_conv"] = f(inputs["cache_conv"][0, c])
        m["st_ffn"] = f(inputs["cache_ffn_conv"][0, c])
        in_maps.append(m)
    nc = _get_nc(NSEQ, NT, debug)
    res = run_bass_kernel_spmd(nc, in_maps, core_ids=list(range(n_cores)))
    return res.results


def kernel(**inputs):
    n = 8
    r = run_cores(inputs, n, 2, 4)
    cat = lambda k: np.concatenate([np.asarray(r[c][k]) for c in range(n)], axis=0)
    stk = lambda k: np.stack([np.asarray(r[c][k]) for c in range(n)], axis=0)
    y_prompt = cat("yp")
    y_sample = stk("ys")
    o_shift_p = cat("o_shift_p")[None, :, None, :]
    o_wkv_p = cat("o_wkv_p")[None]
    o_conv_p = cat("o_conv_p")[None]
    o_ffn_p = cat("o_ffn_p")[None]
    o_shift_s = stk("o_shift_s")[None, :, None, :]
    o_wkv_s = stk("o_wkv_s")[None]
    o_conv_s = stk("o_conv_s")[None]
    o_ffn_s = stk("o_ffn_s")[None]
    outs = (y_prompt, y_sample, o_shift_p, o_wkv_p, o_conv_p, o_ffn_p, o_shift_s, o_wkv_s, o_conv_s, o_ffn_s)
    return tuple(np.ascontiguousarray(o, dtype=np.float32) for o in outs)
```

```python
import numpy as np
import concourse.bass as bass
import concourse.mybir as mybir

F32 = mybir.dt.float32
BF16 = mybir.dt.bfloat16
AF = mybir.ActivationFunctionType
ALU = mybir.AluOpType

_ESZ = {F32: 4, BF16: 2}
try:
    _ESZ[mybir.dt.float32r] = 4
except Exception:
    pass


def _esize(dt):
    if dt in _ESZ:
        return _ESZ[dt]
    s = str(dt)
    if "64" in s:
        return 8
    if "32" in s:
        return 4
    if "16" in s:
        return 2
    return 1


SB_BASE = {}


def ap_box(ap):
    es = _esize(ap.dtype)
    a = ap.ap
    off = ap.offset
    name = ap.tensor.name
    space = str(ap.space)
    if "SB" in space or "PSUM" in space:
        base = SB_BASE.get(name) if "PSUM" not in space else None
        if base is not None:
            name = "SBUF"
            base_b = base
        else:
            base_b = 0
        pstep, pcount = a[0]
        if pstep == 0:
            p0 = 0
            f0 = off
            p1 = 1
        else:
            p0 = off // pstep
            f0 = off % pstep
            p1 = p0 + pcount
        lo = f0
        hi = f0
        for st, cnt in a[1:]:
            ext = st * (cnt - 1)
            if ext < 0:
                lo += ext
            else:
                hi += ext
        if "PSUM" in space:
            return (name, (p0 // 32) * 32, ((p1 + 31) // 32) * 32, 0, 1 << 20)
        return (name, p0, p1, base_b + lo * es, base_b + (hi + 1) * es)
    lo = off
    hi = off
    for st, cnt in a:
        ext = st * (cnt - 1)
        if ext < 0:
            lo += ext
        else:
            hi += ext
    return (name, 0, 1, lo * es, (hi + 1) * es)


def _overlap(a, b):
    return a[1] < b[2] and b[1] < a[2] and a[3] < b[4] and b[3] < a[4]


def _contains(outer, inner):
    return outer[1] <= inner[1] and inner[2] <= outer[2] and outer[3] <= inner[3] and inner[4] <= outer[4]


class Op:
    __slots__ = ("eng", "fn", "idx", "deps", "is_dma", "signal", "sig", "waits", "ring", "ring_val", "gidx")

    def __init__(self, eng, fn, is_dma=False):
        self.eng = eng
        self.fn = fn
        self.is_dma = is_dma
        self.deps = []
        self.signal = False
        self.sig = None
        self.waits = []
        self.ring = None
        self.ring_val = None


ENGS = ("pe", "act", "dve", "pool", "sp")
SIG_WRAP = 16000


class Prog:
    def __init__(self, nc, n_dma_rings=12):
        self.nc = nc
        self.ops = {e: [] for e in ENGS}
        self.all = []
        self.hist = {}
        self.n_rings = n_dma_rings
        self.ring_last = [None] * n_dma_rings
        self.ring_cnt = [0] * n_dma_rings
        self.ring_next = 0

    def _track(self, op, reads, writes):
        rd2 = []
        writes = list(writes)
        for ap in reads:
            if "PSUM" in str(ap.space):
                writes.append(ap)
            else:
                rd2.append(ap)
        reads = rd2
        deps = set()
        for ap in reads:
            bx = ap_box(ap)
            h = self.hist.setdefault(bx[0], [])
            for (b, o, w) in h:
                if w and _overlap(b, bx):
                    deps.add(o)
        for ap in writes:
            bx = ap_box(ap)
            h = self.hist.setdefault(bx[0], [])
            for (b, o, w) in h:
                if _overlap(b, bx):
                    deps.add(o)
        deps.discard(op)
        for ap in reads:
            bx = ap_box(ap)
            h = self.hist[bx[0]]
            if not op.is_dma:
                h[:] = [e for e in h if not ((not e[2]) and e[0] == bx and e[1].eng == op.eng and not e[1].is_dma)]
            h.append((bx, op, False))
        for ap in writes:
            bx = ap_box(ap)
            h = self.hist[bx[0]]
            h[:] = [e for e in h if not _contains(bx, e[0])]
            h.append((bx, op, True))
        op.deps = list(deps)

    def add(self, eng, fn, reads=(), writes=()):
        op = Op(eng, fn)
        op.idx = len(self.ops[eng])
        op.gidx = len(self.all)
        self.ops[eng].append(op)
        self.all.append(op)
        self._track(op, reads, writes)
        return op

    def dma(self, queue, out, in_, **kw):
        def fn(e, out=out, in_=in_, kw=kw):
            return e.dma_start(out=out, in_=in_, **kw)
        op = Op(queue, fn, is_dma=True)
        op.idx = len(self.ops[queue])
        op.gidx = len(self.all)
        self.ops[queue].append(op)
        self.all.append(op)
        self._track(op, [in_], [out])
        r = self.ring_next
        self.ring_next = (self.ring_next + 1) % self.n_rings
        prev = self.ring_last[r]
        if prev is not None and prev not in op.deps:
            op.deps.append(prev)
        self.ring_cnt[r] += 1
        op.ring = r
        op.ring_val = 16 * self.ring_cnt[r]
        self.ring_last[r] = op
        return op

    def finalize(self):
        last_vc = {e: {} for e in ENGS}
        dma_known = {e: {} for e in ENGS}
        op_vc = {}
        for op in self.all:
            E = op.eng
            vc = dict(last_vc[E])
            waits = []
            best = {}
            for d in sorted(op.deps, key=lambda o: o.gidx):
                if d.is_dma:
                    if dma_known[E].get(d.ring, 0) >= d.ring_val:
                        continue
                    dma_known[E][d.ring] = d.ring_val
                    waits.append(d)
                    continue
                if d.eng not in best or d.idx > best[d.eng].idx:
                    best[d.eng] = d
            for De, d in best.items():
                if De == E:
                    if E in ("pe", "sp"):
                        continue
                    if vc.get(E, -1) >= d.idx:
                        continue
                    vc[E] = d.idx
                    waits.append(d)
                    d.signal = True
                    continue
                if vc.get(De, -1) >= d.idx:
                    continue
                waits.append(d)
                d.signal = True
                dvc = op_vc.get(d, {})
                for k2, v2 in dvc.items():
                    if vc.get(k2, -1) < v2:
                        vc[k2] = v2
                if vc.get(De, -1) < d.idx:
                    vc[De] = d.idx
            op.waits = waits
            if not op.is_dma:
                vcc = dict(vc)
                vcc[E] = op.idx
                op_vc[op] = vcc
            last_vc[E] = vc
        self.nsig = {}
        for e in ENGS:
            c = 0
            for op in self.ops[e]:
                if op.is_dma:
                    continue
                if op.signal:
                    op.sig = c
                    c += 1
            self.nsig[e] = c

    def emit(self, final_waits=()):
        nc = self.nc
        self.finalize()
        import contextlib
        with contextlib.ExitStack() as st:
            sems = {}
            for e in ENGS:
                n = (self.nsig[e] + SIG_WRAP - 1) // SIG_WRAP
                sems[e] = [st.enter_context(nc.semaphore(f"s_{e}_{i}")) for i in range(max(n, 1))]
            rsems = [st.enter_context(nc.semaphore(f"s_ring_{i}")) for i in range(self.n_rings)]
            block = st.enter_context(nc.Block())

            def run(engname, eng):
                for op in self.ops[engname]:
                    for d in op.waits:
                        if d.is_dma:
                            eng.wait_ge(rsems[d.ring], d.ring_val)
                        else:
                            eng.wait_ge(sems[d.eng][d.sig // SIG_WRAP], d.sig % SIG_WRAP + 1)
                    ins = op.fn(eng)
                    if op.is_dma:
                        ins.then_inc(rsems[op.ring], 16)
                    elif op.signal:
                        ins.then_inc(sems[engname][op.sig // SIG_WRAP], 1)
                if engname in ("sp", "pool", "act"):
                    lastv = {}
                    for op in self.ops[engname]:
                        if op.is_dma:
                            lastv[op.ring] = max(lastv.get(op.ring, 0), op.ring_val)
                    for r, v in lastv.items():
                        eng.wait_ge(rsems[r], v)

            @block.tensor
            def _(t):
                run("pe", t)

            @block.scalar
            def _(a):
                run("act", a)

            @block.vector
            def _(v):
                run("dve", v)

            @block.gpsimd
            def _(g):
                run("pool", g)

            @block.sync
            def _(s):
                run("sp", s)

    def mm(self, out, lhsT, rhs, start=True, stop=True, extra_reads=()):
        def fn(e):
            return e.matmul(out, lhsT=lhsT, rhs=rhs, start=start, stop=stop)
        rd = [lhsT, rhs] + list(extra_reads)
        if not start:
            rd.append(out)
        return self.add("pe", fn, rd, [out])

    def transpose(self, out, in_, ident):
        def fn(e):
            return e.transpose(out=out, in_=in_, identity=ident)
        return self.add("pe", fn, [in_, ident], [out])

    def act(self, out, in_, func, bias=None, scale=None, accum_out=None, eng="act"):
        kw = {}
        rd = [in_]
        wr = [out]
        if bias is not None:
            kw["bias"] = bias
            if not isinstance(bias, (int, float)):
                rd.append(bias)
        if scale is not None:
            kw["scale"] = scale
            if not isinstance(scale, (int, float)):
                rd.append(scale)
        if accum_out is not None:
            kw["accum_out"] = accum_out
            wr.append(accum_out)

        def fn(e):
            return e.activation(out=out, in_=in_, func=func, **kw)
        return self.add("act", fn, rd, wr)

    def tt(self, eng, out, in0, in1, op):
        def fn(e):
            return e.tensor_tensor(out=out, in0=in0, in1=in1, op=op)
        return self.add(eng, fn, [in0, in1], [out])

    def ts(self, eng, out, in0, s1, op0, s2=None, op1=None, accum_out=None):
        rd = [in0]
        if not isinstance(s1, (int, float)):
            rd.append(s1)
        if s2 is not None and not isinstance(s2, (int, float)):
            rd.append(s2)
        wr = [out]
        kw = {}
        if accum_out is not None:
            kw["accum_out"] = accum_out
            wr.append(accum_out)

        def fn(e):
            if op1 is None:
                return e.tensor_scalar(out=out, in0=in0, scalar1=s1, scalar2=None, op0=op0, **kw)
            return e.tensor_scalar(out=out, in0=in0, scalar1=s1, scalar2=s2, op0=op0, op1=op1, **kw)
        return self.add(eng, fn, rd, wr)

    def stt(self, out, in0, scalar, in1, op0, op1):
        rd = [in0, in1]
        if not isinstance(scalar, (int, float)):
            rd.append(scalar)

        def fn(e):
            return e.scalar_tensor_tensor(out=out, in0=in0, scalar=scalar, in1=in1, op0=op0, op1=op1)
        return self.add("dve", fn, rd, [out])

    def copy(self, eng, out, in_):
        if eng == "act":
            def fn(e):
                return e.copy(out=out, in_=in_)
        else:
            def fn(e):
                return e.tensor_copy(out=out, in_=in_)
        return self.add(eng, fn, [in_], [out])

    def memset(self, eng, ap, val):
        def fn(e):
            return e.memset(ap, val)
        return self.add(eng, fn, [], [ap])

    def recip(self, out, in_):
        def fn(e):
            return e.reciprocal(out=out, in_=in_)
        return self.add("dve", fn, [in_], [out])

    def scan(self, out, d0, d1, init, op0, op1):
        rd = [d0, d1]
        if not isinstance(init, (int, float)):
            rd.append(init)

        def fn(e):
            return e.tensor_tensor_scan(out=out, data0=d0, data1=d1, initial=init, op0=op0, op1=op1)
        return self.add("dve", fn, rd, [out])

    def affine_select(self, out, in_, compare_op, fill, base, pattern, channel_multiplier):
        def fn(e):
            return e.affine_select(out=out, in_=in_, compare_op=compare_op, fill=fill, base=base,
                                   pattern=pattern, channel_multiplier=channel_multiplier)
        return self.add("pool", fn, [in_], [out])

from concourse.bass_utils import run_bass_kernel_spmd
from concourse.ap import AP
import contextlib

D = 2048
RW = 1024
DFF = 5632
NJ = 44
INC = 5408
RWC = 3360
C_DEC = 0.6065306597126334
RMS_EPS = 1e-6
LN_EPS = 1e-5
GN_EPS = 64e-5
F32R = mybir.dt.float32r


def R(ap):
    return ap.bitcast(F32R)

WIN_CONV0 = 0
WIN_LORA0 = 2048
WIN_RKV0 = 2336


def _prod(s):
    r = 1
    for v in s:
        r *= v
    return r


def view(ap2d, shape):
    if len(shape) == 1:
        return ap2d
    if len(shape) == 2:
        return ap2d.rearrange("p (a b) -> p a b", a=shape[0])
    if len(shape) == 3:
        return ap2d.rearrange("p (a b c) -> p a b c", a=shape[0], b=shape[1])
    raise ValueError(shape)


class Mem:
    def __init__(self, t):
        self.t = t

    def f32(self, off, shape):
        n = _prod(shape)
        return view(self.t[:, off:off + n], shape)

    def bf(self, off, shape):
        n = _prod(shape)
        assert n % 2 == 0
        return view(self.t[:, off:off + n // 2].bitcast(BF16), shape)


class _Stop(Exception):
    pass


def build_program(NSEQ=2, NT=4, debug=None):
    def chk(stage):
        if debug is not None and debug == stage:
            raise _Stop()
    nc = bass.Bass("TRN2", target_bir_lowering=False)
    P = Prog(nc)
    T = 512
    SEQ = NT * T

    def din(name, shape):
        return nc.dram_tensor(name, list(shape), F32, kind="ExternalInput").ap()

    def dout(name, shape):
        return nc.dram_tensor(name, list(shape), F32, kind="ExternalOutput").ap()

    xp = din("xp", (NSEQ, SEQ, D))
    xs = din("xs", (16, D))
    st_shift = din("st_shift", (RWC,))
    st_wkv = din("st_wkv", (16, 64, 64))
    st_conv = din("st_conv", (30, 1024))
    st_ffn = din("st_ffn", (2, DFF))
    meta = din("meta", (16, D))
    norm_mix = din("norm_mix", (D,))
    w_in = din("w_in", (D, INC))
    tshift_mu = din("tshift_mu", (RWC,))
    w0 = din("w0", (RW,))
    w_decay_up = din("w_decay_up", (64, RW))
    a0 = din("a0", (RW,))
    w_aaa_up = din("w_aaa_up", (64, RW))
    w_gate_up = din("w_gate_up", (160, RW))
    k_k = din("k_k", (RW,))
    k_a = din("k_a", (RW,))
    r_k = din("r_k", (RW,))
    lnx_g = din("lnx_g", (RW,))
    lnx_b = din("lnx_b", (RW,))
    conv_w = din("conv_w", (31, 1024))
    conv_b = din("conv_b", (1024,))
    conv_ln_g = din("conv_ln_g", (1024,))
    conv_ln_b = din("conv_ln_b", (1024,))
    w_out = din("w_out", (D, D))
    norm_ffn = din("norm_ffn", (D,))
    w_ffn_up = din("w_ffn_up", (D, 2 * DFF))
    ffn_conv_w = din("ffn_conv_w", (3, DFF))
    ffn_conv_b = din("ffn_conv_b", (DFF,))
    w_ffn_down = din("w_ffn_down", (DFF, D))
    final_norm = din("final_norm", (D,))

    yp = dout("yp", (NSEQ, SEQ, D))
    ys = dout("ys", (16, D))
    o_shift_p = dout("o_shift_p", (NSEQ, RWC))
    o_wkv_p = dout("o_wkv_p", (NSEQ, 16, 64, 64))
    o_conv_p = dout("o_conv_p", (NSEQ, 30, 1024))
    o_ffn_p = dout("o_ffn_p", (NSEQ, 2, DFF))
    o_shift_s = dout("o_shift_s", (RWC,))
    o_wkv_s = dout("o_wkv_s", (16, 64, 64))
    o_conv_s = dout("o_conv_s", (30, 1024))
    o_ffn_s = dout("o_ffn_s", (2, DFF))

    win_s = nc.dram_tensor("win_s", [D, INC], BF16).ap()
    wout_s = nc.dram_tensor("wout_s", [D, D], BF16).ap()
    wup_s = nc.dram_tensor("wup_s", [D, 2 * DFF], BF16).ap()
    wdn_s = nc.dram_tensor("wdn_s", [DFF, D], BF16).ap()

    st = contextlib.ExitStack()
    SB0 = 16512
    AB = 25024
    XB = AB + 10496
    XR0 = XB + 4864
    NH = 6912
    NR = 12800
    assert SB0 + 4 * (XR0 + NR) <= 229376
    arena_t = nc.alloc_sbuf_tensor_at("arena", [128, XR0], F32, offset=SB0)
    arena_h = nc.alloc_sbuf_tensor_at("arena_h", [128, NH], F32, offset=SB0 + 4 * XR0)
    arena_r = nc.alloc_sbuf_tensor_at("arena_r", [128, NR], F32, offset=SB0 + 4 * XR0)
    SB_BASE.clear()
    SB_BASE[arena_t.name] = SB0
    SB_BASE[arena_h.name] = SB0 + 4 * XR0
    SB_BASE[arena_r.name] = SB0 + 4 * XR0
    M = Mem(arena_t)
    MH = Mem(arena_h)
    MR = Mem(arena_r)
    HB = XR0 - AB
    banks = [st.enter_context(nc.psum_tensor(f"pb{i}", [128, 512], F32)) for i in range(8)]
    A0, A1, TR, MS, IA, IB, SQ, YT = banks

    XT = M.f32(0, (4, 2048))
    HT = M.bf(8192, (16, 512))
    WR = [M.bf(12288 + i * 4096, (8192,)) for i in range(2)]
    o = 20480
    NPT = 600
    PT = M.f32(o, (NPT,)); o += NPT
    IDF = M.f32(o, (128,)); o += 128
    IDB = M.bf(o, (128,)); o += 64
    MK = M.f32(o, (5, 64)); o += 320
    BONES = M.f32(o, (128,)); o += 128
    CMAT = M.f32(o, (128,)); o += 128
    ONES = M.f32(o, (128,)); o += 128
    RMASK = M.f32(o, (128,)); o += 128
    RMASK16 = M.f32(o, (32,)); o += 32
    WLAD = M.bf(o, (1024,)); o += 512
    WLAA = M.bf(o, (1024,)); o += 512
    WG1 = M.bf(o, (1024,)); o += 512
    WG2 = M.bf(o, (1024,)); o += 512

    class StateSet:
        pass
    sets = []
    for i in range(2):
        s_ = StateSet()
        s_.SH = M.f32(o, (32,)); o += 32
        s_.HBD = MR.f32(10496 + i * 1024, (8, 128))
        s_.CH = M.f32(o, (8, 32)); o += 256
        s_.FH = M.f32(o, (NJ, 2)); o += 88
        sets.append(s_)
    STA, STB = sets
    SMALL = M.f32(o, (64,)); o += 64
    assert o <= AB, (o, AB)

    YCAT = M.bf(AB + 0, (16, 512))
    PBUF = [M.f32(AB + 4096 + i * 512, (512,)) for i in range(2)]
    DTMP = [M.f32(AB + 5120 + i * 512, (512,)) for i in range(2)]
    L1S = M.bf(AB + 6144, (512,))
    SG1 = M.bf(AB + 6400, (512,))
    SG2 = M.bf(AB + 6656, (512,))
    L1F = M.f32(AB + 6912, (512,))
    RKV = [[M.f32(AB + 7424 + (b * 3 + i) * 512, (512,)) for i in range(3)] for b in range(2)]
    HN_A = [M.bf(XB + i * 1024, (2048,)) for i in range(2)]
    GLU_OFF = XB
    CT = [M.f32(XB + 4336, (512,))] + [MH.f32(4096 + i * 512, (512,)) for i in range(5)]
    CO = MH.f32(0, (8, 512))
    BONES_R = MR.f32(12544, (128,))
    CMAT_R = MR.f32(12672, (128,))
    STREAMS = []
    for s_i in range(2):
        S = {}
        S["WT"] = [M.f32(XB + s_i * 2048 + i * 128, (128,)) for i in range(16)]
        S["BKF"] = M.f32(XB + 4096 + s_i * 256, (2, 128))
        S["DT"] = M.f32(XB + 4608 + s_i * 128, (128,))
        S["BK"] = MR.f32(s_i * 256, (2, 128))
        S["SC5"] = MR.f32(512 + s_i * 1280, (4, 5, 64))
        S["NTB"] = [MR.f32(3072 + (s_i * 2 + i) * 512, (4, 128)) for i in range(2)]
        S["MB"] = [MR.f32(5120 + (s_i * 2 + i) * 256, (4, 64)) for i in range(2)]
        S["TTB"] = MR.f32(6144 + s_i * 256, (4, 64))
        S["RU"] = MR.f32(6656 + s_i * 128, (128,))
        S["ARZ"] = [MR.f32(6912 + (s_i * 2 + hd) * 256, (2, 128)) for hd in range(2)]
        S["TM"] = MR.f32(7936 + s_i * 1024, (2, 4, 128))
        S["UZ"] = [MR.f32(9984 + (s_i * 2 + hd) * 128, (128,)) for hd in range(2)]
        S["banks"] = (IA, IB) if s_i == 0 else (SQ, YT)
        STREAMS.append(S)
    GB = M.bf(AB + 0, (NJ, 512))
    FN = M.f32(AB + 11264, (2048,))
    UB = [M.f32(AB + 13312 + i * 520, (520,)) for i in range(2)]
    CV = [M.f32(AB + 14352, (512,)), MH.f32(4096, (512,))]
    YO = [MH.f32(i * 2048, (2048,)) for i in range(2)]
    GL = [MH.f32(4608 + i * 512, (512,)) for i in range(2)]
    HN_F = [M.bf(AB + 4096 + i * 1024, (2048,)) for i in range(2)]
    assert AB + 14864 <= XR0
    ST32 = [M.f32(AB + i * 5632, (5632,)) for i in range(2)]
    ST16 = [M.bf(AB + 11264, (5632,)), MH.bf(0, (5632,))]
    PSTG = [MH.f32(2816 + i * 128, (128,)) for i in range(2)]
    cols = {}
    cpos = [0]

    def pcol(name, n):
        cols[name] = cpos[0]
        cpos[0] += n
    for nm, n in (("mu", 27), ("w0", 8), ("a0", 8), ("k_k", 8), ("k_a", 8), ("r_k", 8), ("lnx_g", 8),
                  ("lnx_b", 8), ("conv_b", 8), ("cln_g", 8), ("cln_b", 8), ("nmix", 16), ("nffn", 16),
                  ("cw", 248), ("fw", 132), ("fb", 44), ("omk_a", 8), ("nw0", 8), ("na0", 8)):
        pcol(nm, n)
    assert cpos[0] <= NPT

    def pt(name, i=0, rows=slice(0, 128)):
        c = cols[name] + i
        return PT[rows, c:c + 1]

    def rows2d(ap1d, n):
        return ap1d.rearrange("(c p) -> c p", p=n)

    P.memset("pool", IDF, 0.0)
    P.affine_select(IDF, IDF, ALU.not_equal, 1.0, 0, [[-1, 128]], 1)
    P.copy("dve", IDB, IDF)
    P.memset("pool", ONES, 1.0)
    P.memset("pool", BONES, 0.0)
    P.memset("pool", BONES[0:64, 0:64], 1.0)
    P.memset("pool", BONES[64:128, 64:128], 1.0)
    P.stt(CMAT, BONES, -1.0 / 64.0, IDF, ALU.mult, ALU.add)
    P.copy("dve", R(BONES_R), BONES)
    P.copy("dve", R(CMAT_R), CMAT)
    P.memset("pool", MK[0:64], 1.0)
    for q, cmp_, cm, pat in ((0, ALU.is_gt, -1, 1), (1, ALU.is_ge, -1, 1), (2, ALU.is_gt, -1, 1),
                             (3, ALU.is_ge, -1, 1), (4, ALU.is_gt, 1, -1)):
        P.affine_select(MK[0:64, q, :], MK[0:64, q, :], cmp_, 0.0, 0, [[pat, 64]], cm)
    P.memset("pool", RMASK, 1.0)
    P.memset("pool", RMASK[:, 0:128:64], 0.0)
    P.memset("pool", RMASK16, 1.0)
    P.memset("pool", RMASK16[:, 0:32:16], 0.0)
    ZERO = M.f32(AB + 0, (2048,))
    P.memset("pool", ZERO, 0.0)
    P.copy("dve", R(MR.f32(6912, (1024,))), ZERO[:, 0:1024])
    P.copy("dve", R(MR.f32(7936, (2048,))[0:64]), ZERO[0:64, 0:2048])
    P.copy("dve", R(MR.f32(9984, (512,))[0:64]), ZERO[0:64, 0:512])
    P.memset("pool", SG2, 0.0)
    for s_ in sets:
        P.copy("dve", R(s_.HBD.rearrange("p a b -> p (a b)")), ZERO[:, 0:1024])
    P.memset("pool", STA.SH, 0.0)
    P.memset("pool", STA.CH, 0.0)
    P.memset("pool", STA.FH, 0.0)
    P.memset("pool", STB.SH, 0.0)
    P.memset("pool", STB.CH, 0.0)

    if debug is not None and debug <= 0:
        P.emit(); st.close(); return nc
    row_jobs = []

    def addrows(ap1d, name, n, base=0):
        row_jobs.append((rows2d(ap1d, 128), n, cols[name] + base))
    def rwkv_rows(src1d):
        jobs = []
        jobs.append((rows2d(src1d[3072:3328], 128), 2, 0))
        jobs.append((src1d[3328:3360].rearrange("(a b) -> a b", a=1), 1, 2))
        for pp in range(8):
            for i in range(3):
                jobs.append((rows2d(src1d[i * 1024 + pp * 128: i * 1024 + (pp + 1) * 128], 128), 1, 3 + pp * 3 + i))
        return jobs
    for (ap_, n, c) in rwkv_rows(tshift_mu):
        row_jobs.append((ap_, n, cols["mu"] + c))
    addrows(w0, "w0", 8); addrows(a0, "a0", 8); addrows(k_k, "k_k", 8); addrows(k_a, "k_a", 8)
    addrows(r_k, "r_k", 8); addrows(lnx_g, "lnx_g", 8); addrows(lnx_b, "lnx_b", 8)
    addrows(conv_b, "conv_b", 8); addrows(conv_ln_g, "cln_g", 8); addrows(conv_ln_b, "cln_b", 8)
    addrows(norm_mix, "nmix", 16); addrows(norm_ffn, "nffn", 16)
    cwf = conv_w.rearrange("w c -> (w c)")
    for blk in range(0, 248, 124):
        row_jobs.append((rows2d(cwf[blk * 128:(blk + 124) * 128], 128), 124, cols["cw"] + blk))
    fwf = ffn_conv_w.rearrange("w c -> (w c)")
    for blk in range(0, 132, 66):
        row_jobs.append((rows2d(fwf[blk * 128:(blk + 66) * 128], 128), 66, cols["fw"] + blk))
    addrows(ffn_conv_b, "fb", 44)

    def run_row_jobs(jobs, dest_fn, k0=0):
        k = k0
        i = 0
        while i < len(jobs):
            stg = PSTG[k % 2]
            k += 1
            batch = []
            used = 0
            P.memset("pool", stg, 0.0)
            while i < len(jobs) and used + jobs[i][1] <= 128:
                ap_, n, c = jobs[i]
                w = ap_.shape[1]
                P.dma("sp", stg[used:used + n, 0:w], ap_)
                batch.append((used, n, c))
                used += n
                i += 1
            P.transpose(TR[:, 0:128], stg, IDF)
            for (r0, n, c) in batch:
                P.copy("dve", dest_fn(c, n), TR[:, r0:r0 + n])
        return k
    kk_ = run_row_jobs(row_jobs, lambda c, n: PT[:, c:c + n])
    P.ts("dve", PT[:, cols["omk_a"]:cols["omk_a"] + 8], PT[:, cols["k_a"]:cols["k_a"] + 8], -1.0, ALU.mult, 1.0, ALU.add)
    P.ts("dve", PT[:, cols["nw0"]:cols["nw0"] + 8], PT[:, cols["w0"]:cols["w0"] + 8], -1.0, ALU.mult)
    P.ts("dve", PT[:, cols["na0"]:cols["na0"] + 8], PT[:, cols["a0"]:cols["a0"] + 8], -1.0, ALU.mult)

    if debug is not None and debug <= 1:
        P.emit(); st.close(); return nc
    P.memset("pool", WLAD, 0.0)
    P.memset("pool", WLAA, 0.0)
    P.memset("pool", WG2, 0.0)
    s32 = ST32[0]
    P.dma("sp", s32[0:64, 0:1024], w_decay_up)
    P.dma("sp", s32[64:128, 0:1024], w_aaa_up)
    P.copy("dve", WLAD[0:64, :], s32[0:64, 0:1024])
    P.copy("dve", WLAA[64:128, :], s32[64:128, 0:1024])
    P.dma("sp", s32[:, 1024:2048], w_gate_up[0:128, :])
    P.dma("sp", s32[0:32, 2048:3072], w_gate_up[128:160, :])
    P.copy("dve", WG1, s32[:, 1024:2048])
    P.copy("dve", WG2[0:32, :], s32[0:32, 2048:3072])

    if debug is not None and debug <= 2:
        P.emit(); st.close(); return nc
    run_row_jobs([(a_, n, c) for (a_, n, c) in rwkv_rows(st_shift)], lambda c, n: STB.SH[:, c:c + n], kk_)
    s32b = ST32[1]
    P.dma("sp", view(s32b[0:64, 0:1024], (16, 64)), st_wkv.rearrange("h v k -> v h k"))
    for pp in range(8):
        P.transpose(TR[:, 0:64], s32b[0:64, pp * 128:(pp + 1) * 128], IDF[0:64, 0:64])
        P.copy("dve", R(STB.HBD[0:64, pp, 0:64]), TR[0:64, 0:64])
        P.copy("dve", R(STB.HBD[64:128, pp, 64:128]), TR[64:128, 0:64])
    P.dma("sp", s32b[0:30, 1024:2048], st_conv)
    for j in range(8):
        P.transpose(TR[:, 0:128], s32b[:, 1024 + j * 128:1024 + (j + 1) * 128], IDF)
        P.copy("dve", STB.CH[:, j, 0:30], TR[:, 0:30])
    P.dma("sp", s32b[0:88, 2048:2176], st_ffn.rearrange("t (j p) -> (t j) p", p=128))
    P.transpose(TR[:, 128:256], s32b[:, 2048:2176], IDF)
    P.copy("dve", STB.FH.rearrange("p j t -> p t j"), view(TR[:, 128:216], (2, NJ)))

    if debug is not None and debug <= 3:
        P.emit(); st.close(); return nc
    cast_engs = ["dve", "act", "pool"]
    cast_i = [0]

    def cast(out, in_):
        e = cast_engs[cast_i[0] % 3]
        cast_i[0] += 1
        P.copy(e, out, in_)
    sidx = [0]

    def conv_rows_generic(src_rows_ap, ncols, dst_rows_ap, permute=None):
        i = sidx[0] % 2
        sidx[0] += 1
        a32 = ST32[i][:, 0:ncols]
        a16 = ST16[i][:, 0:ncols]
        P.dma("sp", a32, src_rows_ap)
        if permute is None:
            half = ncols // 2
            cast(a16[:, 0:half], a32[:, 0:half])
            cast(a16[:, half:ncols], a32[:, half:ncols])
        else:
            permute(a16, a32)
        P.dma("sp", dst_rows_ap, a16)

    def perm_win(a16, a32):
        cast(a16[:, 0:2048].rearrange("q (j t c) -> q j t c", j=8, t=2),
             a32[:, 3360:5408].rearrange("q (t j c) -> q j t c", t=2, j=8))
        cast(a16[:, 2048:2336], a32[:, 3072:3360])
        cast(a16[:, 2336:5408].rearrange("q (p i c) -> q p i c", p=8, i=3),
             a32[:, 0:3072].rearrange("q (i p c) -> q p i c", i=3, p=8))
    for kc in range(16):
        conv_rows_generic(w_in[kc * 128:(kc + 1) * 128, :], INC, win_s[kc * 128:(kc + 1) * 128, :], perm_win)
    for kc in range(16):
        conv_rows_generic(w_out[kc * 128:(kc + 1) * 128, :], D, wout_s[kc * 128:(kc + 1) * 128, :])

    def perm_up(a16, a32):
        cast(a16.rearrange("q (j t c) -> q j t c", j=22, t=2), a32.rearrange("q (t j c) -> q j t c", t=2, j=22))
    for kc in range(16):
        for hf in range(2):
            src = w_ffn_up[kc * 128:(kc + 1) * 128, :].rearrange("q (t n) -> q t n", t=2)[:, :, hf * 2816:(hf + 1) * 2816]
            i = sidx[0] % 2
            sidx[0] += 1
            a32 = ST32[i]
            a16 = ST16[i]
            P.dma("sp", view(a32, (2, 2816)), src)
            perm_up(a16, a32)
            P.dma("sp", wup_s[kc * 128:(kc + 1) * 128, hf * 5632:(hf + 1) * 5632], a16)
    for kc in range(NJ):
        conv_rows_generic(w_ffn_down[kc * 128:(kc + 1) * 128, :], D, wdn_s[kc * 128:(kc + 1) * 128, :])

    if debug is not None and debug <= 4:
        P.emit(); st.close(); return nc
    wslot = [0]

    def wload(src_ap, shape):
        s_ = WR[wslot[0] % 2]
        wslot[0] += 1
        n = _prod(shape)
        v = view(s_[:, 0:n], shape)
        P.dma("sp", v, src_ap)
        return v

    win_v = win_s.rearrange("(kc p) n -> p kc n", p=128)
    wout_v = wout_s.rearrange("(kc p) n -> p kc n", p=128)
    wup_v = wup_s.rearrange("(kc p) n -> p kc n", p=128)
    wdn_v = wdn_s.rearrange("(kc p) n -> p kc n", p=128)

    abank = [0]

    def next_abank():
        b = (A0, A1)[abank[0] % 2]
        abank[0] += 1
        return b

    def tile(Tt, segs, C, x_loads, y_stores):
        nb = max(1, Tt // 128)
        tbs = min(Tt, 128)
        nch_tile = Tt // C
        rmask = RMASK if C == 64 else RMASK16
        nlev = {64: 6, 16: 4}[C]

        for (dst, src) in x_loads:
            P.dma("sp", dst, src)

        def rmsnorm_to_hT(gname, HN):
            for tb in range(nb):
                hn = HN[tb % 2]
                xa = XT[0:tbs, tb, :]
                ss = SMALL[0:tbs, tb:tb + 1]
                P.act(hn[0:tbs, :], xa, AF.Square, accum_out=ss)
                sd = SMALL[0:tbs, 8 + tb:9 + tb]
                P.act(sd, ss, AF.Sqrt, bias=RMS_EPS, scale=1.0 / D)
                rs = SMALL[0:tbs, 16 + tb:17 + tb]
                P.recip(rs, sd)
                P.ts("dve", hn[0:tbs, :], xa, rs, ALU.mult)
                for k4 in range(4):
                    trb = TR[:, (k4 % 2) * 256:(k4 % 2) * 256 + 256].bitcast(BF16)
                    for q in range(4):
                        kc = k4 * 4 + q
                        P.transpose(trb[:, q * 128:q * 128 + tbs], hn[0:tbs, kc * 128:(kc + 1) * 128], IDB[0:tbs, 0:tbs])
                    gain = PT[:, cols[gname] + k4 * 4: cols[gname] + k4 * 4 + 4]
                    gb = AP(gain.tensor, gain.offset, [list(gain.ap[0]), [1, 4], [0, tbs]])
                    src = view(trb, (4, 128))[:, :, 0:tbs]
                    P.tt("dve", HT[:, k4 * 4:(k4 + 1) * 4, tb * 128:tb * 128 + tbs], src, gb, ALU.mult)

        rmsnorm_to_hT("nmix", HN_A)

        chk(5)
        def proj_chunk(wv, c0, Mrows, bank):
            for kc in range(16):
                P.mm(bank[0:Mrows, 0:Tt], wv[:, kc, c0:c0 + Mrows], HT[:, kc, 0:Tt], start=(kc == 0), stop=(kc == 15))

        pb_i = [0]

        def shift_epi(bank, Mrows, mcol, out_ap):
            pb = PBUF[pb_i[0] % 2]
            dt = DTMP[pb_i[0] % 2]
            pb_i[0] += 1
            P.copy("act", pb[0:Mrows, 0:Tt], bank[0:Mrows, 0:Tt])
            for sg in segs:
                s0, L, S_ = sg["start"], sg["L"], sg["st"]
                P.tt("dve", dt[0:Mrows, s0 + 1:s0 + L], pb[0:Mrows, s0:s0 + L - 1], pb[0:Mrows, s0 + 1:s0 + L], ALU.subtract)
                P.tt("dve", dt[0:Mrows, s0:s0 + 1], S_.SH[0:Mrows, mcol:mcol + 1], pb[0:Mrows, s0:s0 + 1], ALU.subtract)
                P.copy("pool", S_.SH[0:Mrows, mcol:mcol + 1], pb[0:Mrows, s0 + L - 1:s0 + L])
            P.stt(out_ap, dt[0:Mrows, 0:Tt], PT[0:Mrows, cols["mu"] + mcol:cols["mu"] + mcol + 1], pb[0:Mrows, 0:Tt], ALU.mult, ALU.add)

        nseg = len(segs)
        Lmax = max(sg["L"] for sg in segs)
        GW = 30 + Lmax
        G = M.f32(GLU_OFF, (8, nseg, GW))
        for jb in range(4):
            wv = wload(win_v[:, :, WIN_CONV0 + jb * 512: WIN_CONV0 + (jb + 1) * 512], (16, 512))
            for jj in range(2):
                j = jb * 2 + jj
                bv = next_abank()
                proj_chunk(wv, jj * 256, 128, bv)
                bg = next_abank()
                proj_chunk(wv, jj * 256 + 128, 128, bg)
                sgt = CT[0]
                P.act(sgt[:, 0:Tt], bg[:, 0:Tt], AF.Sigmoid)
                for si, sg in enumerate(segs):
                    s0, L, S_ = sg["start"], sg["L"], sg["st"]
                    P.copy("pool", G[:, j, si, 0:30], S_.CH[:, j, 0:30])
                    P.tt("dve", G[:, j, si, 30:30 + L], bv[:, s0:s0 + L], sgt[:, s0:s0 + L], ALU.mult)
                    P.copy("pool", S_.CH[:, j, 0:30], G[:, j, si, L:L + 30])
        chk(6)
        for j in range(8):
            for si, sg in enumerate(segs):
                s0, L = sg["start"], sg["L"]
                acc = CO[:, j, s0:s0 + L]
                P.ts("dve", acc, G[:, j, si, 0:L], pt("cw", 0 * 8 + j), ALU.mult, pt("conv_b", j), ALU.add)
                for w in range(1, 31):
                    P.stt(acc, G[:, j, si, w:w + L], pt("cw", w * 8 + j), acc, ALU.mult, ALU.add)
            P.mm(MS[:, 0:Tt], ONES, CO[:, j, 0:Tt], start=(j == 0), stop=(j == 7))
            sq = CT[1 + (j % 2)]
            P.act(sq[:, 0:Tt], CO[:, j, 0:Tt], AF.Square)
            P.mm(TR[:, 0:Tt], ONES, sq[:, 0:Tt], start=(j == 0), stop=(j == 7))
        mean, m2, rstd = CT[3], CT[4], CT[5]
        P.ts("dve", mean[:, 0:Tt], MS[:, 0:Tt], 1.0 / 1024, ALU.mult)
        P.tt("dve", m2[:, 0:Tt], mean[:, 0:Tt], mean[:, 0:Tt], ALU.mult)
        P.stt(m2[:, 0:Tt], TR[:, 0:Tt], 1.0 / 1024, m2[:, 0:Tt], ALU.mult, ALU.subtract)
        P.act(m2[:, 0:Tt], m2[:, 0:Tt], AF.Sqrt, bias=LN_EPS, scale=1.0)
        P.recip(rstd[:, 0:Tt], m2[:, 0:Tt])
        for j in range(8):
            t1 = CT[1 + (j % 2)]
            P.tt("dve", t1[:, 0:Tt], CO[:, j, 0:Tt], mean[:, 0:Tt], ALU.subtract)
            P.tt("dve", t1[:, 0:Tt], t1[:, 0:Tt], rstd[:, 0:Tt], ALU.mult)
            P.act(YCAT[:, 8 + j, 0:Tt], t1[:, 0:Tt], AF.Silu, bias=pt("cln_b", j), scale=pt("cln_g", j))

        chk(7)
        wv = wload(win_v[:, :, WIN_LORA0:WIN_LORA0 + 288], (16, 288))
        b_ = next_abank()
        proj_chunk(wv, 0, 128, b_)
        shift_epi(b_, 128, 0, L1F[:, 0:Tt])
        P.act(L1S[0:64, 0:Tt], L1F[0:64, 0:Tt], AF.Tanh)
        P.copy("pool", L1S[64:128, 0:Tt], L1F[64:128, 0:Tt])
        b_ = next_abank()
        proj_chunk(wv, 128, 128, b_)
        shift_epi(b_, 128, 1, L1F[:, 0:Tt])
        P.act(SG1[:, 0:Tt], L1F[:, 0:Tt], AF.Sigmoid)
        P.memset("pool", SG2[32:64, :], 0.0)
        P.memset("pool", SG2[64:128, :], 0.0)
        b_ = next_abank()
        proj_chunk(wv, 256, 32, b_)
        shift_epi(b_, 32, 2, L1F[0:32, 0:Tt])
        P.act(SG2[0:32, 0:Tt], L1F[0:32, 0:Tt], AF.Sigmoid)

        chk(8)
        QT = min(128, Tt)
        quarters = [(h0, QT) for h0 in range(0, Tt, QT)]

        def wkv_gen(pp, h0, TH, S, rb):
            rS, kS, vS = rb
            pc = slice(pp * 128, (pp + 1) * 128)
            hc = slice(h0, h0 + TH)
            nch = TH // C
            WTs = S["WT"]
            (lws, cum, cumx, ein, einv, eex, alr, kk, t0, kkn, tk, kmod, bvec, rk, bonus, gate) = [w_[:, 0:TH] for w_ in WTs]
            BK_, BKF_, ARZ_, SC5_, TM_, ntb, mb, TTB_, RU_, UZ_, DT2 = (S["BK"], S["BKF"], S["ARZ"], S["SC5"], S["TM"],
                                                                       S["NTB"], S["MB"], S["TTB"], S["RU"], S["UZ"], S["DT"])
            pa, pb_ = S["banks"]
            P.mm(MS[:, 0:TH], WLAD[:, pc], L1S[:, hc])
            P.act(lws, MS[:, 0:TH], AF.Exp, bias=pt("nw0", pp), scale=-1.0)
            P.ts("dve", lws, lws, 1.0, ALU.add)
            P.recip(lws, lws)
            P.scan(cum, rmask[:, 0:TH], lws, 0.0, ALU.mult, ALU.add)
            P.tt("pool", cumx, cum, lws, ALU.subtract)
            yield
            P.act(ein, cum, AF.Exp, scale=-C_DEC)
            P.act(einv, cum, AF.Exp, scale=C_DEC)
            P.act(eex, cumx, AF.Exp, scale=-C_DEC)
            P.mm(MS[:, 0:TH], WLAA[:, pc], L1S[:, hc])
            P.act(alr, MS[:, 0:TH], AF.Exp, bias=pt("na0", pp), scale=-1.0)
            P.ts("dve", alr, alr, 1.0, ALU.add)
            P.recip(alr, alr)
            yield
            P.ts("pool", kk, kS[:, hc], pt("k_k", pp), ALU.mult, 0.0, ALU.add)
            P.act(R(BK_[:, 0, 0:TH]), kk, AF.Square)
            P.mm(MS[:, 0:TH], R(BONES_R), R(BK_[:, 0, 0:TH]))
            P.ts("dve", t0, MS[:, 0:TH], 1e-24, ALU.max)
            yield
            P.act(t0, t0, AF.Ln)
            P.act(t0, t0, AF.Exp, scale=-0.5)
            P.tt("pool", kkn, kk, t0, ALU.mult)
            P.ts("pool", tk, alr, pt("k_a", pp), ALU.mult, pt("omk_a", pp), ALU.add)
            P.tt("pool", kmod, kS[:, hc], tk, ALU.mult)
            yield
            P.stt(R(BK_[:, 1, 0:TH]), rS[:, hc], pt("r_k", pp), kmod, ALU.mult, ALU.mult)
            P.mm(MS[:, 0:TH], R(BONES_R), R(BK_[:, 1, 0:TH]))
            P.tt("dve", bonus, MS[:, 0:TH], vS[:, hc], ALU.mult)
            yield
            for hd in range(2):
                rw = slice(hd * 64, hd * 64 + 64)
                P.stt(R(ARZ_[hd][rw, 0, 0:TH]), kkn[rw], -1.0, eex[rw], ALU.mult, ALU.mult)
                P.tt("dve", R(ARZ_[hd][rw, 1, 0:TH]), rS[rw, hc], ein[rw], ALU.mult)
            yield
            P.tt("pool", bvec, kkn, alr, ALU.mult)
            P.tt("dve", R(BK_[:, 0, 0:TH]), bvec, einv, ALU.mult)
            P.tt("dve", R(BK_[:, 1, 0:TH]), kmod, einv, ALU.mult)
            e0 = WTs[3][:, 0:1]
            wcb = AP(e0.tensor, e0.offset + C - 1, [list(e0.ap[0]), [0, 2], [C, nch], [0, C]])
            P.tt("dve", BKF_[:, :, 0:TH].rearrange("p a (c t) -> p a c t", c=nch),
                 BK_[:, :, 0:TH].rearrange("p a (c t) -> p a c t", c=nch), wcb, ALU.mult)
            yield
            P.mm(MS[:, 0:TH], WG1[:, pc], SG1[:, hc], start=True, stop=False)
            P.mm(MS[:, 0:TH], WG2[:, pc], SG2[:, hc], start=False, stop=True)
            P.copy("act", gate, MS[:, 0:TH])
            yield
            for c in range(nch):
                cc = slice(c * C, (c + 1) * C)
                tb_ = (pa, pb_)[c % 2]
                P.transpose(tb_[0:C, 0:128], BKF_[:, 0, cc], IDF)
                P.transpose(tb_[0:C, 128:256], BKF_[:, 1, cc], IDF)
                P.transpose(tb_[0:C, 256:384], vS[:, h0 + c * C:h0 + (c + 1) * C], IDF)
            yield
            for c in range(nch):
                tb_ = (pa, pb_)[c % 2]
                P.copy("act", R(TM_[0:C, c, 0:2, :]), view(tb_[0:C, 0:256], (2, 128)))
                vz0 = TM_[0:C, c, 2, 0:64]
                vzo = AP(vz0.tensor, vz0.offset, [list(vz0.ap[0]), [192, 2], [1, 64]])
                P.copy("act", R(vzo), view(tb_[0:C, 256:384], (2, 64)))
            yield
            for c in range(nch):
                cc = slice(c * C, (c + 1) * C)
                for hd in range(2):
                    g = c * 2 + hd
                    psb = (pa, pb_)[g % 2]
                    P.mm(psb[0:C, 0:2 * C], R(BK_[:, 0, cc]), R(ARZ_[hd][:, :, cc]))
                    P.mm(psb[0:C, 2 * C:4 * C], R(BK_[:, 1, cc]), R(ARZ_[hd][:, :, cc]))
                    P.mm(psb[0:C, 4 * C:5 * C], R(ARZ_[hd][:, 0, cc]), R(BK_[:, 0, cc]))
                    P.tt("dve", R(SC5_[0:C, g, :, 0:C]), view(psb[0:C, 0:5 * C], (5, C)), MK[0:C, :, 0:C], ALU.mult)
                    yield
            gn = 2 * nch
            pav = view(pa[0:C, :], (4, 128))
            pbv = view(pb_[0:C, :], (4, 128))
            for q in range(gn):
                P.mm(pav[:, q, 0:C], R(SC5_[0:C, q, 4, 0:C]), R(SC5_[0:C, q, 0, 0:C]))
                P.mm(pav[:, q, C:2 * C], R(SC5_[0:C, q, 0, 0:C]), R(SC5_[0:C, q, 4, 0:C]))
            yield
            P.copy("act", R(ntb[0][0:C, 0:gn, 0:2 * C]), pav[:, 0:gn, 0:2 * C])
            idb_ = IDF[0:C, 0:C]
            idbc = AP(idb_.tensor, idb_.offset, [list(idb_.ap[0]), [0, gn], [1, C]])
            P.tt("dve", R(mb[0][0:C, 0:gn, 0:C]), SC5_[0:C, 0:gn, 0, 0:C], idbc, ALU.add)
            yield
            for m in range(1, nlev):
                last = (m == nlev - 1)
                cur = (m - 1) % 2
                nxt = m % 2
                for q in range(gn):
                    Nc = R(ntb[cur][0:C, q, 0:C])
                    Mc = R(ntb[cur][0:C, q, C:2 * C])
                    Tc = R(mb[cur][0:C, q, 0:C])
                    P.mm(pbv[:, q, 0:C], Mc, Tc)
                    if not last:
                        if m < nlev - 2:
                            P.mm(pav[:, q, 0:C], Mc, Nc)
                        P.mm(pav[:, q, C:2 * C], Nc, Mc)
                yield
                if not last:
                    P.copy("act", R(ntb[nxt][0:C, 0:gn, 0:2 * C]), pav[:, 0:gn, 0:2 * C])
                    P.tt("dve", R(mb[nxt][0:C, 0:gn, 0:C]), pbv[:, 0:gn, 0:C], mb[cur][0:C, 0:gn, 0:C], ALU.add)
                else:
                    P.tt("dve", R(TTB_[0:C, 0:gn, 0:C]), pbv[:, 0:gn, 0:C], mb[cur][0:C, 0:gn, 0:C], ALU.add)
                yield
            for c in range(nch):
                cc = slice(c * C, (c + 1) * C)
                tok0 = h0 + c * C
                sg = [s_ for s_ in segs if s_["start"] <= tok0 < s_["start"] + s_["L"]][0]
                H = sg["st"].HBD[:, pp, :]
                P.mm(pa[0:C, 0:128], R(ARZ_[0][:, 0, cc]), R(H), start=True, stop=False)
                P.mm(pa[0:C, 0:128], R(ARZ_[1][:, 0, cc]), R(H), start=False, stop=False)
                for hd in range(2):
                    hv = slice(hd * 64, hd * 64 + 64)
                    P.mm(pa[0:C, hv], R(SC5_[0:C, c * 2 + hd, 2, 0:C]), R(TM_[0:C, c, 2 + hd, hv]), start=False, stop=(hd == 1))
                yield
                P.copy("act", R(RU_[0:C, :]), pa[0:C, 0:128])
                yield
                for hd in range(2):
                    hv = slice(hd * 64, hd * 64 + 64)
                    P.mm(pa[0:C, 128 + hd * 64:192 + hd * 64], R(TTB_[0:C, c * 2 + hd, 0:C]), R(RU_[0:C, hv]))
                yield
                uz0 = UZ_[0][0:C, 0:64]
                uzo = AP(uz0.tensor, uz0.offset, [list(uz0.ap[0]), [192, 2], [1, 64]])
                P.copy("act", R(uzo), view(pa[0:C, 128:256], (2, 64)))
                yield
                yo_ = pb_[:, cc]
                P.mm(yo_, R(H), R(ARZ_[0][:, 1, cc]), start=True, stop=False)
                P.mm(yo_, R(H), R(ARZ_[1][:, 1, cc]), start=False, stop=False)
                for hd in range(2):
                    P.mm(yo_, R(UZ_[hd][0:C, :]), R(SC5_[0:C, c * 2 + hd, 1, 0:C]), start=False, stop=False)
                for hd in range(2):
                    P.mm(yo_, R(TM_[0:C, c, 2 + hd, :]), R(SC5_[0:C, c * 2 + hd, 3, 0:C]), start=False, stop=(hd == 1))
                dps = pa[:, 256:384]
                P.mm(dps, R(TM_[0:C, c, 0, :]), R(UZ_[0][0:C, :]), start=True, stop=False)
                P.mm(dps, R(TM_[0:C, c, 0, :]), R(UZ_[1][0:C, :]), start=False, stop=False)
                P.mm(dps, R(TM_[0:C, c, 1, :]), R(TM_[0:C, c, 2, :]), start=False, stop=False)
                P.mm(dps, R(TM_[0:C, c, 1, :]), R(TM_[0:C, c, 3, :]), start=False, stop=True)
                yield
                P.tt("dve", DT2, dps, BONES, ALU.mult)
                P.stt(R(H), H, WTs[3][:, (c + 1) * C - 1:(c + 1) * C], DT2, ALU.mult, ALU.add)
                yield
            ysb, dd, dsq = lws, cum, cumx
            P.copy("act", R(BK_[:, 0, 0:TH]), pb_[:, 0:TH])
            yield
            P.mm(MS[:, 0:TH], R(CMAT_R), R(BK_[:, 0, 0:TH]))
            P.copy("act", dd, MS[:, 0:TH])
            P.act(R(BK_[:, 1, 0:TH]), MS[:, 0:TH], AF.Square)
            yield
            P.mm(MS[:, 0:TH], R(BONES_R), R(BK_[:, 1, 0:TH]))
            P.act(t0, MS[:, 0:TH], AF.Ln, bias=GN_EPS, scale=1.0 / 64)
            yield
            P.act(t0, t0, AF.Exp, scale=-0.5)
            P.tt("dve", dd, dd, t0, ALU.mult)
            yield
            P.ts("dve", dd, dd, pt("lnx_g", pp), ALU.mult, pt("lnx_b", pp), ALU.add)
            P.tt("dve", dd, dd, bonus, ALU.add)
            P.tt("dve", YCAT[:, pp, hc], dd, gate, ALU.mult)

        def run_lockstep(gens):
            gens = list(gens)
            while gens:
                for g_ in list(gens):
                    try:
                        next(g_)
                    except StopIteration:
                        gens.remove(g_)

        for pp0 in range(0, 8, 2):
            for s_i in range(2):
                pp = pp0 + s_i
                wv = wload(win_v[:, :, WIN_RKV0 + pp * 384: WIN_RKV0 + (pp + 1) * 384], (16, 384))
                rb = RKV[s_i]
                for i in range(3):
                    b_ = next_abank()
                    proj_chunk(wv, i * 128, 128, b_)
                    shift_epi(b_, 128, 3 + pp * 3 + i, rb[i][:, 0:Tt])
            for (h0, TH) in quarters:
                run_lockstep([wkv_gen(pp0 + s_i, h0, TH, STREAMS[s_i], RKV[s_i]) for s_i in range(2)])

        chk(15)
        for ob in range(4):
            wv = wload(wout_v[:, :, ob * 512:(ob + 1) * 512], (16, 512))
            for tb in range(nb):
                b_ = next_abank()
                for kc in range(16):
                    P.mm(b_[0:tbs, :], YCAT[:, kc, tb * 128:tb * 128 + tbs], wv[:, kc, :], start=(kc == 0), stop=(kc == 15))
                xa = XT[0:tbs, tb, ob * 512:(ob + 1) * 512]
                P.tt("dve", xa, xa, b_[0:tbs, :], ALU.add)

        chk(16)
        rmsnorm_to_hT("nffn", HN_F)
        P.dma("sp", FN, final_norm.partition_broadcast(128))

        chk(17)
        zb_i = [0]
        for jb in range(22):
            wv = wload(wup_v[:, :, jb * 512:(jb + 1) * 512], (16, 512))
            for jj in range(2):
                j = jb * 2 + jj
                bu = next_abank()
                proj_chunk(wv, jj * 256, 128, bu)
                bz = (TR, MS)[zb_i[0] % 2]
                zb_i[0] += 1
                proj_chunk(wv, jj * 256 + 128, 128, bz)
                ub = UB[j % 2]
                cv = CV[j % 2]
                gl = GL[j % 2]
                for si, sg in enumerate(segs):
                    s0, L, S_ = sg["start"], sg["L"], sg["st"]
                    o_ = si * (Lmax + 2)
                    P.copy("pool", ub[:, o_:o_ + 2], S_.FH[:, j, :])
                    P.copy("act", ub[:, o_ + 2:o_ + 2 + L], bu[:, s0:s0 + L])
                    P.copy("pool", S_.FH[:, j, :], ub[:, o_ + L:o_ + L + 2])
                    P.act(cv[:, s0:s0 + L], ub[:, o_ + 2:o_ + 2 + L], AF.Identity, bias=pt("fb", j), scale=pt("fw", 2 * NJ + j))
                    P.stt(cv[:, s0:s0 + L], ub[:, o_ + 1:o_ + 1 + L], pt("fw", 1 * NJ + j), cv[:, s0:s0 + L], ALU.mult, ALU.add)
                    P.stt(cv[:, s0:s0 + L], ub[:, o_:o_ + L], pt("fw", 0 * NJ + j), cv[:, s0:s0 + L], ALU.mult, ALU.add)
                P.act(gl[:, 0:Tt], cv[:, 0:Tt], AF.Gelu_apprx_tanh)
                P.tt("dve", GB[:, j, 0:Tt], gl[:, 0:Tt], bz[:, 0:Tt], ALU.mult)

        chk(18)
        dbanks = (IA, IB, SQ, YT)
        kgroups = ((0, 16), (16, 32), (32, 44))
        for ob in range(4):
            for (k0, k1) in kgroups:
                wv = wload(wdn_v[:, k0:k1, ob * 512:(ob + 1) * 512], (k1 - k0, 512))
                for tb in range(nb):
                    for kc in range(k0, k1):
                        P.mm(dbanks[tb][0:tbs, :], GB[:, kc, tb * 128:tb * 128 + tbs], wv[:, kc - k0, :], start=(kc == 0), stop=(kc == NJ - 1))
            for tb in range(nb):
                xa = XT[0:tbs, tb, ob * 512:(ob + 1) * 512]
                P.tt("dve", xa, xa, dbanks[tb][0:tbs, :], ALU.add)

        chk(19)
        for tb in range(nb):
            yo = YO[tb % 2]
            xa = XT[0:tbs, tb, :]
            ss = SMALL[0:tbs, 24 + tb:25 + tb]
            P.act(yo[0:tbs, :], xa, AF.Square, accum_out=ss)
            sd = SMALL[0:tbs, 32 + tb:33 + tb]
            P.act(sd, ss, AF.Sqrt, bias=RMS_EPS, scale=1.0 / D)
            rs = SMALL[0:tbs, 40 + tb:41 + tb]
            P.recip(rs, sd)
            P.stt(yo[0:tbs, :], xa, rs, FN[0:tbs, :], ALU.mult, ALU.mult)
            for (r0, r1, dst) in y_stores(tb):
                P.dma("act", dst, yo[r0:r1, :])

    def store_states(S_, o_shift, o_wkv, o_conv, o_ffn):
        stg = ST32[0]
        P.transpose(TR[0:32, 0:128], S_.SH[:, 0:32], IDF)
        P.copy("dve", stg[0:32, 0:128], TR[0:32, 0:128])
        P.dma("act", rows2d(o_shift[3072:3328], 128), stg[0:2, 0:128])
        P.dma("act", o_shift[3328:3360].rearrange("(a b) -> a b", a=1), stg[2:3, 0:32])
        for pp in range(8):
            for i in range(3):
                r = 3 + pp * 3 + i
                P.dma("act", rows2d(o_shift[i * 1024 + pp * 128:i * 1024 + (pp + 1) * 128], 128), stg[r:r + 1, 0:128])
        for pp in range(8):
            P.transpose(TR[:, 128:256], S_.HBD[:, pp, :], IDF)
            P.copy("act", stg[:, 128 + pp * 128:256 + pp * 128], TR[:, 128:256])
            for hd in range(2):
                P.dma("act", o_wkv[2 * pp + hd], stg[hd * 64:hd * 64 + 64, 128 + pp * 128 + hd * 64:128 + pp * 128 + hd * 64 + 64])
        for j in range(8):
            P.transpose(TR[0:32, 256:384], S_.CH[:, j, :], IDF)
            P.copy("dve", stg[0:30, 1280 + j * 128:1280 + (j + 1) * 128], TR[0:30, 256:384])
        P.dma("act", o_conv, stg[0:30, 1280:2304])
        P.copy("dve", view(stg[:, 2304:2392], (2, NJ)), S_.FH.rearrange("p j t -> p t j"))
        P.transpose(TR[0:96, 384:512], stg[:, 2304:2400], IDF)
        P.copy("dve", stg[0:88, 2432:2560], TR[0:88, 384:512])
        P.dma("act", o_ffn.rearrange("t (j p) -> (t j) p", p=128), stg[0:88, 2432:2560])

    try:
        def small_y(tb):
            return [(16, 32, ys)]
        tile(32, [dict(start=0, L=16, st=STA), dict(start=16, L=16, st=STB)], 16,
             [(XT[0:16, 0, :], meta), (XT[16:32, 0, :], xs)], small_y)
        store_states(STB, o_shift_s, o_wkv_s, o_conv_s, o_ffn_s)
        for q in range(NSEQ):
            P.copy("pool", STB.SH, STA.SH)
            P.copy("pool", R(STB.HBD), STA.HBD)
            P.copy("pool", STB.CH, STA.CH)
            P.copy("pool", STB.FH, STA.FH)
            for tt_ in range(NT):
                def main_y(tb, q=q, tt_=tt_):
                    return [(0, 128, yp[q, tt_ * T + tb * 128: tt_ * T + (tb + 1) * 128, :])]
                tile(T, [dict(start=0, L=T, st=STB)], 64,
                     [(XT, xp[q, tt_ * T:(tt_ + 1) * T, :].rearrange("(nb p) d -> p nb d", p=128))], main_y)
            store_states(STB, o_shift_p[q], o_wkv_p[q], o_conv_p[q], o_ffn_p[q])


    except _Stop:
        pass
    P.emit()
    st.close()
    return nc


_NC_CACHE = {}


def _get_nc(NSEQ, NT, debug=None):
    key = (NSEQ, NT, debug)
    if key not in _NC_CACHE:
        _NC_CACHE[key] = build_program(NSEQ, NT, debug)
    return _NC_CACHE[key]


WEIGHT_KEYS = ("meta", "norm_mix", "w_in", "tshift_mu", "w0", "w_decay_up", "a0", "w_aaa_up", "w_gate_up",
               "k_k", "k_a", "r_k", "lnx_g", "lnx_b", "conv_w", "conv_b", "conv_ln_g", "conv_ln_b",
               "w_out", "norm_ffn", "w_ffn_up", "ffn_conv_w", "ffn_conv_b", "w_ffn_down", "final_norm")


def run_cores(inputs, n_cores, NSEQ, NT, debug=None):
    f = lambda a: np.ascontiguousarray(np.asarray(a, dtype=np.float32))
    shared = {}
    for k in WEIGHT_KEYS:
        a = f(inputs[k])
        if k in ("meta", "final_norm"):
            shared[k] = a
        elif k == "r_k":
            shared[k] = a.reshape(-1)
        else:
            shared[k] = a[0] if a.shape[0] == 1 else a
    in_maps = []
    xpf = f(inputs["x_prompt"])
    for c in range(n_cores):
        m = dict(shared)
        m["xp"] = np.ascontiguousarray(xpf[c * NSEQ:(c + 1) * NSEQ, :NT * 512])
        m["xs"] = f(inputs["x_sample"][c])
        m["st_shift"] = f(inputs["state_shift"][0, c, 0])
        m["st_wkv"] = f(inputs["state_wkv"][0, c])
        m["st_conv"] = f(inputs["cache_conv"][0, c])
        m["st_ffn"] = f(inputs["cache_ffn_conv"][0, c])
        in_maps.append(m)
    nc = _get_nc(NSEQ, NT, debug)
    res = run_bass_kernel_spmd(nc, in_maps, core_ids=list(range(n_cores)))
    return res.results


def kernel(**inputs):
    n = 8
    r = run_cores(inputs, n, 2, 4)
    cat = lambda k: np.concatenate([np.asarray(r[c][k]) for c in range(n)], axis=0)
    stk = lambda k: np.stack([np.asarray(r[c][k]) for c in range(n)], axis=0)
    y_prompt = cat("yp")
    y_sample = stk("ys")
    o_shift_p = cat("o_shift_p")[None, :, None, :]
    o_wkv_p = cat("o_wkv_p")[None]
    o_conv_p = cat("o_conv_p")[None]
    o_ffn_p = cat("o_ffn_p")[None]
    o_shift_s = stk("o_shift_s")[None, :, None, :]
    o_wkv_s = stk("o_wkv_s")[None]
    o_conv_s = stk("o_conv_s")[None]
    o_ffn_s = stk("o_ffn_s")[None]
    outs = (y_prompt, y_sample, o_shift_p, o_wkv_p, o_conv_p, o_ffn_p, o_shift_s, o_wkv_s, o_conv_s, o_ffn_s)
    return tuple(np.ascontiguousarray(o, dtype=np.float32) for o in outs)
```

```python
import numpy as np
import concourse.bass as bass
import concourse.mybir as mybir

F32 = mybir.dt.float32
BF16 = mybir.dt.bfloat16
AF = mybir.ActivationFunctionType
ALU = mybir.AluOpType

_ESZ = {F32: 4, BF16: 2}
try:
    _ESZ[mybir.dt.float32r] = 4
except Exception:
    pass


def _esize(dt):
    if dt in _ESZ:
        return _ESZ[dt]
    s = str(dt)
    if "64" in s:
        return 8
    if "32" in s:
        return 4
    if "16" in s:
        return 2
    return 1


SB_BASE = {}


def ap_box(ap):
    es = _esize(ap.dtype)
    a = ap.ap
    off = ap.offset
    name = ap.tensor.name
    space = str(ap.space)
    if "SB" in space or "PSUM" in space:
        base = SB_BASE.get(name) if "PSUM" not in space else None
        if base is not None:
            name = "SBUF"
            base_b = base
        else:
            base_b = 0
        pstep, pcount = a[0]
        if pstep == 0:
            p0 = 0
            f0 = off
            p1 = 1
        else:
            p0 = off // pstep
            f0 = off % pstep
            p1 = p0 + pcount
        lo = f0
        hi = f0
        for st, cnt in a[1:]:
            ext = st * (cnt - 1)
            if ext < 0:
                lo += ext
            else:
                hi += ext
        if "PSUM" in space:
            return (name, (p0 // 32) * 32, ((p1 + 31) // 32) * 32, 0, 1 << 20)
        return (name, p0, p1, base_b + lo * es, base_b + (hi + 1) * es)
    lo = off
    hi = off
    for st, cnt in a:
        ext = st * (cnt - 1)
        if ext < 0:
            lo += ext
        else:
            hi += ext
    return (name, 0, 1, lo * es, (hi + 1) * es)


def _overlap(a, b):
    return a[1] < b[2] and b[1] < a[2] and a[3] < b[4] and b[3] < a[4]


def _contains(outer, inner):
    return outer[1] <= inner[1] and inner[2] <= outer[2] and outer[3] <= inner[3] and inner[4] <= outer[4]


class Op:
    __slots__ = ("eng", "fn", "idx", "deps", "is_dma", "signal", "sig", "waits", "ring", "ring_val", "gidx")

    def __init__(self, eng, fn, is_dma=False):
        self.eng = eng
        self.fn = fn
        self.is_dma = is_dma
        self.deps = []
        self.signal = False
        self.sig = None
        self.waits = []
        self.ring = None
        self.ring_val = None


ENGS = ("pe", "act", "dve", "pool", "sp")
SIG_WRAP = 16000


class Prog:
    def __init__(self, nc, n_dma_rings=12):
        self.nc = nc
        self.ops = {e: [] for e in ENGS}
        self.all = []
        self.hist = {}
        self.n_rings = n_dma_rings
        self.ring_last = [None] * n_dma_rings
        self.ring_cnt = [0] * n_dma_rings
        self.ring_next = 0

    def _track(self, op, reads, writes):
        rd2 = []
        writes = list(writes)
        for ap in reads:
            if "PSUM" in str(ap.space):
                writes.append(ap)
            else:
                rd2.append(ap)
        reads = rd2
        deps = set()
        for ap in reads:
            bx = ap_box(ap)
            h = self.hist.setdefault(bx[0], [])
            for (b, o, w) in h:
                if w and _overlap(b, bx):
                    deps.add(o)
        for ap in writes:
            bx = ap_box(ap)
            h = self.hist.setdefault(bx[0], [])
            for (b, o, w) in h:
                if _overlap(b, bx):
                    deps.add(o)
        deps.discard(op)
        for ap in reads:
            bx = ap_box(ap)
            h = self.hist[bx[0]]
            if not op.is_dma:
                h[:] = [e for e in h if not ((not e[2]) and e[0] == bx and e[1].eng == op.eng and not e[1].is_dma)]
            h.append((bx, op, False))
        for ap in writes:
            bx = ap_box(ap)
            h = self.hist[bx[0]]
            h[:] = [e for e in h if not _contains(bx, e[0])]
            h.append((bx, op, True))
        op.deps = list(deps)

    def add(self, eng, fn, reads=(), writes=()):
        op = Op(eng, fn)
        op.idx = len(self.ops[eng])
        op.gidx = len(self.all)
        self.ops[eng].append(op)
        self.all.append(op)
        self._track(op, reads, writes)
        return op

    def dma(self, queue, out, in_, **kw):
        def fn(e, out=out, in_=in_, kw=kw):
            return e.dma_start(out=out, in_=in_, **kw)
        op = Op(queue, fn, is_dma=True)
        op.idx = len(self.ops[queue])
        op.gidx = len(self.all)
        self.ops[queue].append(op)
        self.all.append(op)
        self._track(op, [in_], [out])
        r = self.ring_next
        self.ring_next = (self.ring_next + 1) % self.n_rings
        prev = self.ring_last[r]
        if prev is not None and prev not in op.deps:
            op.deps.append(prev)
        self.ring_cnt[r] += 1
        op.ring = r
        op.ring_val = 16 * self.ring_cnt[r]
        self.ring_last[r] = op
        return op

    def finalize(self):
        last_vc = {e: {} for e in ENGS}
        dma_known = {e: {} for e in ENGS}
        op_vc = {}
        for op in self.all:
            E = op.eng
            vc = dict(last_vc[E])
            waits = []
            best = {}
            for d in sorted(op.deps, key=lambda o: o.gidx):
                if d.is_dma:
                    if dma_known[E].get(d.ring, 0) >= d.ring_val:
                        continue
                    dma_known[E][d.ring] = d.ring_val
                    waits.append(d)
                    continue
                if d.eng not in best or d.idx > best[d.eng].idx:
                    best[d.eng] = d
            for De, d in best.items():
                if De == E:
                    if E in ("pe", "sp"):
                        continue
                    if vc.get(E, -1) >= d.idx:
                        continue
                    vc[E] = d.idx
                    waits.append(d)
                    d.signal = True
                    continue
                if vc.get(De, -1) >= d.idx:
                    continue
                waits.append(d)
                d.signal = True
                dvc = op_vc.get(d, {})
                for k2, v2 in dvc.items():
                    if vc.get(k2, -1) < v2:
                        vc[k2] = v2
                if vc.get(De, -1) < d.idx:
                    vc[De] = d.idx
            op.waits = waits
            if not op.is_dma:
                vcc = dict(vc)
                vcc[E] = op.idx
                op_vc[op] = vcc
            last_vc[E] = vc
        self.nsig = {}
        for e in ENGS:
            c = 0
            for op in self.ops[e]:
                if op.is_dma:
                    continue
                if op.signal:
                    op.sig = c
                    c += 1
            self.nsig[e] = c

    def emit(self, final_waits=()):
        nc = self.nc
        self.finalize()
        import contextlib
        with contextlib.ExitStack() as st:
            sems = {}
            for e in ENGS:
                n = (self.nsig[e] + SIG_WRAP - 1) // SIG_WRAP
                sems[e] = [st.enter_context(nc.semaphore(f"s_{e}_{i}")) for i in range(max(n, 1))]
            rsems = [st.enter_context(nc.semaphore(f"s_ring_{i}")) for i in range(self.n_rings)]
            block = st.enter_context(nc.Block())

            def run(engname, eng):
                for op in self.ops[engname]:
                    for d in op.waits:
                        if d.is_dma:
                            eng.wait_ge(rsems[d.ring], d.ring_val)
                        else:
                            eng.wait_ge(sems[d.eng][d.sig // SIG_WRAP], d.sig % SIG_WRAP + 1)
                    ins = op.fn(eng)
                    if op.is_dma:
                        ins.then_inc(rsems[op.ring], 16)
                    elif op.signal:
                        ins.then_inc(sems[engname][op.sig // SIG_WRAP], 1)
                if engname in ("sp", "pool", "act"):
                    lastv = {}
                    for op in self.ops[engname]:
                        if op.is_dma:
                            lastv[op.ring] = max(lastv.get(op.ring, 0), op.ring_val)
                    for r, v in lastv.items():
                        eng.wait_ge(rsems[r], v)

            @block.tensor
            def _(t):
                run("pe", t)

            @block.scalar
            def _(a):
                run("act", a)

            @block.vector
            def _(v):
                run("dve", v)

            @block.gpsimd
            def _(g):
                run("pool", g)

            @block.sync
            def _(s):
                run("sp", s)

    def mm(self, out, lhsT, rhs, start=True, stop=True, extra_reads=()):
        def fn(e):
            return e.matmul(out, lhsT=lhsT, rhs=rhs, start=start, stop=stop)
        rd = [lhsT, rhs] + list(extra_reads)
        if not start:
            rd.append(out)
        return self.add("pe", fn, rd, [out])

    def transpose(self, out, in_, ident):
        def fn(e):
            return e.transpose(out=out, in_=in_, identity=ident)
        return self.add("pe", fn, [in_, ident], [out])

    def act(self, out, in_, func, bias=None, scale=None, accum_out=None, eng="act"):
        kw = {}
        rd = [in_]
        wr = [out]
        if bias is not None:
            kw["bias"] = bias
            if not isinstance(bias, (int, float)):
                rd.append(bias)
        if scale is not None:
            kw["scale"] = scale
            if not isinstance(scale, (int, float)):
                rd.append(scale)
        if accum_out is not None:
            kw["accum_out"] = accum_out
            wr.append(accum_out)

        def fn(e):
            return e.activation(out=out, in_=in_, func=func, **kw)
        return self.add("act", fn, rd, wr)

    def tt(self, eng, out, in0, in1, op):
        def fn(e):
            return e.tensor_tensor(out=out, in0=in0, in1=in1, op=op)
        return self.add(eng, fn, [in0, in1], [out])

    def ts(self, eng, out, in0, s1, op0, s2=None, op1=None, accum_out=None):
        rd = [in0]
        if not isinstance(s1, (int, float)):
            rd.append(s1)
        if s2 is not None and not isinstance(s2, (int, float)):
            rd.append(s2)
        wr = [out]
        kw = {}
        if accum_out is not None:
            kw["accum_out"] = accum_out
            wr.append(accum_out)

        def fn(e):
            if op1 is None:
                return e.tensor_scalar(out=out, in0=in0, scalar1=s1, scalar2=None, op0=op0, **kw)
            return e.tensor_scalar(out=out, in0=in0, scalar1=s1, scalar2=s2, op0=op0, op1=op1, **kw)
        return self.add(eng, fn, rd, wr)

    def stt(self, out, in0, scalar, in1, op0, op1):
        rd = [in0, in1]
        if not isinstance(scalar, (int, float)):
            rd.append(scalar)

        def fn(e):
            return e.scalar_tensor_tensor(out=out, in0=in0, scalar=scalar, in1=in1, op0=op0, op1=op1)
        return self.add("dve", fn, rd, [out])

    def copy(self, eng, out, in_):
        if eng == "act":
            def fn(e):
                return e.copy(out=out, in_=in_)
        else:
            def fn(e):
                return e.tensor_copy(out=out, in_=in_)
        return self.add(eng, fn, [in_], [out])

    def memset(self, eng, ap, val):
        def fn(e):
            return e.memset(ap, val)
        return self.add(eng, fn, [], [ap])

    def recip(self, out, in_):
        def fn(e):
            return e.reciprocal(out=out, in_=in_)
        return self.add("dve", fn, [in_], [out])

    def scan(self, out, d0, d1, init, op0, op1):
        rd = [d0, d1]
        if not isinstance(init, (int, float)):
            rd.append(init)

        def fn(e):
            return e.tensor_tensor_scan(out=out, data0=d0, data1=d1, initial=init, op0=op0, op1=op1)
        return self.add("dve", fn, rd, [out])

    def affine_select(self, out, in_, compare_op, fill, base, pattern, channel_multiplier):
        def fn(e):
            return e.affine_select(out=out, in_=in_, compare_op=compare_op, fill=fill, base=base,
                                   pattern=pattern, channel_multiplier=channel_multiplier)
        return self.add("pool", fn, [in_], [out])

from concourse.bass_utils import run_bass_kernel_spmd
from concourse.ap import AP
import contextlib

D = 2048
RW = 1024
DFF = 5632
NJ = 44
INC = 5408
RWC = 3360
C_DEC = 0.6065306597126334
RMS_EPS = 1e-6
LN_EPS = 1e-5
GN_EPS = 64e-5
F32R = mybir.dt.float32r


def R(ap):
    return ap.bitcast(F32R)

WIN_CONV0 = 0
WIN_LORA0 = 2048
WIN_RKV0 = 2336


def _prod(s):
    r = 1
    for v in s:
        r *= v
    return r


def view(ap2d, shape):
    if len(shape) == 1:
        return ap2d
    if len(shape) == 2:
        return ap2d.rearrange("p (a b) -> p a b", a=shape[0])
    if len(shape) == 3:
        return ap2d.rearrange("p (a b c) -> p a b c", a=shape[0], b=shape[1])
    raise ValueError(shape)


class Mem:
    def __init__(self, t):
        self.t = t

    def f32(self, off, shape):
        n = _prod(shape)
        return view(self.t[:, off:off + n], shape)

    def bf(self, off, shape):
        n = _prod(shape)
        assert n % 2 == 0
        return view(self.t[:, off:off + n // 2].bitcast(BF16), shape)


class _Stop(Exception):
    pass


def build_program(NSEQ=2, NT=4, debug=None):
    def chk(stage):
        if debug is not None and debug == stage:
            raise _Stop()
    nc = bass.Bass("TRN2", target_bir_lowering=False)
    P = Prog(nc)
    T = 512
    SEQ = NT * T

    def din(name, shape):
        return nc.dram_tensor(name, list(shape), F32, kind="ExternalInput").ap()

    def dout(name, shape):
        return nc.dram_tensor(name, list(shape), F32, kind="ExternalOutput").ap()

    xp = din("xp", (NSEQ, SEQ, D))
    xs = din("xs", (16, D))
    st_shift = din("st_shift", (RWC,))
    st_wkv = din("st_wkv", (16, 64, 64))
    st_conv = din("st_conv", (30, 1024))
    st_ffn = din("st_ffn", (2, DFF))
    meta = din("meta", (16, D))
    norm_mix = din("norm_mix", (D,))
    w_in = din("w_in", (D, INC))
    tshift_mu = din("tshift_mu", (RWC,))
    w0 = din("w0", (RW,))
    w_decay_up = din("w_decay_up", (64, RW))
    a0 = din("a0", (RW,))
    w_aaa_up = din("w_aaa_up", (64, RW))
    w_gate_up = din("w_gate_up", (160, RW))
    k_k = din("k_k", (RW,))
    k_a = din("k_a", (RW,))
    r_k = din("r_k", (RW,))
    lnx_g = din("lnx_g", (RW,))
    lnx_b = din("lnx_b", (RW,))
    conv_w = din("conv_w", (31, 1024))
    conv_b = din("conv_b", (1024,))
    conv_ln_g = din("conv_ln_g", (1024,))
    conv_ln_b = din("conv_ln_b", (1024,))
    w_out = din("w_out", (D, D))
    norm_ffn = din("norm_ffn", (D,))
    w_ffn_up = din("w_ffn_up", (D, 2 * DFF))
    ffn_conv_w = din("ffn_conv_w", (3, DFF))
    ffn_conv_b = din("ffn_conv_b", (DFF,))
    w_ffn_down = din("w_ffn_down", (DFF, D))
    final_norm = din("final_norm", (D,))

    yp = dout("yp", (NSEQ, SEQ, D))
    ys = dout("ys", (16, D))
    o_shift_p = dout("o_shift_p", (NSEQ, RWC))
    o_wkv_p = dout("o_wkv_p", (NSEQ, 16, 64, 64))
    o_conv_p = dout("o_conv_p", (NSEQ, 30, 1024))
    o_ffn_p = dout("o_ffn_p", (NSEQ, 2, DFF))
    o_shift_s = dout("o_shift_s", (RWC,))
    o_wkv_s = dout("o_wkv_s", (16, 64, 64))
    o_conv_s = dout("o_conv_s", (30, 1024))
    o_ffn_s = dout("o_ffn_s", (2, DFF))

    win_s = nc.dram_tensor("win_s", [D, INC], BF16).ap()
    wout_s = nc.dram_tensor("wout_s", [D, D], BF16).ap()
    wup_s = nc.dram_tensor("wup_s", [D, 2 * DFF], BF16).ap()
    wdn_s = nc.dram_tensor("wdn_s", [DFF, D], BF16).ap()

    st = contextlib.ExitStack()
    SB0 = 16512
    AB = 25024
    XB = AB + 10496
    XR0 = XB + 4864
    NH = 6912
    NR = 12800
    assert SB0 + 4 * (XR0 + NR) <= 229376
    arena_t = nc.alloc_sbuf_tensor_at("arena", [128, XR0], F32, offset=SB0)
    arena_h = nc.alloc_sbuf_tensor_at("arena_h", [128, NH], F32, offset=SB0 + 4 * XR0)
    arena_r = nc.alloc_sbuf_tensor_at("arena_r", [128, NR], F32, offset=SB0 + 4 * XR0)
    SB_BASE.clear()
    SB_BASE[arena_t.name] = SB0
    SB_BASE[arena_h.name] = SB0 + 4 * XR0
    SB_BASE[arena_r.name] = SB0 + 4 * XR0
    M = Mem(arena_t)
    MH = Mem(arena_h)
    MR = Mem(arena_r)
    HB = XR0 - AB
    banks = [st.enter_context(nc.psum_tensor(f"pb{i}", [128, 512], F32)) for i in range(8)]
    A0, A1, TR, MS, IA, IB, SQ, YT = banks

    XT = M.f32(0, (4, 2048))
    HT = M.bf(8192, (16, 512))
    WR = [M.bf(12288 + i * 4096, (8192,)) for i in range(2)]
    o = 20480
    NPT = 600
    PT = M.f32(o, (NPT,)); o += NPT
    IDF = M.f32(o, (128,)); o += 128
    IDB = M.bf(o, (128,)); o += 64
    MK = M.f32(o, (5, 64)); o += 320
    BONES = M.f32(o, (128,)); o += 128
    CMAT = M.f32(o, (128,)); o += 128
    ONES = M.f32(o, (128,)); o += 128
    RMASK = M.f32(o, (128,)); o += 128
    RMASK16 = M.f32(o, (32,)); o += 32
    WLAD = M.bf(o, (1024,)); o += 512
    WLAA = M.bf(o, (1024,)); o += 512
    WG1 = M.bf(o, (1024,)); o += 512
    WG2 = M.bf(o, (1024,)); o += 512

    class StateSet:
        pass
    sets = []
    for i in range(2):
        s_ = StateSet()
        s_.SH = M.f32(o, (32,)); o += 32
        s_.HBD = MR.f32(10496 + i * 1024, (8, 128))
        s_.CH = M.f32(o, (8, 32)); o += 256
        s_.FH = M.f32(o, (NJ, 2)); o += 88
        sets.append(s_)
    STA, STB = sets
    SMALL = M.f32(o, (64,)); o += 64
    assert o <= AB, (o, AB)

    YCAT = M.bf(AB + 0, (16, 512))
    PBUF = [M.f32(AB + 4096 + i * 512, (512,)) for i in range(2)]
    DTMP = [M.f32(AB + 5120 + i * 512, (512,)) for i in range(2)]
    L1S = M.bf(AB + 6144, (512,))
    SG1 = M.bf(AB + 6400, (512,))
    SG2 = M.bf(AB + 6656, (512,))
    L1F = M.f32(AB + 6912, (512,))
    RKV = [[M.f32(AB + 7424 + (b * 3 + i) * 512, (512,)) for i in range(3)] for b in range(2)]
    HN_A = [M.bf(XB + i * 1024, (2048,)) for i in range(2)]
    GLU_OFF = XB
    CT = [M.f32(XB + 4336, (512,))] + [MH.f32(4096 + i * 512, (512,)) for i in range(5)]
    CO = MH.f32(0, (8, 512))
    BONES_R = MR.f32(12544, (128,))
    CMAT_R = MR.f32(12672, (128,))
    STREAMS = []
    for s_i in range(2):
        S = {}
        S["WT"] = [M.f32(XB + s_i * 2048 + i * 128, (128,)) for i in range(16)]
        S["BKF"] = M.f32(XB + 4096 + s_i * 256, (2, 128))
        S["DT"] = M.f32(XB + 4608 + s_i * 128, (128,))
        S["BK"] = MR.f32(s_i * 256, (2, 128))
        S["SC5"] = MR.f32(512 + s_i * 1280, (4, 5, 64))
        S["NTB"] = [MR.f32(3072 + (s_i * 2 + i) * 512, (4, 128)) for i in range(2)]
        S["MB"] = [MR.f32(5120 + (s_i * 2 + i) * 256, (4, 64)) for i in range(2)]
        S["TTB"] = MR.f32(6144 + s_i * 256, (4, 64))
        S["RU"] = MR.f32(6656 + s_i * 128, (128,))
        S["ARZ"] = [MR.f32(6912 + (s_i * 2 + hd) * 256, (2, 128)) for hd in range(2)]
        S["TM"] = MR.f32(7936 + s_i * 1024, (2, 4, 128))
        S["UZ"] = [MR.f32(9984 + (s_i * 2 + hd) * 128, (128,)) for hd in range(2)]
        S["banks"] = (IA, IB) if s_i == 0 else (SQ, YT)
        STREAMS.append(S)
    GB = M.bf(AB + 0, (NJ, 512))
    FN = M.f32(AB + 11264, (2048,))
    UB = [M.f32(AB + 13312 + i * 520, (520,)) for i in range(2)]
    CV = [M.f32(AB + 14352, (512,)), MH.f32(4096, (512,))]
    YO = [MH.f32(i * 2048, (2048,)) for i in range(2)]
    GL = [MH.f32(4608 + i * 512, (512,)) for i in range(2)]
    HN_F = [M.bf(AB + 4096 + i * 1024, (2048,)) for i in range(2)]
    assert AB + 14864 <= XR0
    ST32 = [M.f32(AB + i * 5632, (5632,)) for i in range(2)]
    ST16 = [M.bf(AB + 11264, (5632,)), MH.bf(0, (5632,))]
    PSTG = [MH.f32(2816 + i * 128, (128,)) for i in range(2)]
    cols = {}
    cpos = [0]

    def pcol(name, n):
        cols[name] = cpos[0]
        cpos[0] += n
    for nm, n in (("mu", 27), ("w0", 8), ("a0", 8), ("k_k", 8), ("k_a", 8), ("r_k", 8), ("lnx_g", 8),
                  ("lnx_b", 8), ("conv_b", 8), ("cln_g", 8), ("cln_b", 8), ("nmix", 16), ("nffn", 16),
                  ("cw", 248), ("fw", 132), ("fb", 44), ("omk_a", 8), ("nw0", 8), ("na0", 8)):
        pcol(nm, n)
    assert cpos[0] <= NPT

    def pt(name, i=0, rows=slice(0, 128)):
        c = cols[name] + i
        return PT[rows, c:c + 1]

    def rows2d(ap1d, n):
        return ap1d.rearrange("(c p) -> c p", p=n)

    P.memset("pool", IDF, 0.0)
    P.affine_select(IDF, IDF, ALU.not_equal, 1.0, 0, [[-1, 128]], 1)
    P.copy("dve", IDB, IDF)
    P.memset("pool", ONES, 1.0)
    P.memset("pool", BONES, 0.0)
    P.memset("pool", BONES[0:64, 0:64], 1.0)
    P.memset("pool", BONES[64:128, 64:128], 1.0)
    P.stt(CMAT, BONES, -1.0 / 64.0, IDF, ALU.mult, ALU.add)
    P.copy("dve", R(BONES_R), BONES)
    P.copy("dve", R(CMAT_R), CMAT)
    P.memset("pool", MK[0:64], 1.0)
    for q, cmp_, cm, pat in ((0, ALU.is_gt, -1, 1), (1, ALU.is_ge, -1, 1), (2, ALU.is_gt, -1, 1),
                             (3, ALU.is_ge, -1, 1), (4, ALU.is_gt, 1, -1)):
        P.affine_select(MK[0:64, q, :], MK[0:64, q, :], cmp_, 0.0, 0, [[pat, 64]], cm)
    P.memset("pool", RMASK, 1.0)
    P.memset("pool", RMASK[:, 0:128:64], 0.0)
    P.memset("pool", RMASK16, 1.0)
    P.memset("pool", RMASK16[:, 0:32:16], 0.0)
    ZERO = M.f32(AB + 0, (2048,))
    P.memset("pool", ZERO, 0.0)
    P.copy("dve", R(MR.f32(6912, (1024,))), ZERO[:, 0:1024])
    P.copy("dve", R(MR.f32(7936, (2048,))[0:64]), ZERO[0:64, 0:2048])
    P.copy("dve", R(MR.f32(9984, (512,))[0:64]), ZERO[0:64, 0:512])
    P.memset("pool", SG2, 0.0)
    for s_ in sets:
        P.copy("dve", R(s_.HBD.rearrange("p a b -> p (a b)")), ZERO[:, 0:1024])
    P.memset("pool", STA.SH, 0.0)
    P.memset("pool", STA.CH, 0.0)
    P.memset("pool", STA.FH, 0.0)
    P.memset("pool", STB.SH, 0.0)
    P.memset("pool", STB.CH, 0.0)

    if debug is not None and debug <= 0:
        P.emit(); st.close(); return nc
    row_jobs = []

    def addrows(ap1d, name, n, base=0):
        row_jobs.append((rows2d(ap1d, 128), n, cols[name] + base))
    def rwkv_rows(src1d):
        jobs = []
        jobs.append((rows2d(src1d[3072:3328], 128), 2, 0))
        jobs.append((src1d[3328:3360].rearrange("(a b) -> a b", a=1), 1, 2))
        for pp in range(8):
            for i in range(3):
                jobs.append((rows2d(src1d[i * 1024 + pp * 128: i * 1024 + (pp + 1) * 128], 128), 1, 3 + pp * 3 + i))
        return jobs
    for (ap_, n, c) in rwkv_rows(tshift_mu):
        row_jobs.append((ap_, n, cols["mu"] + c))
    addrows(w0, "w0", 8); addrows(a0, "a0", 8); addrows(k_k, "k_k", 8); addrows(k_a, "k_a", 8)
    addrows(r_k, "r_k", 8); addrows(lnx_g, "lnx_g", 8); addrows(lnx_b, "lnx_b", 8)
    addrows(conv_b, "conv_b", 8); addrows(conv_ln_g, "cln_g", 8); addrows(conv_ln_b, "cln_b", 8)
    addrows(norm_mix, "nmix", 16); addrows(norm_ffn, "nffn", 16)
    cwf = conv_w.rearrange("w c -> (w c)")
    for blk in range(0, 248, 124):
        row_jobs.append((rows2d(cwf[blk * 128:(blk + 124) * 128], 128), 124, cols["cw"] + blk))
    fwf = ffn_conv_w.rearrange("w c -> (w c)")
    for blk in range(0, 132, 66):
        row_jobs.append((rows2d(fwf[blk * 128:(blk + 66) * 128], 128), 66, cols["fw"] + blk))
    addrows(ffn_conv_b, "fb", 44)

    def run_row_jobs(jobs, dest_fn, k0=0):
        k = k0
        i = 0
        while i < len(jobs):
            stg = PSTG[k % 2]
            k += 1
            batch = []
            used = 0
            P.memset("pool", stg, 0.0)
            while i < len(jobs) and used + jobs[i][1] <= 128:
                ap_, n, c = jobs[i]
                w = ap_.shape[1]
                P.dma("sp", stg[used:used + n, 0:w], ap_)
                batch.append((used, n, c))
                used += n
                i += 1
            P.transpose(TR[:, 0:128], stg, IDF)
            for (r0, n, c) in batch:
                P.copy("dve", dest_fn(c, n), TR[:, r0:r0 + n])
        return k
    kk_ = run_row_jobs(row_jobs, lambda c, n: PT[:, c:c + n])
    P.ts("dve", PT[:, cols["omk_a"]:cols["omk_a"] + 8], PT[:, cols["k_a"]:cols["k_a"] + 8], -1.0, ALU.mult, 1.0, ALU.add)
    P.ts("dve", PT[:, cols["nw0"]:cols["nw0"] + 8], PT[:, cols["w0"]:cols["w0"] + 8], -1.0, ALU.mult)
    P.ts("dve", PT[:, cols["na0"]:cols["na0"] + 8], PT[:, cols["a0"]:cols["a0"] + 8], -1.0, ALU.mult)

    if debug is not None and debug <= 1:
        P.emit(); st.close(); return nc
    P.memset("pool", WLAD, 0.0)
    P.memset("pool", WLAA, 0.0)
    P.memset("pool", WG2, 0.0)
    s32 = ST32[0]
    P.dma("sp", s32[0:64, 0:1024], w_decay_up)
    P.dma("sp", s32[64:128, 0:1024], w_aaa_up)
    P.copy("dve", WLAD[0:64, :], s32[0:64, 0:1024])
    P.copy("dve", WLAA[64:128, :], s32[64:128, 0:1024])
    P.dma("sp", s32[:, 1024:2048], w_gate_up[0:128, :])
    P.dma("sp", s32[0:32, 2048:3072], w_gate_up[128:160, :])
    P.copy("dve", WG1, s32[:, 1024:2048])
    P.copy("dve", WG2[0:32, :], s32[0:32, 2048:3072])

    if debug is not None and debug <= 2:
        P.emit(); st.close(); return nc
    run_row_jobs([(a_, n, c) for (a_, n, c) in rwkv_rows(st_shift)], lambda c, n: STB.SH[:, c:c + n], kk_)
    s32b = ST32[1]
    P.dma("sp", view(s32b[0:64, 0:1024], (16, 64)), st_wkv.rearrange("h v k -> v h k"))
    for pp in range(8):
        P.transpose(TR[:, 0:64], s32b[0:64, pp * 128:(pp + 1) * 128], IDF[0:64, 0:64])
        P.copy("dve", R(STB.HBD[0:64, pp, 0:64]), TR[0:64, 0:64])
        P.copy("dve", R(STB.HBD[64:128, pp, 64:128]), TR[64:128, 0:64])
    P.dma("sp", s32b[0:30, 1024:2048], st_conv)
    for j in range(8):
        P.transpose(TR[:, 0:128], s32b[:, 1024 + j * 128:1024 + (j + 1) * 128], IDF)
        P.copy("dve", STB.CH[:, j, 0:30], TR[:, 0:30])
    P.dma("sp", s32b[0:88, 2048:2176], st_ffn.rearrange("t (j p) -> (t j) p", p=128))
    P.transpose(TR[:, 128:256], s32b[:, 2048:2176], IDF)
    P.copy("dve", STB.FH.rearrange("p j t -> p t j"), view(TR[:, 128:216], (2, NJ)))

    if debug is not None and debug <= 3:
        P.emit(); st.close(); return nc
    cast_engs = ["dve", "act", "pool"]
    cast_i = [0]

    def cast(out, in_):
        e = cast_engs[cast_i[0] % 3]
        cast_i[0] += 1
        P.copy(e, out, in_)
    sidx = [0]

    def conv_rows_generic(src_rows_ap, ncols, dst_rows_ap, permute=None):
        i = sidx[0] % 2
        sidx[0] += 1
        a32 = ST32[i][:, 0:ncols]
        a16 = ST16[i][:, 0:ncols]
        P.dma("sp", a32, src_rows_ap)
        if permute is None:
            half = ncols // 2
            cast(a16[:, 0:half], a32[:, 0:half])
            cast(a16[:, half:ncols], a32[:, half:ncols])
        else:
            permute(a16, a32)
        P.dma("sp", dst_rows_ap, a16)

    def perm_win(a16, a32):
        cast(a16[:, 0:2048].rearrange("q (j t c) -> q j t c", j=8, t=2),
             a32[:, 3360:5408].rearrange("q (t j c) -> q j t c", t=2, j=8))
        cast(a16[:, 2048:2336], a32[:, 3072:3360])
        cast(a16[:, 2336:5408].rearrange("q (p i c) -> q p i c", p=8, i=3),
             a32[:, 0:3072].rearrange("q (i p c) -> q p i c", i=3, p=8))
    for kc in range(16):
        conv_rows_generic(w_in[kc * 128:(kc + 1) * 128, :], INC, win_s[kc * 128:(kc + 1) * 128, :], perm_win)
    for kc in range(16):
        conv_rows_generic(w_out[kc * 128:(kc + 1) * 128, :], D, wout_s[kc * 128:(kc + 1) * 128, :])

    def perm_up(a16, a32):
        cast(a16.rearrange("q (j t c) -> q j t c", j=22, t=2), a32.rearrange("q (t j c) -> q j t c", t=2, j=22))
    for kc in range(16):
        for hf in range(2):
            src = w_ffn_up[kc * 128:(kc + 1) * 128, :].rearrange("q (t n) -> q t n", t=2)[:, :, hf * 2816:(hf + 1) * 2816]
            i = sidx[0] % 2
            sidx[0] += 1
            a32 = ST32[i]
            a16 = ST16[i]
            P.dma("sp", view(a32, (2, 2816)), src)
            perm_up(a16, a32)
            P.dma("sp", wup_s[kc * 128:(kc + 1) * 128, hf * 5632:(hf + 1) * 5632], a16)
    for kc in range(NJ):
        conv_rows_generic(w_ffn_down[kc * 128:(kc + 1) * 128, :], D, wdn_s[kc * 128:(kc + 1) * 128, :])

    if debug is not None and debug <= 4:
        P.emit(); st.close(); return nc
    wslot = [0]

    def wload(src_ap, shape):
        s_ = WR[wslot[0] % 2]
        wslot[0] += 1
        n = _prod(shape)
        v = view(s_[:, 0:n], shape)
        P.dma("sp", v, src_ap)
        return v

    win_v = win_s.rearrange("(kc p) n -> p kc n", p=128)
    wout_v = wout_s.rearrange("(kc p) n -> p kc n", p=128)
    wup_v = wup_s.rearrange("(kc p) n -> p kc n", p=128)
    wdn_v = wdn_s.rearrange("(kc p) n -> p kc n", p=128)

    abank = [0]

    def next_abank():
        b = (A0, A1)[abank[0] % 2]
        abank[0] += 1
        return b

    def tile(Tt, segs, C, x_loads, y_stores):
        nb = max(1, Tt // 128)
        tbs = min(Tt, 128)
        nch_tile = Tt // C
        rmask = RMASK if C == 64 else RMASK16
        nlev = {64: 6, 16: 4}[C]

        for (dst, src) in x_loads:
            P.dma("sp", dst, src)

        def rmsnorm_to_hT(gname, HN):
            for tb in range(nb):
                hn = HN[tb % 2]
                xa = XT[0:tbs, tb, :]
                ss = SMALL[0:tbs, tb:tb + 1]
                P.act(hn[0:tbs, :], xa, AF.Square, accum_out=ss)
                sd = SMALL[0:tbs, 8 + tb:9 + tb]
                P.act(sd, ss, AF.Sqrt, bias=RMS_EPS, scale=1.0 / D)
                rs = SMALL[0:tbs, 16 + tb:17 + tb]
                P.recip(rs, sd)
                P.ts("dve", hn[0:tbs, :], xa, rs, ALU.mult)
                for k4 in range(4):
                    trb = TR[:, (k4 % 2) * 256:(k4 % 2) * 256 + 256].bitcast(BF16)
                    for q in range(4):
                        kc = k4 * 4 + q
                        P.transpose(trb[:, q * 128:q * 128 + tbs], hn[0:tbs, kc * 128:(kc + 1) * 128], IDB[0:tbs, 0:tbs])
                    gain = PT[:, cols[gname] + k4 * 4: cols[gname] + k4 * 4 + 4]
                    gb = AP(gain.tensor, gain.offset, [list(gain.ap[0]), [1, 4], [0, tbs]])
                    src = view(trb, (4, 128))[:, :, 0:tbs]
                    P.tt("dve", HT[:, k4 * 4:(k4 + 1) * 4, tb * 128:tb * 128 + tbs], src, gb, ALU.mult)

        rmsnorm_to_hT("nmix", HN_A)

        chk(5)
        def proj_chunk(wv, c0, Mrows, bank):
            for kc in range(16):
                P.mm(bank[0:Mrows, 0:Tt], wv[:, kc, c0:c0 + Mrows], HT[:, kc, 0:Tt], start=(kc == 0), stop=(kc == 15))

        pb_i = [0]

        def shift_epi(bank, Mrows, mcol, out_ap):
            pb = PBUF[pb_i[0] % 2]
            dt = DTMP[pb_i[0] % 2]
            pb_i[0] += 1
            P.copy("act", pb[0:Mrows, 0:Tt], bank[0:Mrows, 0:Tt])
            for sg in segs:
                s0, L, S_ = sg["start"], sg["L"], sg["st"]
                P.tt("dve", dt[0:Mrows, s0 + 1:s0 + L], pb[0:Mrows, s0:s0 + L - 1], pb[0:Mrows, s0 + 1:s0 + L], ALU.subtract)
                P.tt("dve", dt[0:Mrows, s0:s0 + 1], S_.SH[0:Mrows, mcol:mcol + 1], pb[0:Mrows, s0:s0 + 1], ALU.subtract)
                P.copy("pool", S_.SH[0:Mrows, mcol:mcol + 1], pb[0:Mrows, s0 + L - 1:s0 + L])
            P.stt(out_ap, dt[0:Mrows, 0:Tt], PT[0:Mrows, cols["mu"] + mcol:cols["mu"] + mcol + 1], pb[0:Mrows, 0:Tt], ALU.mult, ALU.add)

        nseg = len(segs)
        Lmax = max(sg["L"] for sg in segs)
        GW = 30 + Lmax
        G = M.f32(GLU_OFF, (8, nseg, GW))
        for jb in range(4):
            wv = wload(win_v[:, :, WIN_CONV0 + jb * 512: WIN_CONV0 + (jb + 1) * 512], (16, 512))
            for jj in range(2):
                j = jb * 2 + jj
                bv = next_abank()
                proj_chunk(wv, jj * 256, 128, bv)
                bg = next_abank()
                proj_chunk(wv, jj * 256 + 128, 128, bg)
                sgt = CT[0]
                P.act(sgt[:, 0:Tt], bg[:, 0:Tt], AF.Sigmoid)
                for si, sg in enumerate(segs):
                    s0, L, S_ = sg["start"], sg["L"], sg["st"]
                    P.copy("pool", G[:, j, si, 0:30], S_.CH[:, j, 0:30])
                    P.tt("dve", G[:, j, si, 30:30 + L], bv[:, s0:s0 + L], sgt[:, s0:s0 + L], ALU.mult)
                    P.copy("pool", S_.CH[:, j, 0:30], G[:, j, si, L:L + 30])
        chk(6)
        for j in range(8):
            for si, sg in enumerate(segs):
                s0, L = sg["start"], sg["L"]
                acc = CO[:, j, s0:s0 + L]
                if j == 7 and Tt >= 256:
                    tmpp = CT[3][:, 0:L]
                    P.ts("pool", acc, G[:, j, si, 0:L], pt("cw", 0 * 8 + j), ALU.mult, pt("conv_b", j), ALU.add)
                    for w in range(1, 31):
                        P.ts("pool", tmpp, G[:, j, si, w:w + L], pt("cw", w * 8 + j), ALU.mult, 0.0, ALU.add)
                        P.tt("pool", acc, acc, tmpp, ALU.add)
                    continue
                P.ts("dve", acc, G[:, j, si, 0:L], pt("cw", 0 * 8 + j), ALU.mult, pt("conv_b", j), ALU.add)
                for w in range(1, 31):
                    P.stt(acc, G[:, j, si, w:w + L], pt("cw", w * 8 + j), acc, ALU.mult, ALU.add)
            P.mm(MS[:, 0:Tt], ONES, CO[:, j, 0:Tt], start=(j == 0), stop=(j == 7))
            sq = CT[1 + (j % 2)]
            P.act(sq[:, 0:Tt], CO[:, j, 0:Tt], AF.Square)
            P.mm(TR[:, 0:Tt], ONES, sq[:, 0:Tt], start=(j == 0), stop=(j == 7))
        mean, m2, rstd = CT[3], CT[4], CT[5]
        P.ts("dve", mean[:, 0:Tt], MS[:, 0:Tt], 1.0 / 1024, ALU.mult)
        P.tt("dve", m2[:, 0:Tt], mean[:, 0:Tt], mean[:, 0:Tt], ALU.mult)
        P.stt(m2[:, 0:Tt], TR[:, 0:Tt], 1.0 / 1024, m2[:, 0:Tt], ALU.mult, ALU.subtract)
        P.act(m2[:, 0:Tt], m2[:, 0:Tt], AF.Sqrt, bias=LN_EPS, scale=1.0)
        P.recip(rstd[:, 0:Tt], m2[:, 0:Tt])
        for j in range(8):
            t1 = CT[1 + (j % 2)]
            P.tt("dve", t1[:, 0:Tt], CO[:, j, 0:Tt], mean[:, 0:Tt], ALU.subtract)
            P.tt("dve", t1[:, 0:Tt], t1[:, 0:Tt], rstd[:, 0:Tt], ALU.mult)
            P.act(YCAT[:, 8 + j, 0:Tt], t1[:, 0:Tt], AF.Silu, bias=pt("cln_b", j), scale=pt("cln_g", j))

        chk(7)
        wv = wload(win_v[:, :, WIN_LORA0:WIN_LORA0 + 288], (16, 288))
        b_ = next_abank()
        proj_chunk(wv, 0, 128, b_)
        shift_epi(b_, 128, 0, L1F[:, 0:Tt])
        P.act(L1S[0:64, 0:Tt], L1F[0:64, 0:Tt], AF.Tanh)
        P.copy("pool", L1S[64:128, 0:Tt], L1F[64:128, 0:Tt])
        b_ = next_abank()
        proj_chunk(wv, 128, 128, b_)
        shift_epi(b_, 128, 1, L1F[:, 0:Tt])
        P.act(SG1[:, 0:Tt], L1F[:, 0:Tt], AF.Sigmoid)
        P.memset("pool", SG2[32:64, :], 0.0)
        P.memset("pool", SG2[64:128, :], 0.0)
        b_ = next_abank()
        proj_chunk(wv, 256, 32, b_)
        shift_epi(b_, 32, 2, L1F[0:32, 0:Tt])
        P.act(SG2[0:32, 0:Tt], L1F[0:32, 0:Tt], AF.Sigmoid)

        chk(8)
        QT = min(128, Tt)
        quarters = [(h0, QT) for h0 in range(0, Tt, QT)]

        def wkv_gen(pp, h0, TH, S, rb):
            rS, kS, vS = rb
            pc = slice(pp * 128, (pp + 1) * 128)
            hc = slice(h0, h0 + TH)
            nch = TH // C
            WTs = S["WT"]
            (lws, cum, cumx, ein, einv, eex, alr, kk, t0, kkn, tk, kmod, bvec, rk, bonus, gate) = [w_[:, 0:TH] for w_ in WTs]
            BK_, BKF_, ARZ_, SC5_, TM_, ntb, mb, TTB_, RU_, UZ_, DT2 = (S["BK"], S["BKF"], S["ARZ"], S["SC5"], S["TM"],
                                                                       S["NTB"], S["MB"], S["TTB"], S["RU"], S["UZ"], S["DT"])
            pa, pb_ = S["banks"]
            P.mm(MS[:, 0:TH], WLAD[:, pc], L1S[:, hc])
            P.act(lws, MS[:, 0:TH], AF.Exp, bias=pt("nw0", pp), scale=-1.0)
            P.ts("dve", lws, lws, 1.0, ALU.add)
            P.recip(lws, lws)
            P.scan(cum, rmask[:, 0:TH], lws, 0.0, ALU.mult, ALU.add)
            P.tt("pool", cumx, cum, lws, ALU.subtract)
            yield
            P.act(ein, cum, AF.Exp, scale=-C_DEC)
            P.act(einv, cum, AF.Exp, scale=C_DEC)
            P.act(eex, cumx, AF.Exp, scale=-C_DEC)
            P.mm(MS[:, 0:TH], WLAA[:, pc], L1S[:, hc])
            P.act(alr, MS[:, 0:TH], AF.Exp, bias=pt("na0", pp), scale=-1.0)
            P.ts("dve", alr, alr, 1.0, ALU.add)
            P.recip(alr, alr)
            yield
            P.ts("pool", kk, kS[:, hc], pt("k_k", pp), ALU.mult, 0.0, ALU.add)
            P.act(R(BK_[:, 0, 0:TH]), kk, AF.Square)
            P.mm(MS[:, 0:TH], R(BONES_R), R(BK_[:, 0, 0:TH]))
            P.ts("dve", t0, MS[:, 0:TH], 1e-24, ALU.max)
            yield
            P.act(t0, t0, AF.Ln)
            P.act(t0, t0, AF.Exp, scale=-0.5)
            P.tt("pool", kkn, kk, t0, ALU.mult)
            P.ts("pool", tk, alr, pt("k_a", pp), ALU.mult, pt("omk_a", pp), ALU.add)
            P.tt("pool", kmod, kS[:, hc], tk, ALU.mult)
            yield
            P.stt(R(BK_[:, 1, 0:TH]), rS[:, hc], pt("r_k", pp), kmod, ALU.mult, ALU.mult)
            P.mm(MS[:, 0:TH], R(BONES_R), R(BK_[:, 1, 0:TH]))
            P.tt("dve", bonus, MS[:, 0:TH], vS[:, hc], ALU.mult)
            yield
            for hd in range(2):
                rw = slice(hd * 64, hd * 64 + 64)
                P.stt(R(ARZ_[hd][rw, 0, 0:TH]), kkn[rw], -1.0, eex[rw], ALU.mult, ALU.mult)
                P.tt("dve", R(ARZ_[hd][rw, 1, 0:TH]), rS[rw, hc], ein[rw], ALU.mult)
            yield
            P.tt("pool", bvec, kkn, alr, ALU.mult)
            P.tt("dve", R(BK_[:, 0, 0:TH]), bvec, einv, ALU.mult)
            P.tt("dve", R(BK_[:, 1, 0:TH]), kmod, einv, ALU.mult)
            e0 = WTs[3][:, 0:1]
            wcb = AP(e0.tensor, e0.offset + C - 1, [list(e0.ap[0]), [0, 2], [C, nch], [0, C]])
            P.tt("dve", BKF_[:, :, 0:TH].rearrange("p a (c t) -> p a c t", c=nch),
                 BK_[:, :, 0:TH].rearrange("p a (c t) -> p a c t", c=nch), wcb, ALU.mult)
            yield
            P.mm(MS[:, 0:TH], WG1[:, pc], SG1[:, hc], start=True, stop=False)
            P.mm(MS[:, 0:TH], WG2[:, pc], SG2[:, hc], start=False, stop=True)
            P.copy("act", gate, MS[:, 0:TH])
            yield
            for c in range(nch):
                cc = slice(c * C, (c + 1) * C)
                tb_ = (pa, pb_)[c % 2]
                P.transpose(tb_[0:C, 0:128], BKF_[:, 0, cc], IDF)
                P.transpose(tb_[0:C, 128:256], BKF_[:, 1, cc], IDF)
                P.transpose(tb_[0:C, 256:384], vS[:, h0 + c * C:h0 + (c + 1) * C], IDF)
            yield
            for c in range(nch):
                tb_ = (pa, pb_)[c % 2]
                P.copy("act", R(TM_[0:C, c, 0:2, :]), view(tb_[0:C, 0:256], (2, 128)))
                vz0 = TM_[0:C, c, 2, 0:64]
                vzo = AP(vz0.tensor, vz0.offset, [list(vz0.ap[0]), [192, 2], [1, 64]])
                P.copy("act", R(vzo), view(tb_[0:C, 256:384], (2, 64)))
            yield
            for c in range(nch):
                cc = slice(c * C, (c + 1) * C)
                for hd in range(2):
                    g = c * 2 + hd
                    psb = (pa, pb_)[g % 2]
                    P.mm(psb[0:C, 0:2 * C], R(BK_[:, 0, cc]), R(ARZ_[hd][:, :, cc]))
                    P.mm(psb[0:C, 2 * C:4 * C], R(BK_[:, 1, cc]), R(ARZ_[hd][:, :, cc]))
                    P.mm(psb[0:C, 4 * C:5 * C], R(ARZ_[hd][:, 0, cc]), R(BK_[:, 0, cc]))
                    P.tt("dve", R(SC5_[0:C, g, :, 0:C]), view(psb[0:C, 0:5 * C], (5, C)), MK[0:C, :, 0:C], ALU.mult)
                    yield
            gn = 2 * nch
            pav = view(pa[0:C, :], (4, 128))
            pbv = view(pb_[0:C, :], (4, 128))
            for q in range(gn):
                P.mm(pav[:, q, 0:C], R(SC5_[0:C, q, 4, 0:C]), R(SC5_[0:C, q, 0, 0:C]))
                P.mm(pav[:, q, C:2 * C], R(SC5_[0:C, q, 0, 0:C]), R(SC5_[0:C, q, 4, 0:C]))
            yield
            P.copy("act", R(ntb[0][0:C, 0:gn, 0:2 * C]), pav[:, 0:gn, 0:2 * C])
            idb_ = IDF[0:C, 0:C]
            idbc = AP(idb_.tensor, idb_.offset, [list(idb_.ap[0]), [0, gn], [1, C]])
            P.tt("dve", R(mb[0][0:C, 0:gn, 0:C]), SC5_[0:C, 0:gn, 0, 0:C], idbc, ALU.add)
            yield
            for m in range(1, nlev):
                last = (m == nlev - 1)
                cur = (m - 1) % 2
                nxt = m % 2
                for q in range(gn):
                    Nc = R(ntb[cur][0:C, q, 0:C])
                    Mc = R(ntb[cur][0:C, q, C:2 * C])
                    Tc = R(mb[cur][0:C, q, 0:C])
                    P.mm(pbv[:, q, 0:C], Mc, Tc)
                    if not last:
                        if m < nlev - 2:
                            P.mm(pav[:, q, 0:C], Mc, Nc)
                        P.mm(pav[:, q, C:2 * C], Nc, Mc)
                yield
                if not last:
                    P.copy("act", R(ntb[nxt][0:C, 0:gn, 0:2 * C]), pav[:, 0:gn, 0:2 * C])
                    P.tt("dve", R(mb[nxt][0:C, 0:gn, 0:C]), pbv[:, 0:gn, 0:C], mb[cur][0:C, 0:gn, 0:C], ALU.add)
                else:
                    P.tt("dve", R(TTB_[0:C, 0:gn, 0:C]), pbv[:, 0:gn, 0:C], mb[cur][0:C, 0:gn, 0:C], ALU.add)
                yield
            for c in range(nch):
                cc = slice(c * C, (c + 1) * C)
                tok0 = h0 + c * C
                sg = [s_ for s_ in segs if s_["start"] <= tok0 < s_["start"] + s_["L"]][0]
                H = sg["st"].HBD[:, pp, :]
                P.mm(pa[0:C, 0:128], R(ARZ_[0][:, 0, cc]), R(H), start=True, stop=False)
                P.mm(pa[0:C, 0:128], R(ARZ_[1][:, 0, cc]), R(H), start=False, stop=False)
                for hd in range(2):
                    hv = slice(hd * 64, hd * 64 + 64)
                    P.mm(pa[0:C, hv], R(SC5_[0:C, c * 2 + hd, 2, 0:C]), R(TM_[0:C, c, 2 + hd, hv]), start=False, stop=(hd == 1))
                yield
                P.copy("act", R(RU_[0:C, :]), pa[0:C, 0:128])
                yield
                for hd in range(2):
                    hv = slice(hd * 64, hd * 64 + 64)
                    P.mm(pa[0:C, 128 + hd * 64:192 + hd * 64], R(TTB_[0:C, c * 2 + hd, 0:C]), R(RU_[0:C, hv]))
                yield
                uz0 = UZ_[0][0:C, 0:64]
                uzo = AP(uz0.tensor, uz0.offset, [list(uz0.ap[0]), [192, 2], [1, 64]])
                P.copy("act", R(uzo), view(pa[0:C, 128:256], (2, 64)))
                yield
                yo_ = pb_[:, cc]
                P.mm(yo_, R(H), R(ARZ_[0][:, 1, cc]), start=True, stop=False)
                P.mm(yo_, R(H), R(ARZ_[1][:, 1, cc]), start=False, stop=False)
                for hd in range(2):
                    P.mm(yo_, R(UZ_[hd][0:C, :]), R(SC5_[0:C, c * 2 + hd, 1, 0:C]), start=False, stop=False)
                for hd in range(2):
                    P.mm(yo_, R(TM_[0:C, c, 2 + hd, :]), R(SC5_[0:C, c * 2 + hd, 3, 0:C]), start=False, stop=(hd == 1))
                dps = pa[:, 256:384]
                P.mm(dps, R(TM_[0:C, c, 0, :]), R(UZ_[0][0:C, :]), start=True, stop=False)
                P.mm(dps, R(TM_[0:C, c, 0, :]), R(UZ_[1][0:C, :]), start=False, stop=False)
                P.mm(dps, R(TM_[0:C, c, 1, :]), R(TM_[0:C, c, 2, :]), start=False, stop=False)
                P.mm(dps, R(TM_[0:C, c, 1, :]), R(TM_[0:C, c, 3, :]), start=False, stop=True)
                yield
                P.tt("dve", DT2, dps, BONES, ALU.mult)
                P.stt(R(H), H, WTs[3][:, (c + 1) * C - 1:(c + 1) * C], DT2, ALU.mult, ALU.add)
                yield
            ysb, dd, dsq = lws, cum, cumx
            P.copy("act", R(BK_[:, 0, 0:TH]), pb_[:, 0:TH])
            yield
            P.mm(MS[:, 0:TH], R(CMAT_R), R(BK_[:, 0, 0:TH]))
            P.copy("act", dd, MS[:, 0:TH])
            P.act(R(BK_[:, 1, 0:TH]), MS[:, 0:TH], AF.Square)
            yield
            P.mm(MS[:, 0:TH], R(BONES_R), R(BK_[:, 1, 0:TH]))
            P.act(t0, MS[:, 0:TH], AF.Ln, bias=GN_EPS, scale=1.0 / 64)
            yield
            P.act(t0, t0, AF.Exp, scale=-0.5)
            P.tt("dve", dd, dd, t0, ALU.mult)
            yield
            P.ts("dve", dd, dd, pt("lnx_g", pp), ALU.mult, pt("lnx_b", pp), ALU.add)
            P.tt("dve", dd, dd, bonus, ALU.add)
            P.tt("dve", YCAT[:, pp, hc], dd, gate, ALU.mult)

        def run_lockstep(gens):
            gens = list(gens)
            while gens:
                for g_ in list(gens):
                    try:
                        next(g_)
                    except StopIteration:
                        gens.remove(g_)

        for pp0 in range(0, 8, 2):
            for s_i in range(2):
                pp = pp0 + s_i
                wv = wload(win_v[:, :, WIN_RKV0 + pp * 384: WIN_RKV0 + (pp + 1) * 384], (16, 384))
                rb = RKV[s_i]
                for i in range(3):
                    b_ = next_abank()
                    proj_chunk(wv, i * 128, 128, b_)
                    shift_epi(b_, 128, 3 + pp * 3 + i, rb[i][:, 0:Tt])
            for (h0, TH) in quarters:
                run_lockstep([wkv_gen(pp0 + s_i, h0, TH, STREAMS[s_i], RKV[s_i]) for s_i in range(2)])

        chk(15)
        for ob in range(4):
            wv = wload(wout_v[:, :, ob * 512:(ob + 1) * 512], (16, 512))
            for tb in range(nb):
                b_ = next_abank()
                for kc in range(16):
                    P.mm(b_[0:tbs, :], YCAT[:, kc, tb * 128:tb * 128 + tbs], wv[:, kc, :], start=(kc == 0), stop=(kc == 15))
                xa = XT[0:tbs, tb, ob * 512:(ob + 1) * 512]
                P.tt("dve", xa, xa, b_[0:tbs, :], ALU.add)

        chk(16)
        rmsnorm_to_hT("nffn", HN_F)
        P.dma("sp", FN, final_norm.partition_broadcast(128))

        chk(17)
        zb_i = [0]
        for jb in range(22):
            wv = wload(wup_v[:, :, jb * 512:(jb + 1) * 512], (16, 512))
            for jj in range(2):
                j = jb * 2 + jj
                bu = next_abank()
                proj_chunk(wv, jj * 256, 128, bu)
                bz = (TR, MS)[zb_i[0] % 2]
                zb_i[0] += 1
                proj_chunk(wv, jj * 256 + 128, 128, bz)
                ub = UB[j % 2]
                cv = CV[j % 2]
                gl = GL[j % 2]
                for si, sg in enumerate(segs):
                    s0, L, S_ = sg["start"], sg["L"], sg["st"]
                    o_ = si * (Lmax + 2)
                    P.copy("pool", ub[:, o_:o_ + 2], S_.FH[:, j, :])
                    P.copy("act", ub[:, o_ + 2:o_ + 2 + L], bu[:, s0:s0 + L])
                    P.copy("pool", S_.FH[:, j, :], ub[:, o_ + L:o_ + L + 2])
                    P.act(cv[:, s0:s0 + L], ub[:, o_ + 2:o_ + 2 + L], AF.Identity, bias=pt("fb", j), scale=pt("fw", 2 * NJ + j))
                    P.stt(cv[:, s0:s0 + L], ub[:, o_ + 1:o_ + 1 + L], pt("fw", 1 * NJ + j), cv[:, s0:s0 + L], ALU.mult, ALU.add)
                    P.stt(cv[:, s0:s0 + L], ub[:, o_:o_ + L], pt("fw", 0 * NJ + j), cv[:, s0:s0 + L], ALU.mult, ALU.add)
                P.act(gl[:, 0:Tt], cv[:, 0:Tt], AF.Gelu_apprx_tanh)
                P.tt("dve", GB[:, j, 0:Tt], gl[:, 0:Tt], bz[:, 0:Tt], ALU.mult)

        chk(18)
        dbanks = (IA, IB, SQ, YT)
        kgroups = ((0, 16), (16, 32), (32, 44))
        for ob in range(4):
            for (k0, k1) in kgroups:
                wv = wload(wdn_v[:, k0:k1, ob * 512:(ob + 1) * 512], (k1 - k0, 512))
                for tb in range(nb):
                    for kc in range(k0, k1):
                        P.mm(dbanks[tb][0:tbs, :], GB[:, kc, tb * 128:tb * 128 + tbs], wv[:, kc - k0, :], start=(kc == 0), stop=(kc == NJ - 1))
            for tb in range(nb):
                xa = XT[0:tbs, tb, ob * 512:(ob + 1) * 512]
                P.tt("dve", xa, xa, dbanks[tb][0:tbs, :], ALU.add)

        chk(19)
        for tb in range(nb):
            yo = YO[tb % 2]
            xa = XT[0:tbs, tb, :]
            ss = SMALL[0:tbs, 24 + tb:25 + tb]
            P.act(yo[0:tbs, :], xa, AF.Square, accum_out=ss)
            sd = SMALL[0:tbs, 32 + tb:33 + tb]
            P.act(sd, ss, AF.Sqrt, bias=RMS_EPS, scale=1.0 / D)
            rs = SMALL[0:tbs, 40 + tb:41 + tb]
            P.recip(rs, sd)
            P.stt(yo[0:tbs, :], xa, rs, FN[0:tbs, :], ALU.mult, ALU.mult)
            for (r0, r1, dst) in y_stores(tb):
                P.dma("act", dst, yo[r0:r1, :])

    def store_states(S_, o_shift, o_wkv, o_conv, o_ffn):
        stg = ST32[0]
        P.transpose(TR[0:32, 0:128], S_.SH[:, 0:32], IDF)
        P.copy("dve", stg[0:32, 0:128], TR[0:32, 0:128])
        P.dma("act", rows2d(o_shift[3072:3328], 128), stg[0:2, 0:128])
        P.dma("act", o_shift[3328:3360].rearrange("(a b) -> a b", a=1), stg[2:3, 0:32])
        for pp in range(8):
            for i in range(3):
                r = 3 + pp * 3 + i
                P.dma("act", rows2d(o_shift[i * 1024 + pp * 128:i * 1024 + (pp + 1) * 128], 128), stg[r:r + 1, 0:128])
        for pp in range(8):
            P.transpose(TR[:, 128:256], S_.HBD[:, pp, :], IDF)
            P.copy("act", stg[:, 128 + pp * 128:256 + pp * 128], TR[:, 128:256])
            for hd in range(2):
                P.dma("act", o_wkv[2 * pp + hd], stg[hd * 64:hd * 64 + 64, 128 + pp * 128 + hd * 64:128 + pp * 128 + hd * 64 + 64])
        for j in range(8):
            P.transpose(TR[0:32, 256:384], S_.CH[:, j, :], IDF)
            P.copy("dve", stg[0:30, 1280 + j * 128:1280 + (j + 1) * 128], TR[0:30, 256:384])
        P.dma("act", o_conv, stg[0:30, 1280:2304])
        P.copy("dve", view(stg[:, 2304:2392], (2, NJ)), S_.FH.rearrange("p j t -> p t j"))
        P.transpose(TR[0:96, 384:512], stg[:, 2304:2400], IDF)
        P.copy("dve", stg[0:88, 2432:2560], TR[0:88, 384:512])
        P.dma("act", o_ffn.rearrange("t (j p) -> (t j) p", p=128), stg[0:88, 2432:2560])

    try:
        def small_y(tb):
            return [(16, 32, ys)]
        tile(32, [dict(start=0, L=16, st=STA), dict(start=16, L=16, st=STB)], 16,
             [(XT[0:16, 0, :], meta), (XT[16:32, 0, :], xs)], small_y)
        store_states(STB, o_shift_s, o_wkv_s, o_conv_s, o_ffn_s)
        for q in range(NSEQ):
            P.copy("pool", STB.SH, STA.SH)
            P.copy("pool", R(STB.HBD), STA.HBD)
            P.copy("pool", STB.CH, STA.CH)
            P.copy("pool", STB.FH, STA.FH)
            for tt_ in range(NT):
                def main_y(tb, q=q, tt_=tt_):
                    return [(0, 128, yp[q, tt_ * T + tb * 128: tt_ * T + (tb + 1) * 128, :])]
                tile(T, [dict(start=0, L=T, st=STB)], 64,
                     [(XT, xp[q, tt_ * T:(tt_ + 1) * T, :].rearrange("(nb p) d -> p nb d", p=128))], main_y)
            store_states(STB, o_shift_p[q], o_wkv_p[q], o_conv_p[q], o_ffn_p[q])


    except _Stop:
        pass
    P.emit()
    st.close()
    return nc


_NC_CACHE = {}


def _get_nc(NSEQ, NT, debug=None):
    key = (NSEQ, NT, debug)
    if key not in _NC_CACHE:
        _NC_CACHE[key] = build_program(NSEQ, NT, debug)
    return _NC_CACHE[key]


WEIGHT_KEYS = ("meta", "norm_mix", "w_in", "tshift_mu", "w0", "w_decay_up", "a0", "w_aaa_up", "w_gate_up",
               "k_k", "k_a", "r_k", "lnx_g", "lnx_b", "conv_w", "conv_b", "conv_ln_g", "conv_ln_b",
               "w_out", "norm_ffn", "w_ffn_up", "ffn_conv_w", "ffn_conv_b", "w_ffn_down", "final_norm")


def run_cores(inputs, n_cores, NSEQ, NT, debug=None):
    f = lambda a: np.ascontiguousarray(np.asarray(a, dtype=np.float32))
    shared = {}
    for k in WEIGHT_KEYS:
        a = f(inputs[k])
        if k in ("meta", "final_norm"):
            shared[k] = a
        elif k == "r_k":
            shared[k] = a.reshape(-1)
        else:
            shared[k] = a[0] if a.shape[0] == 1 else a
    in_maps = []
    xpf = f(inputs["x_prompt"])
    for c in range(n_cores):
        m = dict(shared)
        m["xp"] = np.ascontiguousarray(xpf[c * NSEQ:(c + 1) * NSEQ, :NT * 512])
        m["xs"] = f(inputs["x_sample"][c])
        m["st_shift"] = f(inputs["state_shift"][0, c, 0])
        m["st_wkv"] = f(inputs["state_wkv"][0, c])
        m["st_conv"] = f(inputs["cache_conv"][0, c])
        m["st_ffn"] = f(inputs["cache_ffn_conv"][0, c])
        in_maps.append(m)
    nc = _get_nc(NSEQ, NT, debug)
    res = run_bass_kernel_spmd(nc, in_maps, core_ids=list(range(n_cores)))
    return res.results


def kernel(**inputs):
    n = 8
    r = run_cores(inputs, n, 2, 4)
    cat = lambda k: np.concatenate([np.asarray(r[c][k]) for c in range(n)], axis=0)
    stk = lambda k: np.stack([np.asarray(r[c][k]) for c in range(n)], axis=0)
    y_prompt = cat("yp")
    y_sample = stk("ys")
    o_shift_p = cat("o_shift_p")[None, :, None, :]
    o_wkv_p = cat("o_wkv_p")[None]
    o_conv_p = cat("o_conv_p")[None]
    o_ffn_p = cat("o_ffn_p")[None]
    o_shift_s = stk("o_shift_s")[None, :, None, :]
    o_wkv_s = stk("o_wkv_s")[None]
    o_conv_s = stk("o_conv_s")[None]
    o_ffn_s = stk("o_ffn_s")[None]
    outs = (y_prompt, y_sample, o_shift_p, o_wkv_p, o_conv_p, o_ffn_p, o_shift_s, o_wkv_s, o_conv_s, o_ffn_s)
    return tuple(np.ascontiguousarray(o, dtype=np.float32) for o in outs)
```

```python
import numpy as np
import concourse.bass as bass
import concourse.mybir as mybir

F32 = mybir.dt.float32
BF16 = mybir.dt.bfloat16
AF = mybir.ActivationFunctionType
ALU = mybir.AluOpType

_ESZ = {F32: 4, BF16: 2}
try:
    _ESZ[mybir.dt.float32r] = 4
except Exception:
    pass


def _esize(dt):
    if dt in _ESZ:
        return _ESZ[dt]
    s = str(dt)
    if "64" in s:
        return 8
    if "32" in s:
        return 4
    if "16" in s:
        return 2
    return 1


SB_BASE = {}


def ap_box(ap):
    es = _esize(ap.dtype)
    a = ap.ap
    off = ap.offset
    name = ap.tensor.name
    space = str(ap.space)
    if "SB" in space or "PSUM" in space:
        base = SB_BASE.get(name) if "PSUM" not in space else None
        if base is not None:
            name = "SBUF"
            base_b = base
        else:
            base_b = 0
        pstep, pcount = a[0]
        if pstep == 0:
            p0 = 0
            f0 = off
            p1 = 1
        else:
            p0 = off // pstep
            f0 = off % pstep
            p1 = p0 + pcount
        lo = f0
        hi = f0
        for st, cnt in a[1:]:
            ext = st * (cnt - 1)
            if ext < 0:
                lo += ext
            else:
                hi += ext
        if "PSUM" in space:
            return (name, (p0 // 32) * 32, ((p1 + 31) // 32) * 32, 0, 1 << 20)
        return (name, p0, p1, base_b + lo * es, base_b + (hi + 1) * es)
    lo = off
    hi = off
    for st, cnt in a:
        ext = st * (cnt - 1)
        if ext < 0:
            lo += ext
        else:
            hi += ext
    return (name, 0, 1, lo * es, (hi + 1) * es)


def _overlap(a, b):
    return a[1] < b[2] and b[1] < a[2] and a[3] < b[4] and b[3] < a[4]


def _contains(outer, inner):
    return outer[1] <= inner[1] and inner[2] <= outer[2] and outer[3] <= inner[3] and inner[4] <= outer[4]


class Op:
    __slots__ = ("eng", "fn", "idx", "deps", "is_dma", "signal", "sig", "waits", "ring", "ring_val", "gidx")

    def __init__(self, eng, fn, is_dma=False):
        self.eng = eng
        self.fn = fn
        self.is_dma = is_dma
        self.deps = []
        self.signal = False
        self.sig = None
        self.waits = []
        self.ring = None
        self.ring_val = None


ENGS = ("pe", "act", "dve", "pool", "sp")
SIG_WRAP = 16000


class Prog:
    def __init__(self, nc, n_dma_rings=12):
        self.nc = nc
        self.ops = {e: [] for e in ENGS}
        self.all = []
        self.hist = {}
        self.n_rings = n_dma_rings
        self.ring_last = [None] * n_dma_rings
        self.ring_cnt = [0] * n_dma_rings
        self.ring_next = 0

    def _track(self, op, reads, writes):
        rd2 = []
        writes = list(writes)
        for ap in reads:
            if "PSUM" in str(ap.space):
                writes.append(ap)
            else:
                rd2.append(ap)
        reads = rd2
        deps = set()
        for ap in reads:
            bx = ap_box(ap)
            h = self.hist.setdefault(bx[0], [])
            for (b, o, w) in h:
                if w and _overlap(b, bx):
                    deps.add(o)
        for ap in writes:
            bx = ap_box(ap)
            h = self.hist.setdefault(bx[0], [])
            for (b, o, w) in h:
                if _overlap(b, bx):
                    deps.add(o)
        deps.discard(op)
        for ap in reads:
            bx = ap_box(ap)
            h = self.hist[bx[0]]
            if not op.is_dma:
                h[:] = [e for e in h if not ((not e[2]) and e[0] == bx and e[1].eng == op.eng and not e[1].is_dma)]
            h.append((bx, op, False))
        for ap in writes:
            bx = ap_box(ap)
            h = self.hist[bx[0]]
            h[:] = [e for e in h if not _contains(bx, e[0])]
            h.append((bx, op, True))
        op.deps = list(deps)

    def add(self, eng, fn, reads=(), writes=()):
        op = Op(eng, fn)
        op.idx = len(self.ops[eng])
        op.gidx = len(self.all)
        self.ops[eng].append(op)
        self.all.append(op)
        self._track(op, reads, writes)
        return op

    def dma(self, queue, out, in_, **kw):
        def fn(e, out=out, in_=in_, kw=kw):
            return e.dma_start(out=out, in_=in_, **kw)
        op = Op(queue, fn, is_dma=True)
        op.idx = len(self.ops[queue])
        op.gidx = len(self.all)
        self.ops[queue].append(op)
        self.all.append(op)
        self._track(op, [in_], [out])
        r = self.ring_next
        self.ring_next = (self.ring_next + 1) % self.n_rings
        prev = self.ring_last[r]
        if prev is not None and prev not in op.deps:
            op.deps.append(prev)
        self.ring_cnt[r] += 1
        op.ring = r
        op.ring_val = 16 * self.ring_cnt[r]
        self.ring_last[r] = op
        return op

    def finalize(self):
        last_vc = {e: {} for e in ENGS}
        dma_known = {e: {} for e in ENGS}
        op_vc = {}
        for op in self.all:
            E = op.eng
            vc = dict(last_vc[E])
            waits = []
            best = {}
            for d in sorted(op.deps, key=lambda o: o.gidx):
                if d.is_dma:
                    if dma_known[E].get(d.ring, 0) >= d.ring_val:
                        continue
                    dma_known[E][d.ring] = d.ring_val
                    waits.append(d)
                    continue
                if d.eng not in best or d.idx > best[d.eng].idx:
                    best[d.eng] = d
            for De, d in best.items():
                if De == E:
                    if E in ("pe", "sp"):
                        continue
                    if vc.get(E, -1) >= d.idx:
                        continue
                    vc[E] = d.idx
                    waits.append(d)
                    d.signal = True
                    continue
                if vc.get(De, -1) >= d.idx:
                    continue
                waits.append(d)
                d.signal = True
                dvc = op_vc.get(d, {})
                for k2, v2 in dvc.items():
                    if vc.get(k2, -1) < v2:
                        vc[k2] = v2
                if vc.get(De, -1) < d.idx:
                    vc[De] = d.idx
            op.waits = waits
            if not op.is_dma:
                vcc = dict(vc)
                vcc[E] = op.idx
                op_vc[op] = vcc
            last_vc[E] = vc
        self.nsig = {}
        for e in ENGS:
            c = 0
            for op in self.ops[e]:
                if op.is_dma:
                    continue
                if op.signal:
                    op.sig = c
                    c += 1
            self.nsig[e] = c

    def emit(self, final_waits=()):
        nc = self.nc
        self.finalize()
        import contextlib
        with contextlib.ExitStack() as st:
            sems = {}
            for e in ENGS:
                n = (self.nsig[e] + SIG_WRAP - 1) // SIG_WRAP
                sems[e] = [st.enter_context(nc.semaphore(f"s_{e}_{i}")) for i in range(max(n, 1))]
            rsems = [st.enter_context(nc.semaphore(f"s_ring_{i}")) for i in range(self.n_rings)]
            block = st.enter_context(nc.Block())

            def run(engname, eng):
                for op in self.ops[engname]:
                    for d in op.waits:
                        if d.is_dma:
                            eng.wait_ge(rsems[d.ring], d.ring_val)
                        else:
                            eng.wait_ge(sems[d.eng][d.sig // SIG_WRAP], d.sig % SIG_WRAP + 1)
                    ins = op.fn(eng)
                    if op.is_dma:
                        ins.then_inc(rsems[op.ring], 16)
                    elif op.signal:
                        ins.then_inc(sems[engname][op.sig // SIG_WRAP], 1)
                if engname in ("sp", "pool", "act"):
                    lastv = {}
                    for op in self.ops[engname]:
                        if op.is_dma:
                            lastv[op.ring] = max(lastv.get(op.ring, 0), op.ring_val)
                    for r, v in lastv.items():
                        eng.wait_ge(rsems[r], v)

            @block.tensor
            def _(t):
                run("pe", t)

            @block.scalar
            def _(a):
                run("act", a)

            @block.vector
            def _(v):
                run("dve", v)

            @block.gpsimd
            def _(g):
                run("pool", g)

            @block.sync
            def _(s):
                run("sp", s)

    def mm(self, out, lhsT, rhs, start=True, stop=True, extra_reads=()):
        def fn(e):
            return e.matmul(out, lhsT=lhsT, rhs=rhs, start=start, stop=stop)
        rd = [lhsT, rhs] + list(extra_reads)
        if not start:
            rd.append(out)
        return self.add("pe", fn, rd, [out])

    def transpose(self, out, in_, ident):
        def fn(e):
            return e.transpose(out=out, in_=in_, identity=ident)
        return self.add("pe", fn, [in_, ident], [out])

    def act(self, out, in_, func, bias=None, scale=None, accum_out=None, eng="act"):
        kw = {}
        rd = [in_]
        wr = [out]
        if bias is not None:
            kw["bias"] = bias
            if not isinstance(bias, (int, float)):
                rd.append(bias)
        if scale is not None:
            kw["scale"] = scale
            if not isinstance(scale, (int, float)):
                rd.append(scale)
        if accum_out is not None:
            kw["accum_out"] = accum_out
            wr.append(accum_out)

        def fn(e):
            return e.activation(out=out, in_=in_, func=func, **kw)
        return self.add("act", fn, rd, wr)

    def tt(self, eng, out, in0, in1, op):
        def fn(e):
            return e.tensor_tensor(out=out, in0=in0, in1=in1, op=op)
        return self.add(eng, fn, [in0, in1], [out])

    def ts(self, eng, out, in0, s1, op0, s2=None, op1=None, accum_out=None):
        rd = [in0]
        if not isinstance(s1, (int, float)):
            rd.append(s1)
        if s2 is not None and not isinstance(s2, (int, float)):
            rd.append(s2)
        wr = [out]
        kw = {}
        if accum_out is not None:
            kw["accum_out"] = accum_out
            wr.append(accum_out)

        def fn(e):
            if op1 is None:
                return e.tensor_scalar(out=out, in0=in0, scalar1=s1, scalar2=None, op0=op0, **kw)
            return e.tensor_scalar(out=out, in0=in0, scalar1=s1, scalar2=s2, op0=op0, op1=op1, **kw)
        return self.add(eng, fn, rd, wr)

    def stt(self, out, in0, scalar, in1, op0, op1):
        rd = [in0, in1]
        if not isinstance(scalar, (int, float)):
            rd.append(scalar)

        def fn(e):
            return e.scalar_tensor_tensor(out=out, in0=in0, scalar=scalar, in1=in1, op0=op0, op1=op1)
        return self.add("dve", fn, rd, [out])

    def copy(self, eng, out, in_):
        if eng == "act":
            def fn(e):
                return e.copy(out=out, in_=in_)
        else:
            def fn(e):
                return e.tensor_copy(out=out, in_=in_)
        return self.add(eng, fn, [in_], [out])

    def memset(self, eng, ap, val):
        def fn(e):
            return e.memset(ap, val)
        return self.add(eng, fn, [], [ap])

    def recip(self, out, in_):
        def fn(e):
            return e.reciprocal(out=out, in_=in_)
        return self.add("dve", fn, [in_], [out])

    def scan(self, out, d0, d1, init, op0, op1):
        rd = [d0, d1]
        if not isinstance(init, (int, float)):
            rd.append(init)

        def fn(e):
            return e.tensor_tensor_scan(out=out, data0=d0, data1=d1, initial=init, op0=op0, op1=op1)
        return self.add("dve", fn, rd, [out])

    def affine_select(self, out, in_, compare_op, fill, base, pattern, channel_multiplier):
        def fn(e):
            return e.affine_select(out=out, in_=in_, compare_op=compare_op, fill=fill, base=base,
                                   pattern=pattern, channel_multiplier=channel_multiplier)
        return self.add("pool", fn, [in_], [out])

from concourse.bass_utils import run_bass_kernel_spmd
from concourse.ap import AP
import contextlib

D = 2048
RW = 1024
DFF = 5632
NJ = 44
INC = 5408
RWC = 3360
C_DEC = 0.6065306597126334
RMS_EPS = 1e-6
LN_EPS = 1e-5
GN_EPS = 64e-5
F32R = mybir.dt.float32r


def R(ap):
    return ap.bitcast(F32R)

WIN_CONV0 = 0
WIN_LORA0 = 2048
WIN_RKV0 = 2336


def _prod(s):
    r = 1
    for v in s:
        r *= v
    return r


def view(ap2d, shape):
    if len(shape) == 1:
        return ap2d
    if len(shape) == 2:
        return ap2d.rearrange("p (a b) -> p a b", a=shape[0])
    if len(shape) == 3:
        return ap2d.rearrange("p (a b c) -> p a b c", a=shape[0], b=shape[1])
    raise ValueError(shape)


class Mem:
    def __init__(self, t):
        self.t = t

    def f32(self, off, shape):
        n = _prod(shape)
        return view(self.t[:, off:off + n], shape)

    def bf(self, off, shape):
        n = _prod(shape)
        assert n % 2 == 0
        return view(self.t[:, off:off + n // 2].bitcast(BF16), shape)


class _Stop(Exception):
    pass


def build_program(NSEQ=2, NT=4, debug=None):
    def chk(stage):
        if debug is not None and debug == stage:
            raise _Stop()
    nc = bass.Bass("TRN2", target_bir_lowering=False)
    P = Prog(nc)
    T = 512
    SEQ = NT * T

    def din(name, shape):
        return nc.dram_tensor(name, list(shape), F32, kind="ExternalInput").ap()

    def dout(name, shape):
        return nc.dram_tensor(name, list(shape), F32, kind="ExternalOutput").ap()

    xp = din("xp", (NSEQ, SEQ, D))
    xs = din("xs", (16, D))
    st_shift = din("st_shift", (RWC,))
    st_wkv = din("st_wkv", (16, 64, 64))
    st_conv = din("st_conv", (30, 1024))
    st_ffn = din("st_ffn", (2, DFF))
    meta = din("meta", (16, D))
    norm_mix = din("norm_mix", (D,))
    w_in = din("w_in", (D, INC))
    tshift_mu = din("tshift_mu", (RWC,))
    w0 = din("w0", (RW,))
    w_decay_up = din("w_decay_up", (64, RW))
    a0 = din("a0", (RW,))
    w_aaa_up = din("w_aaa_up", (64, RW))
    w_gate_up = din("w_gate_up", (160, RW))
    k_k = din("k_k", (RW,))
    k_a = din("k_a", (RW,))
    r_k = din("r_k", (RW,))
    lnx_g = din("lnx_g", (RW,))
    lnx_b = din("lnx_b", (RW,))
    conv_w = din("conv_w", (31, 1024))
    conv_b = din("conv_b", (1024,))
    conv_ln_g = din("conv_ln_g", (1024,))
    conv_ln_b = din("conv_ln_b", (1024,))
    w_out = din("w_out", (D, D))
    norm_ffn = din("norm_ffn", (D,))
    w_ffn_up = din("w_ffn_up", (D, 2 * DFF))
    ffn_conv_w = din("ffn_conv_w", (3, DFF))
    ffn_conv_b = din("ffn_conv_b", (DFF,))
    w_ffn_down = din("w_ffn_down", (DFF, D))
    final_norm = din("final_norm", (D,))

    yp = dout("yp", (NSEQ, SEQ, D))
    ys = dout("ys", (16, D))
    o_shift_p = dout("o_shift_p", (NSEQ, RWC))
    o_wkv_p = dout("o_wkv_p", (NSEQ, 16, 64, 64))
    o_conv_p = dout("o_conv_p", (NSEQ, 30, 1024))
    o_ffn_p = dout("o_ffn_p", (NSEQ, 2, DFF))
    o_shift_s = dout("o_shift_s", (RWC,))
    o_wkv_s = dout("o_wkv_s", (16, 64, 64))
    o_conv_s = dout("o_conv_s", (30, 1024))
    o_ffn_s = dout("o_ffn_s", (2, DFF))

    win_s = nc.dram_tensor("win_s", [D, INC], BF16).ap()
    wout_s = nc.dram_tensor("wout_s", [D, D], BF16).ap()
    wup_s = nc.dram_tensor("wup_s", [D, 2 * DFF], BF16).ap()
    wdn_s = nc.dram_tensor("wdn_s", [DFF, D], BF16).ap()

    st = contextlib.ExitStack()
    SB0 = 16512
    AB = 25024
    XB = AB + 10496
    XR0 = XB + 4864
    NH = 6912
    NR = 12800
    assert SB0 + 4 * (XR0 + NR) <= 229376
    arena_t = nc.alloc_sbuf_tensor_at("arena", [128, XR0], F32, offset=SB0)
    arena_h = nc.alloc_sbuf_tensor_at("arena_h", [128, NH], F32, offset=SB0 + 4 * XR0)
    arena_r = nc.alloc_sbuf_tensor_at("arena_r", [128, NR], F32, offset=SB0 + 4 * XR0)
    SB_BASE.clear()
    SB_BASE[arena_t.name] = SB0
    SB_BASE[arena_h.name] = SB0 + 4 * XR0
    SB_BASE[arena_r.name] = SB0 + 4 * XR0
    M = Mem(arena_t)
    MH = Mem(arena_h)
    MR = Mem(arena_r)
    HB = XR0 - AB
    banks = [st.enter_context(nc.psum_tensor(f"pb{i}", [128, 512], F32)) for i in range(8)]
    A0, A1, TR, MS, IA, IB, SQ, YT = banks

    XT = M.f32(0, (4, 2048))
    HT = M.bf(8192, (16, 512))
    WR = [M.bf(12288 + i * 4096, (8192,)) for i in range(2)]
    o = 20480
    NPT = 600
    PT = M.f32(o, (NPT,)); o += NPT
    IDF = M.f32(o, (128,)); o += 128
    IDB = M.bf(o, (128,)); o += 64
    MK = M.f32(o, (5, 64)); o += 320
    BONES = M.f32(o, (128,)); o += 128
    CMAT = M.f32(o, (128,)); o += 128
    ONES = M.f32(o, (128,)); o += 128
    RMASK = M.f32(o, (128,)); o += 128
    RMASK16 = M.f32(o, (32,)); o += 32
    WLAD = M.bf(o, (1024,)); o += 512
    WLAA = M.bf(o, (1024,)); o += 512
    WG1 = M.bf(o, (1024,)); o += 512
    WG2 = M.bf(o, (1024,)); o += 512

    class StateSet:
        pass
    sets = []
    for i in range(2):
        s_ = StateSet()
        s_.SH = M.f32(o, (32,)); o += 32
        s_.HBD = MR.f32(10496 + i * 1024, (8, 128))
        s_.CH = M.f32(o, (8, 32)); o += 256
        s_.FH = M.f32(o, (NJ, 2)); o += 88
        sets.append(s_)
    STA, STB = sets
    SMALL = M.f32(o, (64,)); o += 64
    assert o <= AB, (o, AB)

    YCAT = M.bf(AB + 0, (16, 512))
    PBUF = [M.f32(AB + 4096 + i * 512, (512,)) for i in range(2)]
    DTMP = [M.f32(AB + 5120 + i * 512, (512,)) for i in range(2)]
    L1S = M.bf(AB + 6144, (512,))
    SG1 = M.bf(AB + 6400, (512,))
    SG2 = M.bf(AB + 6656, (512,))
    L1F = M.f32(AB + 6912, (512,))
    RKV = [[M.f32(AB + 7424 + (b * 3 + i) * 512, (512,)) for i in range(3)] for b in range(2)]
    HN_A = [M.bf(XB + i * 1024, (2048,)) for i in range(2)]
    GLU_OFF = XB
    CT = [M.f32(XB + 4336, (512,))] + [MH.f32(4096 + i * 512, (512,)) for i in range(5)]
    CO = MH.f32(0, (8, 512))
    BONES_R = MR.f32(12544, (128,))
    CMAT_R = MR.f32(12672, (128,))
    STREAMS = []
    for s_i in range(2):
        S = {}
        S["WT"] = [M.f32(XB + s_i * 2048 + i * 128, (128,)) for i in range(16)]
        S["BKF"] = M.f32(XB + 4096 + s_i * 256, (2, 128))
        S["DT"] = M.f32(XB + 4608 + s_i * 128, (128,))
        S["BK"] = MR.f32(s_i * 256, (2, 128))
        S["SC5"] = MR.f32(512 + s_i * 1280, (4, 5, 64))
        S["NTB"] = [MR.f32(3072 + (s_i * 2 + i) * 512, (4, 128)) for i in range(2)]
        S["MB"] = [MR.f32(5120 + (s_i * 2 + i) * 256, (4, 64)) for i in range(2)]
        S["TTB"] = MR.f32(6144 + s_i * 256, (4, 64))
        S["RU"] = MR.f32(6656 + s_i * 128, (128,))
        S["ARZ"] = [MR.f32(6912 + (s_i * 2 + hd) * 256, (2, 128)) for hd in range(2)]
        S["TM"] = MR.f32(7936 + s_i * 1024, (2, 4, 128))
        S["UZ"] = [MR.f32(9984 + (s_i * 2 + hd) * 128, (128,)) for hd in range(2)]
        S["banks"] = (IA, IB) if s_i == 0 else (SQ, YT)
        STREAMS.append(S)
    GB = M.bf(AB + 0, (NJ, 512))
    FN = M.f32(AB + 11264, (2048,))
    UB = [M.f32(AB + 13312 + i * 520, (520,)) for i in range(2)]
    CV = [M.f32(AB + 14352, (512,)), MH.f32(4096, (512,))]
    YO = [MH.f32(i * 2048, (2048,)) for i in range(2)]
    GL = [MH.f32(4608 + i * 512, (512,)) for i in range(2)]
    HN_F = [M.bf(AB + 4096 + i * 1024, (2048,)) for i in range(2)]
    assert AB + 14864 <= XR0
    ST32 = [M.f32(AB + i * 5632, (5632,)) for i in range(2)]
    ST16 = [M.bf(AB + 11264, (5632,)), MH.bf(0, (5632,))]
    PSTG = [MH.f32(2816 + i * 128, (128,)) for i in range(2)]
    cols = {}
    cpos = [0]

    def pcol(name, n):
        cols[name] = cpos[0]
        cpos[0] += n
    for nm, n in (("mu", 27), ("w0", 8), ("a0", 8), ("k_k", 8), ("k_a", 8), ("r_k", 8), ("lnx_g", 8),
                  ("lnx_b", 8), ("conv_b", 8), ("cln_g", 8), ("cln_b", 8), ("nmix", 16), ("nffn", 16),
                  ("cw", 248), ("fw", 132), ("fb", 44), ("omk_a", 8), ("nw0", 8), ("na0", 8)):
        pcol(nm, n)
    assert cpos[0] <= NPT

    def pt(name, i=0, rows=slice(0, 128)):
        c = cols[name] + i
        return PT[rows, c:c + 1]

    def rows2d(ap1d, n):
        return ap1d.rearrange("(c p) -> c p", p=n)

    P.memset("pool", IDF, 0.0)
    P.affine_select(IDF, IDF, ALU.not_equal, 1.0, 0, [[-1, 128]], 1)
    P.copy("dve", IDB, IDF)
    P.memset("pool", ONES, 1.0)
    P.memset("pool", BONES, 0.0)
    P.memset("pool", BONES[0:64, 0:64], 1.0)
    P.memset("pool", BONES[64:128, 64:128], 1.0)
    P.stt(CMAT, BONES, -1.0 / 64.0, IDF, ALU.mult, ALU.add)
    P.copy("dve", R(BONES_R), BONES)
    P.copy("dve", R(CMAT_R), CMAT)
    P.memset("pool", MK[0:64], 1.0)
    for q, cmp_, cm, pat in ((0, ALU.is_gt, -1, 1), (1, ALU.is_ge, -1, 1), (2, ALU.is_gt, -1, 1),
                             (3, ALU.is_ge, -1, 1), (4, ALU.is_gt, 1, -1)):
        P.affine_select(MK[0:64, q, :], MK[0:64, q, :], cmp_, 0.0, 0, [[pat, 64]], cm)
    P.memset("pool", RMASK, 1.0)
    P.memset("pool", RMASK[:, 0:128:64], 0.0)
    P.memset("pool", RMASK16, 1.0)
    P.memset("pool", RMASK16[:, 0:32:16], 0.0)
    ZERO = M.f32(AB + 0, (2048,))
    P.memset("pool", ZERO, 0.0)
    P.copy("dve", R(MR.f32(6912, (1024,))), ZERO[:, 0:1024])
    P.copy("dve", R(MR.f32(7936, (2048,))[0:64]), ZERO[0:64, 0:2048])
    P.copy("dve", R(MR.f32(9984, (512,))[0:64]), ZERO[0:64, 0:512])
    P.memset("pool", SG2, 0.0)
    for s_ in sets:
        P.copy("dve", R(s_.HBD.rearrange("p a b -> p (a b)")), ZERO[:, 0:1024])
    P.memset("pool", STA.SH, 0.0)
    P.memset("pool", STA.CH, 0.0)
    P.memset("pool", STA.FH, 0.0)
    P.memset("pool", STB.SH, 0.0)
    P.memset("pool", STB.CH, 0.0)

    if debug is not None and debug <= 0:
        P.emit(); st.close(); return nc
    row_jobs = []

    def addrows(ap1d, name, n, base=0):
        row_jobs.append((rows2d(ap1d, 128), n, cols[name] + base))
    def rwkv_rows(src1d):
        jobs = []
        jobs.append((rows2d(src1d[3072:3328], 128), 2, 0))
        jobs.append((src1d[3328:3360].rearrange("(a b) -> a b", a=1), 1, 2))
        for pp in range(8):
            for i in range(3):
                jobs.append((rows2d(src1d[i * 1024 + pp * 128: i * 1024 + (pp + 1) * 128], 128), 1, 3 + pp * 3 + i))
        return jobs
    for (ap_, n, c) in rwkv_rows(tshift_mu):
        row_jobs.append((ap_, n, cols["mu"] + c))
    addrows(w0, "w0", 8); addrows(a0, "a0", 8); addrows(k_k, "k_k", 8); addrows(k_a, "k_a", 8)
    addrows(r_k, "r_k", 8); addrows(lnx_g, "lnx_g", 8); addrows(lnx_b, "lnx_b", 8)
    addrows(conv_b, "conv_b", 8); addrows(conv_ln_g, "cln_g", 8); addrows(conv_ln_b, "cln_b", 8)
    addrows(norm_mix, "nmix", 16); addrows(norm_ffn, "nffn", 16)
    cwf = conv_w.rearrange("w c -> (w c)")
    for blk in range(0, 248, 124):
        row_jobs.append((rows2d(cwf[blk * 128:(blk + 124) * 128], 128), 124, cols["cw"] + blk))
    fwf = ffn_conv_w.rearrange("w c -> (w c)")
    for blk in range(0, 132, 66):
        row_jobs.append((rows2d(fwf[blk * 128:(blk + 66) * 128], 128), 66, cols["fw"] + blk))
    addrows(ffn_conv_b, "fb", 44)

    def run_row_jobs(jobs, dest_fn, k0=0):
        k = k0
        i = 0
        while i < len(jobs):
            stg = PSTG[k % 2]
            k += 1
            batch = []
            used = 0
            P.memset("pool", stg, 0.0)
            while i < len(jobs) and used + jobs[i][1] <= 128:
                ap_, n, c = jobs[i]
                w = ap_.shape[1]
                P.dma("sp", stg[used:used + n, 0:w], ap_)
                batch.append((used, n, c))
                used += n
                i += 1
            P.transpose(TR[:, 0:128], stg, IDF)
            for (r0, n, c) in batch:
                P.copy("dve", dest_fn(c, n), TR[:, r0:r0 + n])
        return k
    kk_ = run_row_jobs(row_jobs, lambda c, n: PT[:, c:c + n])
    P.ts("dve", PT[:, cols["omk_a"]:cols["omk_a"] + 8], PT[:, cols["k_a"]:cols["k_a"] + 8], -1.0, ALU.mult, 1.0, ALU.add)
    P.ts("dve", PT[:, cols["nw0"]:cols["nw0"] + 8], PT[:, cols["w0"]:cols["w0"] + 8], -1.0, ALU.mult)
    P.ts("dve", PT[:, cols["na0"]:cols["na0"] + 8], PT[:, cols["a0"]:cols["a0"] + 8], -1.0, ALU.mult)

    if debug is not None and debug <= 1:
        P.emit(); st.close(); return nc
    P.memset("pool", WLAD, 0.0)
    P.memset("pool", WLAA, 0.0)
    P.memset("pool", WG2, 0.0)
    s32 = ST32[0]
    P.dma("sp", s32[0:64, 0:1024], w_decay_up)
    P.dma("sp", s32[64:128, 0:1024], w_aaa_up)
    P.copy("dve", WLAD[0:64, :], s32[0:64, 0:1024])
    P.copy("dve", WLAA[64:128, :], s32[64:128, 0:1024])
    P.dma("sp", s32[:, 1024:2048], w_gate_up[0:128, :])
    P.dma("sp", s32[0:32, 2048:3072], w_gate_up[128:160, :])
    P.copy("dve", WG1, s32[:, 1024:2048])
    P.copy("dve", WG2[0:32, :], s32[0:32, 2048:3072])

    if debug is not None and debug <= 2:
        P.emit(); st.close(); return nc
    run_row_jobs([(a_, n, c) for (a_, n, c) in rwkv_rows(st_shift)], lambda c, n: STB.SH[:, c:c + n], kk_)
    s32b = ST32[1]
    P.dma("sp", view(s32b[0:64, 0:1024], (16, 64)), st_wkv.rearrange("h v k -> v h k"))
    for pp in range(8):
        P.transpose(TR[:, 0:64], s32b[0:64, pp * 128:(pp + 1) * 128], IDF[0:64, 0:64])
        P.copy("dve", R(STB.HBD[0:64, pp, 0:64]), TR[0:64, 0:64])
        P.copy("dve", R(STB.HBD[64:128, pp, 64:128]), TR[64:128, 0:64])
    P.dma("sp", s32b[0:30, 1024:2048], st_conv)
    for j in range(8):
        P.transpose(TR[:, 0:128], s32b[:, 1024 + j * 128:1024 + (j + 1) * 128], IDF)
        P.copy("dve", STB.CH[:, j, 0:30], TR[:, 0:30])
    P.dma("sp", s32b[0:88, 2048:2176], st_ffn.rearrange("t (j p) -> (t j) p", p=128))
    P.transpose(TR[:, 128:256], s32b[:, 2048:2176], IDF)
    P.copy("dve", STB.FH.rearrange("p j t -> p t j"), view(TR[:, 128:216], (2, NJ)))

    if debug is not None and debug <= 3:
        P.emit(); st.close(); return nc
    cast_engs = ["dve", "act"]
    cast_i = [0]

    def cast(out, in_):
        e = cast_engs[cast_i[0] % 2]
        cast_i[0] += 1
        P.copy(e, out, in_)
    sidx = [0]

    def conv_rows_generic(src_rows_ap, ncols, dst_rows_ap, permute=None):
        i = sidx[0] % 2
        sidx[0] += 1
        a32 = ST32[i][:, 0:ncols]
        a16 = ST16[i][:, 0:ncols]
        P.dma("sp", a32, src_rows_ap)
        if permute is None:
            half = ncols // 2
            cast(a16[:, 0:half], a32[:, 0:half])
            cast(a16[:, half:ncols], a32[:, half:ncols])
        else:
            permute(a16, a32)
        P.dma("sp", dst_rows_ap, a16)

    def perm_win(a16, a32):
        cast(a16[:, 0:2048].rearrange("q (j t c) -> q j t c", j=8, t=2),
             a32[:, 3360:5408].rearrange("q (t j c) -> q j t c", t=2, j=8))
        cast(a16[:, 2048:2336], a32[:, 3072:3360])
        cast(a16[:, 2336:5408].rearrange("q (p i c) -> q p i c", p=8, i=3),
             a32[:, 0:3072].rearrange("q (i p c) -> q p i c", i=3, p=8))
    for kc in range(16):
        conv_rows_generic(w_in[kc * 128:(kc + 1) * 128, :], INC, win_s[kc * 128:(kc + 1) * 128, :], perm_win)
    for kc in range(16):
        conv_rows_generic(w_out[kc * 128:(kc + 1) * 128, :], D, wout_s[kc * 128:(kc + 1) * 128, :])

    def perm_up(a16, a32):
        cast(a16.rearrange("q (j t c) -> q j t c", j=22, t=2), a32.rearrange("q (t j c) -> q j t c", t=2, j=22))
    for kc in range(16):
        for hf in range(2):
            src = w_ffn_up[kc * 128:(kc + 1) * 128, :].rearrange("q (t n) -> q t n", t=2)[:, :, hf * 2816:(hf + 1) * 2816]
            i = sidx[0] % 2
            sidx[0] += 1
            a32 = ST32[i]
            a16 = ST16[i]
            P.dma("sp", view(a32, (2, 2816)), src)
            perm_up(a16, a32)
            P.dma("sp", wup_s[kc * 128:(kc + 1) * 128, hf * 5632:(hf + 1) * 5632], a16)
    for kc in range(NJ):
        conv_rows_generic(w_ffn_down[kc * 128:(kc + 1) * 128, :], D, wdn_s[kc * 128:(kc + 1) * 128, :])

    if debug is not None and debug <= 4:
        P.emit(); st.close(); return nc
    wslot = [0]

    def wload(src_ap, shape):
        s_ = WR[wslot[0] % 2]
        wslot[0] += 1
        n = _prod(shape)
        v = view(s_[:, 0:n], shape)
        P.dma("sp", v, src_ap)
        return v

    win_v = win_s.rearrange("(kc p) n -> p kc n", p=128)
    wout_v = wout_s.rearrange("(kc p) n -> p kc n", p=128)
    wup_v = wup_s.rearrange("(kc p) n -> p kc n", p=128)
    wdn_v = wdn_s.rearrange("(kc p) n -> p kc n", p=128)

    abank = [0]

    def next_abank():
        b = (A0, A1)[abank[0] % 2]
        abank[0] += 1
        return b

    def tile(Tt, segs, C, x_loads, y_stores):
        nb = max(1, Tt // 128)
        tbs = min(Tt, 128)
        nch_tile = Tt // C
        rmask = RMASK if C == 64 else RMASK16
        nlev = {64: 6, 16: 4}[C]

        for (dst, src) in x_loads:
            P.dma("sp", dst, src)

        def rmsnorm_to_hT(gname, HN):
            for tb in range(nb):
                hn = HN[tb % 2]
                xa = XT[0:tbs, tb, :]
                ss = SMALL[0:tbs, tb:tb + 1]
                P.act(hn[0:tbs, :], xa, AF.Square, accum_out=ss)
                sd = SMALL[0:tbs, 8 + tb:9 + tb]
                P.act(sd, ss, AF.Sqrt, bias=RMS_EPS, scale=1.0 / D)
                rs = SMALL[0:tbs, 16 + tb:17 + tb]
                P.recip(rs, sd)
                P.ts("dve", hn[0:tbs, :], xa, rs, ALU.mult)
                for k4 in range(4):
                    trb = TR[:, (k4 % 2) * 256:(k4 % 2) * 256 + 256].bitcast(BF16)
                    for q in range(4):
                        kc = k4 * 4 + q
                        P.transpose(trb[:, q * 128:q * 128 + tbs], hn[0:tbs, kc * 128:(kc + 1) * 128], IDB[0:tbs, 0:tbs])
                    gain = PT[:, cols[gname] + k4 * 4: cols[gname] + k4 * 4 + 4]
                    gb = AP(gain.tensor, gain.offset, [list(gain.ap[0]), [1, 4], [0, tbs]])
                    src = view(trb, (4, 128))[:, :, 0:tbs]
                    P.tt("dve", HT[:, k4 * 4:(k4 + 1) * 4, tb * 128:tb * 128 + tbs], src, gb, ALU.mult)

        rmsnorm_to_hT("nmix", HN_A)

        chk(5)
        def proj_chunk(wv, c0, Mrows, bank):
            for kc in range(16):
                P.mm(bank[0:Mrows, 0:Tt], wv[:, kc, c0:c0 + Mrows], HT[:, kc, 0:Tt], start=(kc == 0), stop=(kc == 15))

        pb_i = [0]

        def shift_epi(bank, Mrows, mcol, out_ap):
            pb = PBUF[pb_i[0] % 2]
            dt = DTMP[pb_i[0] % 2]
            pb_i[0] += 1
            P.copy("act", pb[0:Mrows, 0:Tt], bank[0:Mrows, 0:Tt])
            for sg in segs:
                s0, L, S_ = sg["start"], sg["L"], sg["st"]
                P.tt("dve", dt[0:Mrows, s0 + 1:s0 + L], pb[0:Mrows, s0:s0 + L - 1], pb[0:Mrows, s0 + 1:s0 + L], ALU.subtract)
                P.tt("dve", dt[0:Mrows, s0:s0 + 1], S_.SH[0:Mrows, mcol:mcol + 1], pb[0:Mrows, s0:s0 + 1], ALU.subtract)
                P.copy("pool", S_.SH[0:Mrows, mcol:mcol + 1], pb[0:Mrows, s0 + L - 1:s0 + L])
            P.stt(out_ap, dt[0:Mrows, 0:Tt], PT[0:Mrows, cols["mu"] + mcol:cols["mu"] + mcol + 1], pb[0:Mrows, 0:Tt], ALU.mult, ALU.add)

        nseg = len(segs)
        Lmax = max(sg["L"] for sg in segs)
        GW = 30 + Lmax
        G = M.f32(GLU_OFF, (8, nseg, GW))
        for jb in range(4):
            wv = wload(win_v[:, :, WIN_CONV0 + jb * 512: WIN_CONV0 + (jb + 1) * 512], (16, 512))
            for jj in range(2):
                j = jb * 2 + jj
                bv = next_abank()
                proj_chunk(wv, jj * 256, 128, bv)
                bg = next_abank()
                proj_chunk(wv, jj * 256 + 128, 128, bg)
                sgt = CT[0]
                P.act(sgt[:, 0:Tt], bg[:, 0:Tt], AF.Sigmoid)
                for si, sg in enumerate(segs):
                    s0, L, S_ = sg["start"], sg["L"], sg["st"]
                    P.copy("pool", G[:, j, si, 0:30], S_.CH[:, j, 0:30])
                    P.tt("dve", G[:, j, si, 30:30 + L], bv[:, s0:s0 + L], sgt[:, s0:s0 + L], ALU.mult)
                    P.copy("pool", S_.CH[:, j, 0:30], G[:, j, si, L:L + 30])
        chk(6)
        for j in range(8):
            for si, sg in enumerate(segs):
                s0, L = sg["start"], sg["L"]
                acc = CO[:, j, s0:s0 + L]
                P.ts("dve", acc, G[:, j, si, 0:L], pt("cw", 0 * 8 + j), ALU.mult, pt("conv_b", j), ALU.add)
                for w in range(1, 31):
                    P.stt(acc, G[:, j, si, w:w + L], pt("cw", w * 8 + j), acc, ALU.mult, ALU.add)
            P.mm(MS[:, 0:Tt], ONES, CO[:, j, 0:Tt], start=(j == 0), stop=(j == 7))
            sq = CT[1 + (j % 2)]
            P.act(sq[:, 0:Tt], CO[:, j, 0:Tt], AF.Square)
            P.mm(TR[:, 0:Tt], ONES, sq[:, 0:Tt], start=(j == 0), stop=(j == 7))
        mean, m2, rstd = CT[3], CT[4], CT[5]
        P.ts("dve", mean[:, 0:Tt], MS[:, 0:Tt], 1.0 / 1024, ALU.mult)
        P.tt("dve", m2[:, 0:Tt], mean[:, 0:Tt], mean[:, 0:Tt], ALU.mult)
        P.stt(m2[:, 0:Tt], TR[:, 0:Tt], 1.0 / 1024, m2[:, 0:Tt], ALU.mult, ALU.subtract)
        P.act(m2[:, 0:Tt], m2[:, 0:Tt], AF.Sqrt, bias=LN_EPS, scale=1.0)
        P.recip(rstd[:, 0:Tt], m2[:, 0:Tt])
        for j in range(8):
            t1 = CT[1 + (j % 2)]
            P.tt("dve", t1[:, 0:Tt], CO[:, j, 0:Tt], mean[:, 0:Tt], ALU.subtract)
            P.tt("dve", t1[:, 0:Tt], t1[:, 0:Tt], rstd[:, 0:Tt], ALU.mult)
            P.act(YCAT[:, 8 + j, 0:Tt], t1[:, 0:Tt], AF.Silu, bias=pt("cln_b", j), scale=pt("cln_g", j))

        chk(7)
        wv = wload(win_v[:, :, WIN_LORA0:WIN_LORA0 + 288], (16, 288))
        b_ = next_abank()
        proj_chunk(wv, 0, 128, b_)
        shift_epi(b_, 128, 0, L1F[:, 0:Tt])
        P.act(L1S[0:64, 0:Tt], L1F[0:64, 0:Tt], AF.Tanh)
        P.copy("pool", L1S[64:128, 0:Tt], L1F[64:128, 0:Tt])
        b_ = next_abank()
        proj_chunk(wv, 128, 128, b_)
        shift_epi(b_, 128, 1, L1F[:, 0:Tt])
        P.act(SG1[:, 0:Tt], L1F[:, 0:Tt], AF.Sigmoid)
        P.memset("pool", SG2[32:64, :], 0.0)
        P.memset("pool", SG2[64:128, :], 0.0)
        b_ = next_abank()
        proj_chunk(wv, 256, 32, b_)
        shift_epi(b_, 32, 2, L1F[0:32, 0:Tt])
        P.act(SG2[0:32, 0:Tt], L1F[0:32, 0:Tt], AF.Sigmoid)

        chk(8)
        QT = min(128, Tt)
        quarters = [(h0, QT) for h0 in range(0, Tt, QT)]

        def wkv_gen(pp, h0, TH, S, rb):
            rS, kS, vS = rb
            pc = slice(pp * 128, (pp + 1) * 128)
            hc = slice(h0, h0 + TH)
            nch = TH // C
            WTs = S["WT"]
            (lws, cum, cumx, ein, einv, eex, alr, kk, t0, kkn, tk, kmod, bvec, rk, bonus, gate) = [w_[:, 0:TH] for w_ in WTs]
            BK_, BKF_, ARZ_, SC5_, TM_, ntb, mb, TTB_, RU_, UZ_, DT2 = (S["BK"], S["BKF"], S["ARZ"], S["SC5"], S["TM"],
                                                                       S["NTB"], S["MB"], S["TTB"], S["RU"], S["UZ"], S["DT"])
            pa, pb_ = S["banks"]
            P.mm(MS[:, 0:TH], WLAD[:, pc], L1S[:, hc])
            P.act(lws, MS[:, 0:TH], AF.Exp, bias=pt("nw0", pp), scale=-1.0)
            P.ts("dve", lws, lws, 1.0, ALU.add)
            P.recip(lws, lws)
            P.scan(cum, rmask[:, 0:TH], lws, 0.0, ALU.mult, ALU.add)
            P.tt("pool", cumx, cum, lws, ALU.subtract)
            yield
            P.act(ein, cum, AF.Exp, scale=-C_DEC)
            P.act(einv, cum, AF.Exp, scale=C_DEC)
            P.act(eex, cumx, AF.Exp, scale=-C_DEC)
            P.mm(MS[:, 0:TH], WLAA[:, pc], L1S[:, hc])
            P.act(alr, MS[:, 0:TH], AF.Exp, bias=pt("na0", pp), scale=-1.0)
            P.ts("dve", alr, alr, 1.0, ALU.add)
            P.recip(alr, alr)
            yield
            P.ts("pool", kk, kS[:, hc], pt("k_k", pp), ALU.mult, 0.0, ALU.add)
            P.act(R(BK_[:, 0, 0:TH]), kk, AF.Square)
            P.mm(MS[:, 0:TH], R(BONES_R), R(BK_[:, 0, 0:TH]))
            P.ts("dve", t0, MS[:, 0:TH], 1e-24, ALU.max)
            yield
            P.act(t0, t0, AF.Ln)
            P.act(t0, t0, AF.Exp, scale=-0.5)
            P.tt("pool", kkn, kk, t0, ALU.mult)
            P.ts("pool", tk, alr, pt("k_a", pp), ALU.mult, pt("omk_a", pp), ALU.add)
            P.tt("pool", kmod, kS[:, hc], tk, ALU.mult)
            yield
            P.stt(R(BK_[:, 1, 0:TH]), rS[:, hc], pt("r_k", pp), kmod, ALU.mult, ALU.mult)
            P.mm(MS[:, 0:TH], R(BONES_R), R(BK_[:, 1, 0:TH]))
            P.tt("dve", bonus, MS[:, 0:TH], vS[:, hc], ALU.mult)
            yield
            for hd in range(2):
                rw = slice(hd * 64, hd * 64 + 64)
                P.stt(R(ARZ_[hd][rw, 0, 0:TH]), kkn[rw], -1.0, eex[rw], ALU.mult, ALU.mult)
                P.tt("dve", R(ARZ_[hd][rw, 1, 0:TH]), rS[rw, hc], ein[rw], ALU.mult)
            yield
            P.tt("pool", bvec, kkn, alr, ALU.mult)
            P.tt("dve", R(BK_[:, 0, 0:TH]), bvec, einv, ALU.mult)
            P.tt("dve", R(BK_[:, 1, 0:TH]), kmod, einv, ALU.mult)
            e0 = WTs[3][:, 0:1]
            wcb = AP(e0.tensor, e0.offset + C - 1, [list(e0.ap[0]), [0, 2], [C, nch], [0, C]])
            P.tt("dve", BKF_[:, :, 0:TH].rearrange("p a (c t) -> p a c t", c=nch),
                 BK_[:, :, 0:TH].rearrange("p a (c t) -> p a c t", c=nch), wcb, ALU.mult)
            yield
            P.mm(MS[:, 0:TH], WG1[:, pc], SG1[:, hc], start=True, stop=False)
            P.mm(MS[:, 0:TH], WG2[:, pc], SG2[:, hc], start=False, stop=True)
            P.copy("act", gate, MS[:, 0:TH])
            yield
            for c in range(nch):
                cc = slice(c * C, (c + 1) * C)
                tb_ = (pa, pb_)[c % 2]
                P.transpose(tb_[0:C, 0:128], BKF_[:, 0, cc], IDF)
                P.transpose(tb_[0:C, 128:256], BKF_[:, 1, cc], IDF)
                P.transpose(tb_[0:C, 256:384], vS[:, h0 + c * C:h0 + (c + 1) * C], IDF)
            yield
            for c in range(nch):
                tb_ = (pa, pb_)[c % 2]
                P.copy("act", R(TM_[0:C, c, 0:2, :]), view(tb_[0:C, 0:256], (2, 128)))
                vz0 = TM_[0:C, c, 2, 0:64]
                vzo = AP(vz0.tensor, vz0.offset, [list(vz0.ap[0]), [192, 2], [1, 64]])
                P.copy("act", R(vzo), view(tb_[0:C, 256:384], (2, 64)))
            yield
            for c in range(nch):
                cc = slice(c * C, (c + 1) * C)
                for hd in range(2):
                    g = c * 2 + hd
                    psb = (pa, pb_)[g % 2]
                    P.mm(psb[0:C, 0:2 * C], R(BK_[:, 0, cc]), R(ARZ_[hd][:, :, cc]))
                    P.mm(psb[0:C, 2 * C:4 * C], R(BK_[:, 1, cc]), R(ARZ_[hd][:, :, cc]))
                    P.mm(psb[0:C, 4 * C:5 * C], R(ARZ_[hd][:, 0, cc]), R(BK_[:, 0, cc]))
                    P.tt("dve", R(SC5_[0:C, g, :, 0:C]), view(psb[0:C, 0:5 * C], (5, C)), MK[0:C, :, 0:C], ALU.mult)
                    yield
            gn = 2 * nch
            pav = view(pa[0:C, :], (4, 128))
            pbv = view(pb_[0:C, :], (4, 128))
            for q in range(gn):
                P.mm(pav[:, q, 0:C], R(SC5_[0:C, q, 4, 0:C]), R(SC5_[0:C, q, 0, 0:C]))
                P.mm(pav[:, q, C:2 * C], R(SC5_[0:C, q, 0, 0:C]), R(SC5_[0:C, q, 4, 0:C]))
            yield
            P.copy("act", R(ntb[0][0:C, 0:gn, 0:2 * C]), pav[:, 0:gn, 0:2 * C])
            idb_ = IDF[0:C, 0:C]
            idbc = AP(idb_.tensor, idb_.offset, [list(idb_.ap[0]), [0, gn], [1, C]])
            P.tt("dve", R(mb[0][0:C, 0:gn, 0:C]), SC5_[0:C, 0:gn, 0, 0:C], idbc, ALU.add)
            yield
            for m in range(1, nlev):
                last = (m == nlev - 1)
                cur = (m - 1) % 2
                nxt = m % 2
                for q in range(gn):
                    Nc = R(ntb[cur][0:C, q, 0:C])
                    Mc = R(ntb[cur][0:C, q, C:2 * C])
                    Tc = R(mb[cur][0:C, q, 0:C])
                    P.mm(pbv[:, q, 0:C], Mc, Tc)
                    if not last:
                        if m < nlev - 2:
                            P.mm(pav[:, q, 0:C], Mc, Nc)
                        P.mm(pav[:, q, C:2 * C], Nc, Mc)
                yield
                if not last:
                    P.copy("act", R(ntb[nxt][0:C, 0:gn, 0:2 * C]), pav[:, 0:gn, 0:2 * C])
                    P.tt("dve", R(mb[nxt][0:C, 0:gn, 0:C]), pbv[:, 0:gn, 0:C], mb[cur][0:C, 0:gn, 0:C], ALU.add)
                else:
                    P.tt("dve", R(TTB_[0:C, 0:gn, 0:C]), pbv[:, 0:gn, 0:C], mb[cur][0:C, 0:gn, 0:C], ALU.add)
                yield
            for c in range(nch):
                cc = slice(c * C, (c + 1) * C)
                tok0 = h0 + c * C
                sg = [s_ for s_ in segs if s_["start"] <= tok0 < s_["start"] + s_["L"]][0]
                H = sg["st"].HBD[:, pp, :]
                P.mm(pa[0:C, 0:128], R(ARZ_[0][:, 0, cc]), R(H), start=True, stop=False)
                P.mm(pa[0:C, 0:128], R(ARZ_[1][:, 0, cc]), R(H), start=False, stop=False)
                for hd in range(2):
                    hv = slice(hd * 64, hd * 64 + 64)
                    P.mm(pa[0:C, hv], R(SC5_[0:C, c * 2 + hd, 2, 0:C]), R(TM_[0:C, c, 2 + hd, hv]), start=False, stop=(hd == 1))
                yield
                P.copy("act", R(RU_[0:C, :]), pa[0:C, 0:128])
                yield
                for hd in range(2):
                    hv = slice(hd * 64, hd * 64 + 64)
                    P.mm(pa[0:C, 128 + hd * 64:192 + hd * 64], R(TTB_[0:C, c * 2 + hd, 0:C]), R(RU_[0:C, hv]))
                yield
                uz0 = UZ_[0][0:C, 0:64]
                uzo = AP(uz0.tensor, uz0.offset, [list(uz0.ap[0]), [192, 2], [1, 64]])
                P.copy("act", R(uzo), view(pa[0:C, 128:256], (2, 64)))
                yield
                yo_ = pb_[:, cc]
                P.mm(yo_, R(H), R(ARZ_[0][:, 1, cc]), start=True, stop=False)
                P.mm(yo_, R(H), R(ARZ_[1][:, 1, cc]), start=False, stop=False)
                for hd in range(2):
                    P.mm(yo_, R(UZ_[hd][0:C, :]), R(SC5_[0:C, c * 2 + hd, 1, 0:C]), start=False, stop=False)
                for hd in range(2):
                    P.mm(yo_, R(TM_[0:C, c, 2 + hd, :]), R(SC5_[0:C, c * 2 + hd, 3, 0:C]), start=False, stop=(hd == 1))
                dps = pa[:, 256:384]
                P.mm(dps, R(TM_[0:C, c, 0, :]), R(UZ_[0][0:C, :]), start=True, stop=False)
                P.mm(dps, R(TM_[0:C, c, 0, :]), R(UZ_[1][0:C, :]), start=False, stop=False)
                P.mm(dps, R(TM_[0:C, c, 1, :]), R(TM_[0:C, c, 2, :]), start=False, stop=False)
                P.mm(dps, R(TM_[0:C, c, 1, :]), R(TM_[0:C, c, 3, :]), start=False, stop=True)
                yield
                P.tt("dve", DT2, dps, BONES, ALU.mult)
                P.stt(R(H), H, WTs[3][:, (c + 1) * C - 1:(c + 1) * C], DT2, ALU.mult, ALU.add)
                yield
            ysb, dd, dsq = lws, cum, cumx
            P.copy("act", R(BK_[:, 0, 0:TH]), pb_[:, 0:TH])
            yield
            P.mm(MS[:, 0:TH], R(CMAT_R), R(BK_[:, 0, 0:TH]))
            P.copy("act", dd, MS[:, 0:TH])
            P.act(R(BK_[:, 1, 0:TH]), MS[:, 0:TH], AF.Square)
            yield
            P.mm(MS[:, 0:TH], R(BONES_R), R(BK_[:, 1, 0:TH]))
            P.act(t0, MS[:, 0:TH], AF.Ln, bias=GN_EPS, scale=1.0 / 64)
            yield
            P.act(t0, t0, AF.Exp, scale=-0.5)
            P.tt("dve", dd, dd, t0, ALU.mult)
            yield
            P.ts("dve", dd, dd, pt("lnx_g", pp), ALU.mult, pt("lnx_b", pp), ALU.add)
            P.tt("dve", dd, dd, bonus, ALU.add)
            P.tt("dve", YCAT[:, pp, hc], dd, gate, ALU.mult)

        def run_lockstep(gens):
            gens = list(gens)
            while gens:
                for g_ in list(gens):
                    try:
                        next(g_)
                    except StopIteration:
                        gens.remove(g_)

        for pp0 in range(0, 8, 2):
            for s_i in range(2):
                pp = pp0 + s_i
                wv = wload(win_v[:, :, WIN_RKV0 + pp * 384: WIN_RKV0 + (pp + 1) * 384], (16, 384))
                rb = RKV[s_i]
                for i in range(3):
                    b_ = next_abank()
                    proj_chunk(wv, i * 128, 128, b_)
                    shift_epi(b_, 128, 3 + pp * 3 + i, rb[i][:, 0:Tt])
            for (h0, TH) in quarters:
                run_lockstep([wkv_gen(pp0 + s_i, h0, TH, STREAMS[s_i], RKV[s_i]) for s_i in range(2)])

        chk(15)
        for ob in range(4):
            wv = wload(wout_v[:, :, ob * 512:(ob + 1) * 512], (16, 512))
            for tb in range(nb):
                b_ = next_abank()
                for kc in range(16):
                    P.mm(b_[0:tbs, :], YCAT[:, kc, tb * 128:tb * 128 + tbs], wv[:, kc, :], start=(kc == 0), stop=(kc == 15))
                xa = XT[0:tbs, tb, ob * 512:(ob + 1) * 512]
                P.tt("dve", xa, xa, b_[0:tbs, :], ALU.add)

        chk(16)
        rmsnorm_to_hT("nffn", HN_F)
        P.dma("sp", FN, final_norm.partition_broadcast(128))

        chk(17)
        zb_i = [0]
        for jb in range(22):
            wv = wload(wup_v[:, :, jb * 512:(jb + 1) * 512], (16, 512))
            for jj in range(2):
                j = jb * 2 + jj
                bu = next_abank()
                proj_chunk(wv, jj * 256, 128, bu)
                bz = (TR, MS)[zb_i[0] % 2]
                zb_i[0] += 1
                proj_chunk(wv, jj * 256 + 128, 128, bz)
                ub = UB[j % 2]
                cv = CV[j % 2]
                gl = GL[j % 2]
                for si, sg in enumerate(segs):
                    s0, L, S_ = sg["start"], sg["L"], sg["st"]
                    o_ = si * (Lmax + 2)
                    P.copy("pool", ub[:, o_:o_ + 2], S_.FH[:, j, :])
                    P.copy("act", ub[:, o_ + 2:o_ + 2 + L], bu[:, s0:s0 + L])
                    P.copy("pool", S_.FH[:, j, :], ub[:, o_ + L:o_ + L + 2])
                    P.act(cv[:, s0:s0 + L], ub[:, o_ + 2:o_ + 2 + L], AF.Identity, bias=pt("fb", j), scale=pt("fw", 2 * NJ + j))
                    P.stt(cv[:, s0:s0 + L], ub[:, o_ + 1:o_ + 1 + L], pt("fw", 1 * NJ + j), cv[:, s0:s0 + L], ALU.mult, ALU.add)
                    P.stt(cv[:, s0:s0 + L], ub[:, o_:o_ + L], pt("fw", 0 * NJ + j), cv[:, s0:s0 + L], ALU.mult, ALU.add)
                P.act(gl[:, 0:Tt], cv[:, 0:Tt], AF.Gelu_apprx_tanh)
                P.tt("dve", GB[:, j, 0:Tt], gl[:, 0:Tt], bz[:, 0:Tt], ALU.mult)

        chk(18)
        dbanks = (IA, IB, SQ, YT)
        kgroups = ((0, 16), (16, 32), (32, 44))
        for ob in range(4):
            for (k0, k1) in kgroups:
                wv = wload(wdn_v[:, k0:k1, ob * 512:(ob + 1) * 512], (k1 - k0, 512))
                for tb in range(nb):
                    for kc in range(k0, k1):
                        P.mm(dbanks[tb][0:tbs, :], GB[:, kc, tb * 128:tb * 128 + tbs], wv[:, kc - k0, :], start=(kc == 0), stop=(kc == NJ - 1))
            for tb in range(nb):
                xa = XT[0:tbs, tb, ob * 512:(ob + 1) * 512]
                P.tt("dve", xa, xa, dbanks[tb][0:tbs, :], ALU.add)

        chk(19)
        for tb in range(nb):
            yo = YO[tb % 2]
            xa = XT[0:tbs, tb, :]
            ss = SMALL[0:tbs, 24 + tb:25 + tb]
            P.act(yo[0:tbs, :], xa, AF.Square, accum_out=ss)
            sd = SMALL[0:tbs, 32 + tb:33 + tb]
            P.act(sd, ss, AF.Sqrt, bias=RMS_EPS, scale=1.0 / D)
            rs = SMALL[0:tbs, 40 + tb:41 + tb]
            P.recip(rs, sd)
            P.stt(yo[0:tbs, :], xa, rs, FN[0:tbs, :], ALU.mult, ALU.mult)
            for (r0, r1, dst) in y_stores(tb):
                P.dma("act", dst, yo[r0:r1, :])

    def store_states(S_, o_shift, o_wkv, o_conv, o_ffn):
        stg = ST32[0]
        P.transpose(TR[0:32, 0:128], S_.SH[:, 0:32], IDF)
        P.copy("dve", stg[0:32, 0:128], TR[0:32, 0:128])
        P.dma("act", rows2d(o_shift[3072:3328], 128), stg[0:2, 0:128])
        P.dma("act", o_shift[3328:3360].rearrange("(a b) -> a b", a=1), stg[2:3, 0:32])
        for pp in range(8):
            for i in range(3):
                r = 3 + pp * 3 + i
                P.dma("act", rows2d(o_shift[i * 1024 + pp * 128:i * 1024 + (pp + 1) * 128], 128), stg[r:r + 1, 0:128])
        for pp in range(8):
            P.transpose(TR[:, 128:256], S_.HBD[:, pp, :], IDF)
            P.copy("act", stg[:, 128 + pp * 128:256 + pp * 128], TR[:, 128:256])
            for hd in range(2):
                P.dma("act", o_wkv[2 * pp + hd], stg[hd * 64:hd * 64 + 64, 128 + pp * 128 + hd * 64:128 + pp * 128 + hd * 64 + 64])
        for j in range(8):
            P.transpose(TR[0:32, 256:384], S_.CH[:, j, :], IDF)
            P.copy("dve", stg[0:30, 1280 + j * 128:1280 + (j + 1) * 128], TR[0:30, 256:384])
        P.dma("act", o_conv, stg[0:30, 1280:2304])
        P.copy("dve", view(stg[:, 2304:2392], (2, NJ)), S_.FH.rearrange("p j t -> p t j"))
        P.transpose(TR[0:96, 384:512], stg[:, 2304:2400], IDF)
        P.copy("dve", stg[0:88, 2432:2560], TR[0:88, 384:512])
        P.dma("act", o_ffn.rearrange("t (j p) -> (t j) p", p=128), stg[0:88, 2432:2560])

    try:
        def small_y(tb):
            return [(16, 32, ys)]
        tile(32, [dict(start=0, L=16, st=STA), dict(start=16, L=16, st=STB)], 16,
             [(XT[0:16, 0, :], meta), (XT[16:32, 0, :], xs)], small_y)
        store_states(STB, o_shift_s, o_wkv_s, o_conv_s, o_ffn_s)
        for q in range(NSEQ):
            P.copy("pool", STB.SH, STA.SH)
            P.copy("pool", R(STB.HBD), STA.HBD)
            P.copy("pool", STB.CH, STA.CH)
            P.copy("pool", STB.FH, STA.FH)
            for tt_ in range(NT):
                def main_y(tb, q=q, tt_=tt_):
                    return [(0, 128, yp[q, tt_ * T + tb * 128: tt_ * T + (tb + 1) * 128, :])]
                tile(T, [dict(start=0, L=T, st=STB)], 64,
                     [(XT, xp[q, tt_ * T:(tt_ + 1) * T, :].rearrange("(nb p) d -> p nb d", p=128))], main_y)
            store_states(STB, o_shift_p[q], o_wkv_p[q], o_conv_p[q], o_ffn_p[q])


    except _Stop:
        pass
    P.emit()
    st.close()
    return nc


_NC_CACHE = {}


def _get_nc(NSEQ, NT, debug=None):
    key = (NSEQ, NT, debug)
    if key not in _NC_CACHE:
        _NC_CACHE[key] = build_program(NSEQ, NT, debug)
    return _NC_CACHE[key]


WEIGHT_KEYS = ("meta", "norm_mix", "w_in", "tshift_mu", "w0", "w_decay_up", "a0", "w_aaa_up", "w_gate_up",
               "k_k", "k_a", "r_k", "lnx_g", "lnx_b", "conv_w", "conv_b", "conv_ln_g", "conv_ln_b",
               "w_out", "norm_ffn", "w_ffn_up", "ffn_conv_w", "ffn_conv_b", "w_ffn_down", "final_norm")


def run_cores(inputs, n_cores, NSEQ, NT, debug=None):
    f = lambda a: np.ascontiguousarray(np.asarray(a, dtype=np.float32))
    shared = {}
    for k in WEIGHT_KEYS:
        a = f(inputs[k])
        if k in ("meta", "final_norm"):
            shared[k] = a
        elif k == "r_k":
            shared[k] = a.reshape(-1)
        else:
            shared[k] = a[0] if a.shape[0] == 1 else a
    in_maps = []
    xpf = f(inputs["x_prompt"])
    for c in range(n_cores):
        m = dict(shared)
        m["xp"] = np.ascontiguousarray(xpf[c * NSEQ:(c + 1) * NSEQ, :NT * 512])
        m["xs"] = f(inputs["x_sample"][c])
        m["st_shift"] = f(inputs["state_shift"][0, c, 0])
        m["st_wkv"] = f(inputs["state_wkv"][0, c])
        m["st_conv"] = f(inputs["cache_conv"][0, c])
        m["st_ffn"] = f(inputs["cache_ffn_conv"][0, c])
        in_maps.append(m)
    nc = _get_nc(NSEQ, NT, debug)
    res = run_bass_kernel_spmd(nc, in_maps, core_ids=list(range(n_cores)))
    return res.results


def kernel(**inputs):
    n = 8
    r = run_cores(inputs, n, 2, 4)
    cat = lambda k: np.concatenate([np.asarray(r[c][k]) for c in range(n)], axis=0)
    stk = lambda k: np.stack([np.asarray(r[c][k]) for c in range(n)], axis=0)
    y_prompt = cat("yp")
    y_sample = stk("ys")
    o_shift_p = cat("o_shift_p")[None, :, None, :]
    o_wkv_p = cat("o_wkv_p")[None]
    o_conv_p = cat("o_conv_p")[None]
    o_ffn_p = cat("o_ffn_p")[None]
    o_shift_s = stk("o_shift_s")[None, :, None, :]
    o_wkv_s = stk("o_wkv_s")[None]
    o_conv_s = stk("o_conv_s")[None]
    o_ffn_s = stk("o_ffn_s")[None]
    outs = (y_prompt, y_sample, o_shift_p, o_wkv_p, o_conv_p, o_ffn_p, o_shift_s, o_wkv_s, o_conv_s, o_ffn_s)
    return tuple(np.ascontiguousarray(o, dtype=np.float32) for o in outs)
```
